# Optimizing a Trainium2 kernel written in Bass

```python
import math
import jax, jax.numpy as jnp
from jax import lax
import numpy as np

D_MODEL = 1024
BATCH = 8
SEQ = 4096
DEPTH = 4

GRID_W = 64
CTX_LEN = 256
N_EVEN = (DEPTH + 1) // 2
N_ODD = DEPTH // 2
EPS = 1e-6
ROPE_THETA = 10000.0
Q_BLOCK = 128
MOD_STD = 0.5

D_RNN = D_MODEL // 2
LRU_BLOCKS = 8
LRU_BS = D_RNN // LRU_BLOCKS
LRU_CONV_W = 4
LRU_PAD = (1, 2)
RG_C = 8.0
LRU_A_MIN = 0.9
LRU_A_MAX = 0.999

MLA_HEADS = 8
MLA_NOPE = 64
MLA_ROPE = 32
MLA_V = (D_MODEL - D_RNN) // MLA_HEADS
MLA_Q_RANK = 3 * D_MODEL // 8
MLA_KV_RANK = D_MODEL // 4
IN_AB = 2 * D_RNN + MLA_Q_RANK + MLA_KV_RANK + MLA_ROPE

DIFF_HEADS = 8
DIFF_DH = D_MODEL // (2 * DIFF_HEADS)
IN_C = 3 * D_MODEL

D_FF = 2816
FFN_CONV_W = 3
FFN_PAD = (1, 1)

kernel_name = 'hybrid_rglru_mla_diffattn_convffn_dit'


def rmsnorm(x, g):
    xf = x.astype(jnp.float32)
    y = xf * lax.rsqrt(jnp.mean(xf * xf, axis=-1, keepdims=True) + EPS)
    return (y * g.astype(jnp.float32)).astype(x.dtype)


def modulate(u, shift, scale):
    return u * (1 + scale) + shift


def split_cols(z, sizes):
    idx = [int(s) for s in np.cumsum(sizes)[:-1]]
    return jnp.split(z, idx, axis=-1)


def dwconv(x, w, b, pad):
    C = x.shape[-1]
    y = lax.conv_general_dilated(x, w[:, None, :], window_strides=(1,), padding=[pad],
                                 dimension_numbers=('NWC', 'WIO', 'NWC'), feature_group_count=C)
    return y + b


def axial_angles(row, col, rot_dim):
    da = rot_dim // 2
    inv = ROPE_THETA ** (-jnp.arange(0, da, 2, dtype=jnp.float32) / da)
    return row[:, None] * inv, col[:, None] * inv


def rope_1d(x, ang):
    n = ang.shape[-1]
    ang = ang.reshape((1, ang.shape[0]) + (1,) * (x.ndim - 3) + (n,))
    cos = jnp.cos(ang).astype(x.dtype)
    sin = jnp.sin(ang).astype(x.dtype)
    x1, x2 = x[..., :n], x[..., n:]
    return jnp.concatenate([x1 * cos - x2 * sin, x2 * cos + x1 * sin], axis=-1)


def rope_2d(x, ang_row, ang_col):
    da = x.shape[-1] // 2
    return jnp.concatenate([rope_1d(x[..., :da], ang_row), rope_1d(x[..., da:], ang_col)], axis=-1)


def _sweep_queries(fn, q):
    B, S = q.shape[:2]
    nb = S // Q_BLOCK
    qb = jnp.moveaxis(q.reshape((B, nb, Q_BLOCK) + q.shape[2:]), 1, 0)
    out = jnp.moveaxis(lax.map(fn, qb), 0, 1)
    return out.reshape((B, S) + out.shape[3:])


def dense_attention(q, k, v):
    scale = q.shape[-1] ** -0.5
    def blk(qb):
        s = jnp.einsum('bqhd,bkhd->bhqk', qb, k).astype(jnp.float32) * scale
        p = jax.nn.softmax(s, axis=-1).astype(v.dtype)
        return jnp.einsum('bhqk,bkhd->bqhd', p, v)
    return _sweep_queries(blk, q)


def diff_attention(q, k, v, lam):
    scale = DIFF_DH ** -0.5
    def blk(qb):
        s = jnp.einsum('bqhmd,bkhmd->bhmqk', qb, k).astype(jnp.float32) * scale
        p = jax.nn.softmax(s, axis=-1)
        w = (p[:, :, 0] - lam * p[:, :, 1]).astype(v.dtype)
        return jnp.einsum('bhqk,bkhe->bqhe', w, v)
    return _sweep_queries(blk, q)


def rglru_coeffs(x, gate_w, gate_b, lam):
    B, T, _ = x.shape
    xb = x.reshape(B, T, LRU_BLOCKS, LRU_BS)
    pre = jnp.einsum('btnc,gncd->gbtnd', xb, gate_w).reshape(2, B, T, D_RNN) + gate_b[:, None, None, :]
    gates = jax.nn.sigmoid(pre)
    r, i = gates[0], gates[1]
    log_a = -RG_C * r * jax.nn.softplus(-lam)
    a = jnp.exp(log_a)
    b = jnp.sqrt(-jnp.expm1(2.0 * log_a)) * (i * x)
    return a, b


def _combine(left, right):
    a_l, b_l = left
    a_r, b_r = right
    return a_l * a_r, a_r * b_l + b_r


def linear_scan(a, b, h0, reverse):
    A, H = lax.associative_scan(_combine, (a, b), axis=1, reverse=reverse)
    if h0 is not None:
        H = H + A * h0[:, None, :]
    return H


def rglru_direction(x, xc, gate_w, gate_b, lam, reverse):
    a_c, b_c = rglru_coeffs(xc, gate_w, gate_b, lam)
    hc = linear_scan(a_c, b_c, None, reverse)
    h0 = hc[:, 0] if reverse else hc[:, -1]
    a, b = rglru_coeffs(x, gate_w, gate_b, lam)
    return linear_scan(a, b, h0, reverse), hc


def mla_project(q_lat, kv_lat, k_rope, g_q, g_kv, w_uq, w_ukv, gqn, gkn, gqr, gkr, angs):
    B, T, _ = q_lat.shape
    q = (rmsnorm(q_lat, g_q) @ w_uq).reshape(B, T, MLA_HEADS, MLA_NOPE + MLA_ROPE)
    kv = (rmsnorm(kv_lat, g_kv) @ w_ukv).reshape(B, T, MLA_HEADS, MLA_NOPE + MLA_V)
    q_nope = rmsnorm(q[..., :MLA_NOPE], gqn)
    q_rope = rmsnorm(q[..., MLA_NOPE:], gqr)
    k_nope = rmsnorm(kv[..., :MLA_NOPE], gkn)
    v = kv[..., MLA_NOPE:]
    k_rope = rmsnorm(k_rope[:, :, None, :], gkr)
    if angs is not None:
        q_rope = rope_2d(q_rope, *angs)
        k_rope = rope_2d(k_rope, *angs)
    k_rope = jnp.broadcast_to(k_rope, (B, T, MLA_HEADS, MLA_ROPE))
    return (jnp.concatenate([q_nope, q_rope], axis=-1),
            jnp.concatenate([k_nope, k_rope], axis=-1), v)


def even_mixer(u, uc, angs, w_in, conv_w, conv_b, gate_w, gate_b, lam, g_q, g_kv, w_uq, w_ukv,
               gqn, gkn, gqr, gkr, w_out, ctx_out):
    B, T, _ = u.shape
    sizes = (D_RNN, D_RNN, MLA_Q_RANK, MLA_KV_RANK, MLA_ROPE)
    xr, gr, ql, kvl, kr = split_cols(u @ w_in, sizes)
    xrc, grc, qlc, kvlc, krc = split_cols(uc @ w_in, sizes)
    xr = dwconv(xr, conv_w, conv_b, LRU_PAD)
    xrc = dwconv(xrc, conv_w, conv_b, LRU_PAD)
    hf, hfc = rglru_direction(xr, xrc, gate_w[0], gate_b[0], lam[0], False)
    hb, hbc = rglru_direction(xr, xrc, gate_w[1], gate_b[1], lam[1], True)
    y_a = (hf + hb) * jax.nn.gelu(gr)
    q, k, v = mla_project(ql, kvl, kr, g_q, g_kv, w_uq, w_ukv, gqn, gkn, gqr, gkr, angs)
    qc, kc, vc = mla_project(qlc, kvlc, krc, g_q, g_kv, w_uq, w_ukv, gqn, gkn, gqr, gkr, None)
    y_b = dense_attention(q, jnp.concatenate([k, kc], axis=1), jnp.concatenate([v, vc], axis=1))
    y = jnp.concatenate([y_a, y_b.reshape(B, T, -1)], axis=-1) @ w_out
    if not ctx_out:
        return y, None
    Bc, Tc, _ = uc.shape
    y_ac = (hfc + hbc) * jax.nn.gelu(grc)
    y_bc = dense_attention(qc, kc, vc).reshape(Bc, Tc, -1)
    yc = jnp.concatenate([y_ac, y_bc], axis=-1) @ w_out
    return y, yc


def diff_project(u, w_in, gq, gk, angs):
    B, T, _ = u.shape
    q, k, v = jnp.split(u @ w_in, 3, axis=-1)
    q = rmsnorm(q.reshape(B, T, DIFF_HEADS, 2, DIFF_DH), gq)
    k = rmsnorm(k.reshape(B, T, DIFF_HEADS, 2, DIFF_DH), gk)
    v = v.reshape(B, T, DIFF_HEADS, 2 * DIFF_DH)
    if angs is not None:
        q = rope_2d(q, *angs)
        k = rope_2d(k, *angs)
    return q, k, v


def diff_mixer(u, uc, angs, w_in, gq, gk, lam_vecs, g_out, w_out, lam_init, ctx_out):
    q, k, v = diff_project(u, w_in, gq, gk, angs)
    qc, kc, vc = diff_project(uc, w_in, gq, gk, None)
    lf = lam_vecs.astype(jnp.float32)
    lam = jnp.exp(jnp.sum(lf[0] * lf[1])) - jnp.exp(jnp.sum(lf[2] * lf[3])) + lam_init
    def head_out(o):
        B, T = o.shape[:2]
        return (rmsnorm(o, g_out) * (1.0 - lam_init)).reshape(B, T, -1) @ w_out
    y = head_out(diff_attention(q, jnp.concatenate([k, kc], axis=1),
                                jnp.concatenate([v, vc], axis=1), lam))
    if not ctx_out:
        return y, None
    return y, head_out(diff_attention(qc, kc, vc, lam))


def conv_ffn(u, w_up, conv_w, conv_b, w_down):
    a, g = jnp.split(u @ w_up, 2, axis=-1)
    g = dwconv(g, conv_w, conv_b, FFN_PAD)
    return (jax.nn.gelu(g) * a) @ w_down


def setup_inputs(seed: int = 0) -> dict:
    key = jax.random.key(seed)
    ks = iter(jax.random.split(key, 48))
    f32 = jnp.float32
    D = D_MODEL
    NE, NO = N_EVEN, N_ODD
    def nrm(shape, std):
        return std * jax.random.normal(next(ks), shape, f32)
    def gain(shape):
        return 1.0 + nrm(shape, 0.02)
    u = jax.random.uniform(next(ks), (NE, 2, D_RNN), f32, LRU_A_MIN, LRU_A_MAX)
    a0 = u ** (1.0 / RG_C)
    lru_lambda = jnp.log(a0) - jnp.log1p(-a0)
    return {
        'x': nrm((BATCH, SEQ, D), 1.0),
        'c': nrm((BATCH, D), 1.0),
        'ctx': nrm((BATCH, CTX_LEN, D), 1.0),
        'c_ctx': nrm((D,), 1.0),
        'w_mod': nrm((DEPTH, D, 6 * D), MOD_STD * D ** -0.5),
        'b_mod': nrm((DEPTH, 6 * D), 0.01),
        'g_norm1': gain((DEPTH, D)),
        'g_norm2': gain((DEPTH, D)),
        'w_in_ab': nrm((NE, D, IN_AB), D ** -0.5),
        'lru_conv_w': nrm((NE, LRU_CONV_W, D_RNN), LRU_CONV_W ** -0.5),
        'lru_conv_b': nrm((NE, D_RNN), 0.01),
        'lru_gate_w': nrm((NE, 2, 2, LRU_BLOCKS, LRU_BS, LRU_BS), LRU_BS ** -0.5),
        'lru_gate_b': nrm((NE, 2, 2, D_RNN), 0.01),
        'lru_lambda': lru_lambda,
        'mla_g_q': gain((NE, MLA_Q_RANK)),
        'mla_g_kv': gain((NE, MLA_KV_RANK)),
        'mla_w_uq': nrm((NE, MLA_Q_RANK, MLA_HEADS * (MLA_NOPE + MLA_ROPE)), MLA_Q_RANK ** -0.5),
        'mla_w_ukv': nrm((NE, MLA_KV_RANK, MLA_HEADS * (MLA_NOPE + MLA_V)), MLA_KV_RANK ** -0.5),
        'mla_gq_nope': gain((NE, MLA_NOPE)),
        'mla_gk_nope': gain((NE, MLA_NOPE)),
        'mla_gq_rope': gain((NE, MLA_ROPE)),
        'mla_gk_rope': gain((NE, MLA_ROPE)),
        'w_out_ab': nrm((NE, D_RNN + MLA_HEADS * MLA_V, D), D ** -0.5),
        'w_in_c': nrm((NO, D, IN_C), D ** -0.5),
        'diff_gq': gain((NO, DIFF_DH)),
        'diff_gk': gain((NO, DIFF_DH)),
        'diff_lam': nrm((NO, 4, DIFF_DH), 0.1),
        'diff_g_out': gain((NO, 2 * DIFF_DH)),
        'w_out_c': nrm((NO, D, D), D ** -0.5),
        'ffn_w_up': nrm((DEPTH, D, 2 * D_FF), D ** -0.5),
        'ffn_conv_w': nrm((DEPTH, FFN_CONV_W, D_FF), FFN_CONV_W ** -0.5),
        'ffn_conv_b': nrm((DEPTH, D_FF), 0.01),
        'ffn_w_down': nrm((DEPTH, D_FF, D), D_FF ** -0.5),
    }


def reference(x, c, ctx, c_ctx, w_mod, b_mod, g_norm1, g_norm2, w_in_ab, lru_conv_w, lru_conv_b,
              lru_gate_w, lru_gate_b, lru_lambda, mla_g_q, mla_g_kv, mla_w_uq, mla_w_ukv,
              mla_gq_nope, mla_gk_nope, mla_gq_rope, mla_gk_rope, w_out_ab, w_in_c, diff_gq,
              diff_gk, diff_lam, diff_g_out, w_out_c, ffn_w_up, ffn_conv_w, ffn_conv_b, ffn_w_down):
    S = x.shape[1]
    rows = S // GRID_W
    row = jnp.repeat(jnp.arange(rows, dtype=jnp.float32), GRID_W)
    col = jnp.tile(jnp.arange(GRID_W, dtype=jnp.float32), rows)
    ang_mla = axial_angles(row, col, MLA_ROPE)
    ang_diff = axial_angles(row, col, DIFF_DH)
    silu_c = jax.nn.silu(c)
    silu_cc = jax.nn.silu(c_ctx)
    h, hc = x, ctx
    for li in range(DEPTH):
        ctx_out = li < DEPTH - 1
        mod = (silu_c @ w_mod[li] + b_mod[li])[:, None, :]
        mod_c = (silu_cc @ w_mod[li] + b_mod[li])[None, None, :]
        sh1, sc1, ga1, sh2, sc2, ga2 = jnp.split(mod, 6, axis=-1)
        csh1, csc1, cga1, csh2, csc2, cga2 = jnp.split(mod_c, 6, axis=-1)
        u = modulate(rmsnorm(h, g_norm1[li]), sh1, sc1)
        uc = modulate(rmsnorm(hc, g_norm1[li]), csh1, csc1)
        if li % 2 == 0:
            e = li // 2
            y, yc = even_mixer(u, uc, ang_mla, w_in_ab[e], lru_conv_w[e], lru_conv_b[e],
                               lru_gate_w[e], lru_gate_b[e], lru_lambda[e], mla_g_q[e], mla_g_kv[e],
                               mla_w_uq[e], mla_w_ukv[e], mla_gq_nope[e], mla_gk_nope[e],
                               mla_gq_rope[e], mla_gk_rope[e], w_out_ab[e], ctx_out)
        else:
            o = li // 2
            lam_init = 0.8 - 0.6 * math.exp(-0.3 * li)
            y, yc = diff_mixer(u, uc, ang_diff, w_in_c[o], diff_gq[o], diff_gk[o], diff_lam[o],
                               diff_g_out[o], w_out_c[o], lam_init, ctx_out)
        h = h + ga1 * y
        u = modulate(rmsnorm(h, g_norm2[li]), sh2, sc2)
        h = h + ga2 * conv_ffn(u, ffn_w_up[li], ffn_conv_w[li], ffn_conv_b[li], ffn_w_down[li])
        if ctx_out:
            hc = hc + cga1 * yc
            uc = modulate(rmsnorm(hc, g_norm2[li]), csh2, csc2)
            hc = hc + cga2 * conv_ffn(uc, ffn_w_up[li], ffn_conv_w[li], ffn_conv_b[li], ffn_w_down[li])
    return h
```

```python
import math
from contextlib import ExitStack

import numpy as np
import concourse.bass as bass
import concourse.mybir as mybir
from concourse.bass_utils import run_bass_kernel_spmd

F32 = mybir.dt.float32
BF16 = mybir.dt.bfloat16
AF = mybir.ActivationFunctionType
ALU = mybir.AluOpType
AX = mybir.AxisListType

EPOCH = 50000
D = 1024
T = 4096
TC = 256
NT = T + TC
DEPTH = 4
EPS = 1e-6
DFF = 2816
NFC = DFF // 128
CH = [(j * 512, 512) for j in range(8)] + [(T, TC)]
GRID_W = 64


class Buf:
    __slots__ = ("w", "r")

    def __init__(self):
        self.w = {}
        self.r = {}


class G:
    def __init__(self, nc, sems, plan=None):
        self.nc = nc
        self.emit = plan is not None
        self.plan = plan if plan is not None else set()
        self.n = 0
        self.bufs = {}
        self.op_stream = []
        self.op_event = {}
        self.is_dma = []
        self.cnt = {"pe": 0, "act": 0, "dve": 0, "pool": 0}
        self.last = {}
        self.sems = sems
        self.qn = {"sp": 0, "pool": 0, "act": 0}
        self.seen_idx = {}
        self.seen_sem = {}
        self.eng = {"pe": nc.tensor, "act": nc.scalar, "dve": nc.vector, "pool": nc.gpsimd, "sp": nc.sync}
        self.nwaits = 0
        self.ninst = 0
        self._q = None

    def record(self):
        self._q = []

    def stop(self):
        q, self._q = self._q, None
        return q

    def interleave(self, *lists):
        n = max(len(x) for x in lists)
        for i in range(n):
            for lst in lists:
                if i < len(lst):
                    lst[i]()

    def buf(self, ap):
        nm = ap.name
        b = self.bufs.get(nm)
        if b is None:
            b = self.bufs[nm] = Buf()
        return b

    def _wait_sem(self, stream, sem, val):
        k2 = (stream, id(sem))
        if self.seen_sem.get(k2, 0) >= val:
            return
        self.seen_sem[k2] = val
        self.eng[stream].wait_ge(sem, val)
        self.nwaits += 1

    def _wait_event(self, stream, p):
        sem, val = self.op_event[p]
        if not self.is_dma[p]:
            k = (stream, self.op_stream[p])
            if self.seen_idx.get(k, -1) >= p:
                return
            self.seen_idx[k] = p
        self._wait_sem(stream, sem, val)

    def _begin(self, stream, dma, reads, writes, partial=False):
        idx = self.n
        self.n += 1
        self.op_stream.append(stream)
        self.is_dma.append(dma)
        deps = set()
        selfkey = ("d", idx) if dma else stream
        for ap in reads:
            b = self.buf(ap)
            for k, p in b.w.items():
                deps.add(p)
            if ap.name.startswith("ps"):
                for k, p in b.r.items():
                    if k != stream:
                        deps.add(p)
        for ap in writes:
            b = self.buf(ap)
            for k, p in b.r.items():
                if dma or k != stream:
                    deps.add(p)
            if not partial:
                for k, p in b.w.items():
                    if dma or k != stream:
                        deps.add(p)
        for ap in reads:
            self.buf(ap).r[selfkey] = idx
        for ap in writes:
            b = self.buf(ap)
            if partial:
                b.w[selfkey] = idx
            else:
                b.w = {selfkey: idx}
                b.r = {}
        if not dma:
            self.last[stream] = idx
        if not self.emit:
            self.plan.update(deps)
        else:
            for p in sorted(deps):
                self._wait_event(stream, p)
        return idx

    def _end(self, idx, stream, ins):
        self.ninst += 1
        if idx in self.plan:
            c = self.cnt[stream]
            sem = self.sems[stream][c // EPOCH]
            val = c % EPOCH + 1
            self.cnt[stream] = c + 1
            ins.then_inc(sem, 1)
            self.op_event[idx] = (sem, val)

    def op(self, stream, meth, partial=False, **kw):
        if self._q is not None:
            self._q.append(lambda: self.op(stream, meth, partial=partial, **kw))
            return
        reads, writes = [], []
        for k, v in kw.items():
            if isinstance(v, bass.AP):
                (writes if k in ("out", "accum_out", "ap") else reads).append(v)
        idx = self._begin(stream, False, reads, writes, partial)
        if self.emit:
            ins = getattr(self.eng[stream], meth)(**kw)
            self._end(idx, stream, ins)

    def mm(self, out, lhsT, rhs, start=True, stop=True):
        if self._q is not None:
            self._q.append(lambda: self.mm(out, lhsT, rhs, start=start, stop=stop))
            return
        idx = self._begin("pe", False, [lhsT, rhs], [out], partial=not start)
        if self.emit:
            ins = self.nc.tensor.matmul(out, lhsT, rhs, start=start, stop=stop)
            self._end(idx, "pe", ins)

    def tr(self, out, in_, ident, partial=False):
        if self._q is not None:
            self._q.append(lambda: self.tr(out, in_, ident, partial=partial))
            return
        idx = self._begin("pe", False, [in_, ident], [out], partial=partial)
        if self.emit:
            ins = self.nc.tensor.transpose(out, in_, ident)
            self._end(idx, "pe", ins)

    def act(self, out, in_, func, partial=False, **kw):
        self.op("act", "activation", partial=partial, out=out, in_=in_, func=func, **kw)

    def dma(self, q, out, in_, partial=False, **kw):
        if self._q is not None:
            self._q.append(lambda: self.dma(q, out, in_, partial=partial, **kw))
            return
        idx = self._begin(q, True, [in_], [out], partial)
        k = self.qn[q]
        self.qn[q] = k + 1
        pool = self.sems["q_" + q]
        P = len(pool)
        sem = pool[k % P]
        gen = k // P
        if self.emit:
            if gen > 0:
                self._wait_sem(q, sem, 16 * gen)
            ins = self.eng[q].dma_start(out=out, in_=in_, **kw)
            ins.then_inc(sem, 16)
            self.ninst += 1
        self.op_event[idx] = (sem, 16 * (gen + 1))

    def _dma_finals(self):
        res = []
        for q in ("sp", "pool", "act"):
            pool = self.sems["q_" + q]
            P = len(pool)
            k = self.qn[q]
            for j in range(min(P, k)):
                uses = (k - 1 - j) // P + 1
                res.append((pool[j], 16 * uses))
        return res

    def barrier(self):
        lasts = dict(self.last)
        if not self.emit:
            self.plan.update(lasts.values())
        else:
            fin = self._dma_finals()
            for s in ("pe", "act", "dve", "pool", "sp"):
                for e, p in lasts.items():
                    if e != s:
                        self._wait_event(s, p)
                for sem, val in fin:
                    self._wait_sem(s, sem, val)
        self.bufs = {}

    def drain(self):
        if not self.emit:
            return
        for sem, val in self._dma_finals():
            self._wait_sem("sp", sem, val)


def _rev(ap):
    n = ap.shape[-1]
    a = ap[:, n - 1:n]
    lst = [list(x) for x in a.ap]
    lst[-1] = [-1, n]
    return bass.AP(a.tensor, a.offset, lst)


def build_program(dbg_stop=None):
    nc = bass.Bass("TRN2", target_bir_lowering=False)

    def din(name, shape, dt=F32):
        return nc.dram_tensor(name, list(shape), dt, kind="ExternalInput").ap()

    def dscr(name, shape, dt=F32):
        return nc.dram_tensor(name, list(shape), dt, kind="Internal").ap()

    x_d = din("x", [T, D])
    ctx_d = din("ctx", [TC, D])
    cvec_d = din("cvec", [128, 8, 2])
    ident_d = din("ident", [128, 128])
    wmod_d = din("w_mod", [DEPTH, D, 6 * D])
    bmod_d = din("bmod", [128, DEPTH, 48])
    g1_d = din("g1", [128, DEPTH, 8])
    g2_d = din("g2", [128, DEPTH, 8])
    w_in_ab_d = din("w_in_ab", [2, D, 1696])
    lru_cw_d = din("lru_cw", [2, 128, 4, 4])
    lru_cb_d = din("lru_cb", [2, 128, 4])
    lru_bd_d = din("lru_bd", [2, 128, 2, 2, 4, 128])
    lru_gb_d = din("lru_gb", [2, 128, 2, 2, 4])
    lru_lam_d = din("lru_lam", [2, 128, 2, 4])
    mla_gq_d = din("mla_gq", [2, 128, 3])
    mla_gkv_d = din("mla_gkv", [2, 128, 2])
    w_uq_d = din("w_uq", [2, 384, 768])
    w_ukv_d = din("w_ukv", [2, 256, 1024])
    mla_gvec_d = din("mla_gvec", [2, 192])
    w_out_ab_d = din("w_out_ab", [2, D, D])
    w_in_c_d = din("w_in_c", [2, D, 3 * D])
    diff_gvec_d = din("diff_gvec", [2, 512])
    w_out_c_d = din("w_out_c", [2, D, D])
    w_up_d = din("w_up", [DEPTH, D, 2 * DFF])
    ffn_cw_d = din("ffn_cw", [DEPTH, 128, NFC, 3])
    ffn_cb_d = din("ffn_cb", [DEPTH, 128, NFC])
    w_down_d = din("w_down", [DEPTH, DFF, D])
    rope_mla_d = din("rope_mla", [T, 64])
    rope_diff_d = din("rope_diff", [T, 128])
    out_d = nc.dram_tensor("out", [T, D], F32, kind="ExternalOutput").ap()
    dbg_d = None
    if dbg_stop is not None:
        dbg_d = nc.dram_tensor("dbg", [128, 8, NT], F32, kind="ExternalOutput").ap()

    HTa = [dscr(f"HTa{j}", [128, 8, n]) for j, (t0, n) in enumerate(CH)]
    HTb = [dscr(f"HTb{j}", [128, 8, n]) for j, (t0, n) in enumerate(CH)]
    XR = [dscr(f"XR{j}", [128, 4, n]) for j, (t0, n) in enumerate(CH)]
    GR = [dscr(f"GR{j}", [128, 4, n], BF16) for j, (t0, n) in enumerate(CH)]
    YA = [dscr(f"YA{j}", [128, 4, n], BF16) for j, (t0, n) in enumerate(CH)]
    QT = [dscr(f"QT{j}", [128, 8, n], BF16) for j, (t0, n) in enumerate(CH)]
    UT = [dscr(f"UT{j}", [128, 8, n], BF16) for j, (t0, n) in enumerate(CH)]

    es = ExitStack()
    with es:
        sems = {}
        for k, n in (("pe", 4), ("act", 3), ("dve", 4), ("pool", 2), ("q_sp", 32), ("q_pool", 16), ("q_act", 2)):
            sems[k] = [es.enter_context(nc.semaphore(f"s_{k}_{i}")) for i in range(n)]
        PS = [es.enter_context(nc.psum_tensor(f"ps{i}", [128, 512], F32)) for i in range(8)]

        uid = [0]

        def gen(g):
            def sb(st, name, shape, dt=F32):
                uid[0] += 1
                return st.enter_context(nc.sbuf_tensor(f"{name}_u{uid[0]}", list(shape), dt))

            def psb(i):
                return PS[i][:].bitcast(BF16)

            alt = [0]

            def evac(out, in_):
                alt[0] ^= 1
                if alt[0]:
                    g.act(out, in_, AF.Copy)
                else:
                    g.op("dve", "tensor_copy", out=out, in_=in_)

            def tt(out, in0, in1, op, eng="dve"):
                g.op(eng, "tensor_tensor", out=out, in0=in0, in1=in1, op=op)

            def stt(out, in0, scalar, in1, op0=ALU.mult, op1=ALU.add):
                g.op("dve", "scalar_tensor_tensor", out=out, in0=in0, scalar=scalar, in1=in1, op0=op0, op1=op1)

            def ts(out, in0, s1, s2=None, op0=ALU.mult, op1=None, eng="dve"):
                if op1 is None:
                    g.op(eng, "tensor_scalar", out=out, in0=in0, scalar1=s1, scalar2=None, op0=op0)
                else:
                    g.op(eng, "tensor_scalar", out=out, in0=in0, scalar1=s1, scalar2=s2, op0=op0, op1=op1)

            gst = ExitStack()
            with gst:
                ident_f = sb(gst, "ident_f", [128, 128])
                ident_b = sb(gst, "ident_b", [128, 128], BF16)
                ones_b = sb(gst, "ones_b", [128, 128], BF16)
                eps_t = sb(gst, "eps_t", [128, 1])
                one_t = sb(gst, "one_t", [128, 1])
                SC = sb(gst, "SC", [128, 8, 2])
                MOD = sb(gst, "MOD", [128, DEPTH, 48, 2])
                A1 = sb(gst, "A1", [128, DEPTH, 8, 2])
                A2 = sb(gst, "A2", [128, DEPTH, 8, 2])
                bmod = sb(gst, "bmod", [128, DEPTH, 48])
                g1 = sb(gst, "g1", [128, DEPTH, 8])
                g2 = sb(gst, "g2", [128, DEPTH, 8])
                rs_t = sb(gst, "rs_t", [128, 512])
                nt_t = [sb(gst, f"nt_t{i}", [128, 512]) for i in range(2)]

                g.dma("sp", out=ident_f[:], in_=ident_d)
                g.dma("pool", out=ident_b[:], in_=ident_d)
                g.op("dve", "memset", ap=ones_b[:], constant=1.0)
                g.op("dve", "memset", ap=eps_t[:], constant=EPS)
                g.op("dve", "memset", ap=one_t[:], constant=1.0)
                g.dma("sp", out=SC[:], in_=cvec_d)
                g.dma("sp", out=bmod[:], in_=bmod_d)
                g.dma("sp", out=g1[:], in_=g1_d)
                g.dma("sp", out=g2[:], in_=g2_d)
                g.act(SC[:], SC[:], AF.Silu)

                def norm_mod(hs3, ut3, w, l, which, v, bank):
                    Acoef = (A1 if which == 1 else A2)
                    boff = 0 if which == 1 else 24
                    g.act(ut3, hs3, AF.Square)
                    for kc in range(8):
                        g.mm(PS[bank][:, 0:w], ones_b[:], ut3[:, kc, :], start=(kc == 0), stop=(kc == 7))
                    g.act(rs_t[:, 0:w], PS[bank][:, 0:w], AF.Sqrt, scale=1.0 / D, bias=eps_t[:, 0:1])
                    g.op("dve", "reciprocal", out=rs_t[:, 0:w], in_=rs_t[:, 0:w])
                    for kc in range(8):
                        t_ = nt_t[kc % 2]
                        stt(t_[:, 0:w], hs3[:, kc, :], Acoef[:, l, kc, v:v + 1], rs_t[:, 0:w], ALU.mult, ALU.mult)
                        g.act(ut3[:, kc, :], t_[:, 0:w], AF.Identity, bias=MOD[:, l, boff + kc, v:v + 1], partial=True)

                def headnorm(src3, dst3, gain, H, w, sqs, ss, tmp3):
                    tt(sqs, src3, src3, ALU.mult)
                    g.op("dve", "tensor_reduce", out=ss, in_=sqs, axis=AX.X, op=ALU.add)
                    g.act(ss, ss, AF.Sqrt, scale=1.0 / w, bias=eps_t[:, 0:1])
                    g.op("dve", "reciprocal", out=ss, in_=ss)
                    tt(tmp3, src3, ss.unsqueeze(2).to_broadcast([128, H, w]), ALU.mult)
                    tt(dst3, tmp3, gain.unsqueeze(1).to_broadcast([128, H, w]), ALU.mult)

                def rope(src3, dst3, cs, H, R, t1, t2):
                    cos = cs[:, 0:R].unsqueeze(1).to_broadcast([128, H, R])
                    sin4 = cs[:, R:2 * R].rearrange("p (a s n) -> p a s n", a=2, s=2)
                    s5 = src3.rearrange("p h (a s n) -> p h a s n", a=2, s=2)
                    t25 = t2.rearrange("p h (a s n) -> p h a s n", a=2, s=2)
                    n = R // 4
                    tt(t1, src3, cos, ALU.mult)
                    for s_ in range(2):
                        tt(t25[:, :, :, s_, :], s5[:, :, :, 1 - s_, :],
                           sin4[:, :, s_, :].unsqueeze(1).to_broadcast([128, H, 2, n]), ALU.mult)
                    tt(dst3, t1, t2, ALU.add)

                st = ExitStack()
                with st:
                    xs = [sb(st, f"xs{i}", [128, D]) for i in range(4)]
                    hsT = [sb(st, f"hsT{i}", [128, 8, 512]) for i in range(2)]
                    wm = [sb(st, f"wm{i}", [128, 8, 384]) for i in range(6)]
                    mod_state = {"it": 0}

                    def emit_mod(l, blk):
                        wsrc = wmod_d[l].rearrange("(k p) n -> p k n", p=128)
                        mb = 4 + l % 2
                        wt = wm[mod_state["it"] % 6]
                        mod_state["it"] += 1
                        g.dma("sp", out=wt[:], in_=wsrc[:, :, blk * 384:(blk + 1) * 384])
                        for o3 in range(3):
                            oc = blk * 3 + o3
                            for kc in range(8):
                                g.mm(PS[mb][:, oc * 2:oc * 2 + 2], wt[:, kc, o3 * 128:(o3 + 1) * 128],
                                     SC[:, kc, :], start=(kc == 0), stop=(kc == 7))
                        if blk == 15:
                            tt(MOD[:, l, :, :], PS[mb][:, 0:96].rearrange("p (a b) -> p a b", b=2),
                               bmod[:, l, :].unsqueeze(2).to_broadcast([128, 48, 2]), ALU.add)
                            stt(A1[:, l, :, :], MOD[:, l, 8:16, :], 1.0,
                                g1[:, l, :].unsqueeze(2).to_broadcast([128, 8, 2]), ALU.add, ALU.mult)
                            stt(A2[:, l, :, :], MOD[:, l, 32:40, :], 1.0,
                                g2[:, l, :].unsqueeze(2).to_broadcast([128, 8, 2]), ALU.add, ALU.mult)
                    mod_items = [(l, blk) for l in range(DEPTH) for blk in range(16)]
                    it = 0
                    for j, (t0, n) in enumerate(CH):
                        hs = hsT[j % 2]
                        for s in range(n // 128):
                            xt = xs[it % 4]
                            it += 1
                            src = x_d[t0 + s * 128:t0 + (s + 1) * 128, :] if j < 8 else ctx_d[s * 128:(s + 1) * 128, :]
                            g.dma("sp", out=xt[:], in_=src)
                            for half in range(2):
                                bank = (it * 2 + half) % 4
                                for q4 in range(4):
                                    kc = half * 4 + q4
                                    g.tr(PS[bank][:, q4 * 128:(q4 + 1) * 128], xt[:, kc * 128:(kc + 1) * 128],
                                         ident_f[:], partial=(q4 > 0))
                                evac(hs[:, half * 4:(half + 1) * 4, s * 128:(s + 1) * 128],
                                     PS[bank][:].rearrange("p (a b) -> p a b", a=4))
                            for _ in range(2):
                                if mod_items:
                                    emit_mod(*mod_items.pop(0))
                        g.dma("pool", out=HTb[j], in_=hs[:, :, 0:n])
                    while mod_items:
                        emit_mod(*mod_items.pop(0))
                    g.barrier()

                def dump(HT):
                    st = ExitStack()
                    with st:
                        t = sb(st, "dump_t", [128, 8, 512])
                        for j, (t0, n) in enumerate(CH):
                            g.dma("sp", out=t[:, :, 0:n], in_=HT[j])
                            g.dma("pool", out=dbg_d[:, :, t0:t0 + n], in_=t[:, :, 0:n])
                        g.barrier()

                def attn_steps(groups, n, sc_att, ncol, PT, sbanks=(0, 1), LA=1):
                    nsub = n // 128
                    steps = [(gi, i) for gi, grp in enumerate(groups) for i in range(len(grp["kcs"]))]

                    def emitS(t):
                        gi, i = steps[t]
                        grp = groups[gi]
                        if i == 0 and grp.get("prep") is not None:
                            grp["prep"]()
                        grp["S"](grp["kcs"][i], PS[sbanks[t % len(sbanks)]][:, 0:n])
                    for t0_ in range(min(LA, len(steps))):
                        emitS(t0_)
                    for t, (gi, i) in enumerate(steps):
                        grp = groups[gi]
                        nk = len(grp["kcs"])
                        if t + LA < len(steps):
                            emitS(t + LA)
                        pt = PT[t % len(PT)]
                        g.act(pt[:, 0:n], PS[sbanks[t % len(sbanks)]][:, 0:n], AF.Exp, scale=sc_att)
                        ob = 2 + gi % 2
                        if ncol == 128:
                            g.mm(PS[ob][:, 0:n], grp["V"](grp["kcs"][i]), pt[:, 0:n], start=(i == 0), stop=(i == nk - 1))
                        else:
                            rhs = grp["V"](grp["kcs"][i])
                            for s in range(nsub):
                                g.mm(PS[2 + s][:, 0:ncol], pt[:, s * 128:(s + 1) * 128], rhs,
                                     start=(i == 0), stop=(i == nk - 1))
                        if i == nk - 1:
                            grp["fin"](ob)

                def attention_out_fm(l, v, j, n, hs, OT, nfc, wout, ya):
                    for oc in range(8):
                        bank = 6 + oc % 2
                        rhs = []
                        if ya is not None:
                            rhs += [ya[:, kc, 0:n] for kc in range(4)]
                        rhs += [OT[:, fc, 0:n] for fc in range(nfc)]
                        for kc in range(8):
                            g.mm(PS[bank][:, 0:n], wout[:, kc, oc * 128:(oc + 1) * 128], rhs[kc],
                                 start=(kc == 0), stop=(kc == 7))
                        hsm = hs[oc % 2]
                        g.dma("sp", out=hsm[:, 0:n], in_=HTb[j][:, oc, :])
                        stt(hsm[:, 0:n], PS[bank][:, 0:n], MOD[:, l, 16 + oc, v:v + 1], hsm[:, 0:n])
                        g.dma("pool", out=HTa[j][:, oc, :], in_=hsm[:, 0:n])

                def attention_out(l, v, j, n, hs, Otm, OT, nfc, wout, ya):
                    nsub = n // 128
                    blocks = [(s, fc) for s in range(nsub) for fc in range(nfc)]
                    for r0 in range(0, len(blocks), 8):
                        grp = blocks[r0:r0 + 8]
                        bank = 6 + (r0 // 8) % 2
                        for i, (s, fc) in enumerate(grp):
                            g.tr(psb(bank)[:, i * 128:(i + 1) * 128], Otm[:, s, fc * 128:(fc + 1) * 128], ident_b[:],
                                 partial=(i > 0))
                        for i, (s, fc) in enumerate(grp):
                            if i == 0 or grp[i - 1][0] != s:
                                cnt = sum(1 for (s2, _) in grp[i:] if s2 == s)
                                fc0 = fc
                                evac(OT[:, fc0:fc0 + cnt, s * 128:(s + 1) * 128],
                                     psb(bank)[:, i * 128:(i + cnt) * 128].rearrange("p (f t) -> p f t", f=cnt))
                    for oc in range(8):
                        bank = 6 + oc % 2
                        rhs = []
                        if ya is not None:
                            rhs += [ya[:, kc, 0:n] for kc in range(4)]
                        rhs += [OT[:, fc, 0:n] for fc in range(nfc)]
                        for kc in range(8):
                            g.mm(PS[bank][:, 0:n], wout[:, kc, oc * 128:(oc + 1) * 128], rhs[kc],
                                 start=(kc == 0), stop=(kc == 7))
                        hsm = hs[oc % 2]
                        g.dma("sp", out=hsm[:, 0:n], in_=HTb[j][:, oc, :])
                        stt(hsm[:, 0:n], PS[bank][:, 0:n], MOD[:, l, 16 + oc, v:v + 1], hsm[:, 0:n])
                        g.dma("pool", out=HTa[j][:, oc, :], in_=hsm[:, 0:n])

                def even_layer(l, ctx_out):
                    e = l // 2
                    st = ExitStack()
                    with st:
                        w1 = sb(st, "w_in1", [128, 8, 1024], BF16)
                        for kc in range(8):
                            g.dma("pool", out=w1[:, kc, :], in_=w_in_ab_d[e, kc * 128:(kc + 1) * 128, 0:1024],
                                  partial=(kc > 0))
                        hsb = [sb(st, f"hs{i}", [128, 8, 512]) for i in range(2)]
                        utb = [sb(st, f"ut{i}", [128, 8, 512], BF16) for i in range(2)]
                        xrs = [sb(st, f"xrs{i}", [128, 4, 512]) for i in range(2)]
                        grs = [sb(st, f"grs{i}", [128, 4, 512], BF16) for i in range(2)]
                        for j, (t0, n) in enumerate(CH):
                            v = 0 if j < 8 else 1
                            hs = hsb[j % 2]
                            ut = utb[j % 2]
                            g.dma("sp", out=hs[:, :, 0:n], in_=HTb[j])
                            norm_mod(hs[:, :, 0:n], ut[:, :, 0:n], n, l, 1, v, 0)
                            g.dma("pool", out=UT[j], in_=ut[:, :, 0:n])
                            for c in range(4):
                                bank = 1 + c % 2
                                for kc in range(8):
                                    g.mm(PS[bank][:, 0:n], w1[:, kc, c * 128:(c + 1) * 128], ut[:, kc, 0:n],
                                         start=(kc == 0), stop=(kc == 7))
                                evac(xrs[j % 2][:, c, 0:n], PS[bank][:, 0:n])
                            for c in range(4):
                                bank = 3 + c % 2
                                for kc in range(8):
                                    g.mm(PS[bank][:, 0:n], w1[:, kc, 512 + c * 128:512 + (c + 1) * 128], ut[:, kc, 0:n],
                                         start=(kc == 0), stop=(kc == 7))
                                g.act(grs[j % 2][:, c, 0:n], PS[bank][:, 0:n], AF.Gelu_apprx_tanh)
                            g.dma("pool", out=XR[j], in_=xrs[j % 2][:, :, 0:n])
                            g.dma("pool", out=GR[j], in_=grs[j % 2][:, :, 0:n])
                        g.barrier()
                    if dbg_stop == "b1a":
                        return
                    st = ExitStack()
                    with st:
                        xr = sb(st, "l_xr", [128, NT])
                        xc = sb(st, "l_xc", [128, NT])
                        xcb = sb(st, "l_xcb", [128, NT], BF16)
                        Rts = [sb(st, f"l_R{i}", [128, NT]) for i in range(2)]
                        Its = [sb(st, f"l_I{i}", [128, NT]) for i in range(2)]
                        Ats = [sb(st, f"l_A{i}", [128, NT]) for i in range(2)]
                        grt = sb(st, "l_gr", [128, NT], BF16)
                        ya = sb(st, "l_ya", [128, NT], BF16)
                        bd = sb(st, "l_bd", [128, 2, 2, 4, 128], BF16)
                        gb = sb(st, "l_gb", [128, 2, 2, 4])
                        lam = sb(st, "l_lam", [128, 2, 4])
                        cw = sb(st, "l_cw", [128, 4, 4])
                        cb = sb(st, "l_cb", [128, 4])
                        cf = sb(st, "l_cf", [128, 2, 4])
                        cf2 = sb(st, "l_cf2", [128, 2, 4])
                        z = sb(st, "l_z", [128, 2, 4])
                        pz = sb(st, "l_pz", [128, 2, 4])
                        mk = sb(st, "l_mk", [128, 2, 4])
                        g.dma("pool", out=bd[:], in_=lru_bd_d[e])
                        g.dma("sp", out=gb[:], in_=lru_gb_d[e])
                        g.dma("sp", out=lam[:], in_=lru_lam_d[e])
                        g.dma("sp", out=cw[:], in_=lru_cw_d[e])
                        g.dma("sp", out=cb[:], in_=lru_cb_d[e])
                        g.act(z[:], lam[:], AF.Exp, scale=-1.0)
                        ts(pz[:], z[:], -0.25, 1.0 / 3.0, ALU.mult, ALU.add)
                        tt(pz[:], pz[:], z[:], ALU.mult)
                        ts(pz[:], pz[:], -0.5, None, ALU.add)
                        tt(pz[:], pz[:], z[:], ALU.mult)
                        ts(pz[:], pz[:], 1.0, None, ALU.add)
                        tt(pz[:], pz[:], z[:], ALU.mult)
                        g.act(cf[:], z[:], AF.Ln, bias=one_t[:, 0:1])
                        ts(mk[:], z[:], 0.05, None, ALU.is_lt)
                        tt(pz[:], pz[:], cf[:], ALU.subtract)
                        tt(pz[:], pz[:], mk[:], ALU.mult)
                        tt(cf[:], cf[:], pz[:], ALU.add)
                        ts(cf2[:], cf[:], -16.0)
                        ts(cf[:], cf[:], -8.0)
                        for c in range(4):
                            for j, (t0, n) in enumerate(CH):
                                g.dma("sp", out=xr[:, t0:t0 + n], in_=XR[j][:, c, :], partial=(j > 0))
                                g.dma("sp", out=grt[:, t0:t0 + n], in_=GR[j][:, c, :], partial=(j > 0))
                            g.act(xc[:], xr[:], AF.Identity, scale=cw[:, c, 1:2], bias=cb[:, c:c + 1])
                            for (a, b_) in ((0, T), (T, NT)):
                                stt(xc[:, a + 1:b_], xr[:, a:b_ - 1], cw[:, c, 0:1], xc[:, a + 1:b_])
                                stt(xc[:, a:b_ - 1], xr[:, a + 1:b_], cw[:, c, 2:3], xc[:, a:b_ - 1])
                                stt(xc[:, a:b_ - 2], xr[:, a + 2:b_], cw[:, c, 3:4], xc[:, a:b_ - 2])
                            g.op("pool", "tensor_copy", out=xcb[:], in_=xc[:])
                            chains = []
                            for d in range(2):
                                g.record()
                                Rt, It, At = Rts[d], Its[d], Ats[d]
                                for j, (t0, n) in enumerate(CH):
                                    b0 = (j % 2) * 2 + 4 * d
                                    g.mm(PS[b0][:, 0:n], bd[:, d, 0, c, :], xcb[:, t0:t0 + n])
                                    g.act(Rt[:, t0:t0 + n], PS[b0][:, 0:n], AF.Sigmoid, bias=gb[:, d, 0, c:c + 1],
                                          partial=(j > 0))
                                    g.mm(PS[b0 + 1][:, 0:n], bd[:, d, 1, c, :], xcb[:, t0:t0 + n])
                                    g.act(It[:, t0:t0 + n], PS[b0 + 1][:, 0:n], AF.Sigmoid, bias=gb[:, d, 1, c:c + 1],
                                          partial=(j > 0))
                                g.act(At[:], Rt[:], AF.Exp, scale=cf[:, d, c:c + 1])
                                g.act(Rt[:], Rt[:], AF.Exp, scale=cf2[:, d, c:c + 1])
                                ts(Rt[:], Rt[:], 1.0, -1.0, ALU.min, ALU.mult)
                                g.act(Rt[:], Rt[:], AF.Sqrt, bias=one_t[:, 0:1])
                                tt(It[:], It[:], Rt[:], ALU.mult)
                                tt(It[:], It[:], xc[:], ALU.mult)

                                def scan(o, a_, b_, init):
                                    g.op("dve", "tensor_tensor_scan", out=o, data0=a_, data1=b_, initial=init,
                                         op0=ALU.mult, op1=ALU.add)
                                if d == 0:
                                    scan(Rt[:, T:NT], At[:, T:NT], It[:, T:NT], 0.0)
                                    prev = Rt[:, NT - 1:NT]
                                    for k in range(4):
                                        a, b_ = k * 1024, (k + 1) * 1024
                                        scan(Rt[:, a:b_], At[:, a:b_], It[:, a:b_], prev)
                                        prev = Rt[:, b_ - 1:b_]
                                    g.op("pool", "tensor_copy", out=xr[:], in_=Rt[:])
                                else:
                                    scan(_rev(Rt[:, T:NT]), _rev(At[:, T:NT]), _rev(It[:, T:NT]), 0.0)
                                    prev = Rt[:, T:T + 1]
                                    for k in range(3, -1, -1):
                                        a, b_ = k * 1024, (k + 1) * 1024
                                        scan(_rev(Rt[:, a:b_]), _rev(At[:, a:b_]), _rev(It[:, a:b_]), prev)
                                        prev = Rt[:, a:a + 1]
                                    tt(xr[:], xr[:], Rt[:], ALU.add)
                                chains.append(g.stop())
                            g.interleave(*chains)
                            tt(ya[:], xr[:], grt[:], ALU.mult)
                            for j, (t0, n) in enumerate(CH):
                                g.dma("pool", out=YA[j][:, c, :], in_=ya[:, t0:t0 + n])
                        g.barrier()
                    if dbg_stop == "b2":
                        return
                    st = ExitStack()
                    with st:
                        KT = sb(st, "KT", [128, 8, NT], BF16)
                        VV = sb(st, "VV", [128, 34, 4, 192], BF16)
                        w2 = sb(st, "w_in2", [128, 8, 672], BF16)
                        wuq = sb(st, "wuq", [128, 3, 768], BF16)
                        wukv = sb(st, "wukv", [128, 2, 1024], BF16)
                        gq = sb(st, "m_gq", [128, 3])
                        gkv = sb(st, "m_gkv", [128, 2])
                        gvec = sb(st, "m_gvec", [128, 192])
                        for kc in range(8):
                            g.dma("pool", out=w2[:, kc, :], in_=w_in_ab_d[e, kc * 128:(kc + 1) * 128, 1024:1696],
                                  partial=(kc > 0))
                        for c in range(3):
                            g.dma("pool", out=wuq[:, c, :], in_=w_uq_d[e, c * 128:(c + 1) * 128, :], partial=(c > 0))
                        for c in range(2):
                            g.dma("pool", out=wukv[:, c, :], in_=w_ukv_d[e, c * 128:(c + 1) * 128, :], partial=(c > 0))
                        g.dma("sp", out=gq[:], in_=mla_gq_d[e])
                        g.dma("sp", out=gkv[:], in_=mla_gkv_d[e])
                        g.dma("sp", out=gvec[:], in_=mla_gvec_d[e:e + 1, :].partition_broadcast(128))
                        g.op("dve", "memset", ap=VV[:, :, :, 64:128], constant=1.0)
                        for h in range(8):
                            g.op("dve", "memset", ap=KT[:, h, :], constant=0.0, partial=(h > 0))
                        st2 = ExitStack()
                        with st2:
                            utb = [sb(st2, f"ut{i}", [128, 8, 512], BF16) for i in range(2)]
                            sq3 = sb(st2, "sq3", [128, 3, 512], BF16)
                            rsq = sb(st2, "rsq", [128, 512])
                            qn = sb(st2, "qn", [128, 3, 512], BF16)
                            kvn = sb(st2, "kvn", [128, 2, 512], BF16)
                            qf = sb(st2, "qf", [128, 8, 96])
                            kf = sb(st2, "kf", [128, 8, 64])
                            krf = sb(st2, "krf", [128, 1, 32])
                            sqs_ = [sb(st2, f"sqs{i}", [128, 8, 64]) for i in range(2)]
                            ss_ = [sb(st2, f"ss{i}", [128, 8]) for i in range(2)]
                            tmp3_ = [sb(st2, f"tmp3{i}", [128, 8, 64]) for i in range(2)]
                            qr = sb(st2, "qr", [128, 8, 32])
                            t1_ = [sb(st2, f"t1{i}", [128, 8, 32]) for i in range(2)]
                            t2_ = [sb(st2, f"t2{i}", [128, 8, 32]) for i in range(2)]
                            krn = sb(st2, "krn", [128, 1, 32])
                            krfin = sb(st2, "krfin", [128, 1, 32])
                            Qtm = sb(st2, "Qtm", [128, 8, 96], BF16)
                            Ktm = sb(st2, "Ktm", [128, 8, 96], BF16)
                            QTs = [sb(st2, f"QTs{i}", [128, 8, 512], BF16) for i in range(1)] * 2
                            cs = [sb(st2, f"cs{i}", [128, 64]) for i in range(2)]
                            it = 0
                            for j, (t0, n) in enumerate(CH):
                                if dbg_stop == "b1b_s":
                                    break
                                v = 0 if j < 8 else 1
                                lat = j < 8
                                need_q = lat or ctx_out
                                ut = utb[j % 2]
                                g.dma("sp", out=ut[:, :, 0:n], in_=UT[j])
                                for (nch, c0, dst, gvv, sc) in ((3, 0, qn, gq, 1.0 / 384), (2, 384, kvn, gkv, 1.0 / 256)):
                                    if nch == 3 and not need_q:
                                        continue
                                    for c in range(nch):
                                        bank = 4 + c % 2
                                        for kc in range(8):
                                            g.mm(PS[bank][:, 0:n], w2[:, kc, c0 + c * 128:c0 + (c + 1) * 128],
                                                 ut[:, kc, 0:n], start=(kc == 0), stop=(kc == 7))
                                        g.act(sq3[:, c, 0:n], PS[bank][:, 0:n], AF.Square, partial=(c > 0))
                                        g.op("dve", "tensor_copy", out=dst[:, c, 0:n], in_=PS[bank][:, 0:n], partial=(c > 0))
                                    for c in range(nch):
                                        g.mm(PS[3][:, 0:n], ones_b[:], sq3[:, c, 0:n], start=(c == 0), stop=(c == nch - 1))
                                    g.act(rsq[:, 0:n], PS[3][:, 0:n], AF.Sqrt, scale=sc, bias=eps_t[:, 0:1])
                                    g.op("dve", "reciprocal", out=rsq[:, 0:n], in_=rsq[:, 0:n])
                                    for c in range(nch):
                                        stt(dst[:, c, 0:n], dst[:, c, 0:n], gvv[:, c:c + 1], rsq[:, 0:n], ALU.mult, ALU.mult)
                                qts = QTs[j % 2]
                                for s in range(n // 128):
                                    if dbg_stop == "b1b_0":
                                        break
                                    tok = slice(s * 128, (s + 1) * 128)
                                    ti = t0 // 128 + s
                                    cst = cs[it % 2]
                                    it += 1
                                    if lat:
                                        g.dma("sp", out=cst[:], in_=rope_mla_d[t0 + s * 128:t0 + (s + 1) * 128, :])
                                    g.record()
                                    if need_q and dbg_stop != "b1b_k":
                                        for hb in range(2):
                                            bank = 4 + hb
                                            for c in range(3):
                                                g.mm(PS[bank][:, 0:384], qn[:, c, tok], wuq[:, c, hb * 384:(hb + 1) * 384],
                                                     start=(c == 0), stop=(c == 2))
                                            evac(qf[:, hb * 4:(hb + 1) * 4, :],
                                                 PS[bank][:, 0:384].rearrange("p (h w) -> p h w", h=4))
                                        sqs, ss, tmp3, t1, t2 = sqs_[0], ss_[0], tmp3_[0], t1_[0], t2_[0]
                                        headnorm(qf[:, :, 0:64], Qtm[:, :, 0:64], gvec[:, 0:64], 8, 64,
                                                 sqs[:, :, 0:64], ss[:, 0:8], tmp3[:, :, 0:64])
                                        if lat:
                                            headnorm(qf[:, :, 64:96], qr[:], gvec[:, 128:160], 8, 32,
                                                     sqs[:, :, 0:32], ss[:, 0:8], tmp3[:, :, 0:32])
                                            rope(qr[:], Qtm[:, :, 64:96], cst, 8, 32, t1[:], t2[:])
                                        else:
                                            headnorm(qf[:, :, 64:96], Qtm[:, :, 64:96], gvec[:, 128:160], 8, 32,
                                                     sqs[:, :, 0:32], ss[:, 0:8], tmp3[:, :, 0:32])
                                        for h in range(8):
                                            g.tr(psb(6)[0:96, h * 128:(h + 1) * 128], Qtm[:, h, :], ident_b[:],
                                                 partial=(h > 0))
                                        evac(qts[0:96, :, tok], psb(6)[0:96, :].rearrange("p (h t) -> p h t", h=8))
                                    qchain = g.stop()
                                    g.record()
                                    sqs, ss, tmp3, t1, t2 = sqs_[1], ss_[1], tmp3_[1], t1_[1], t2_[1]
                                    for hb in range(2):
                                        bank = 1 + hb
                                        for c in range(2):
                                            g.mm(PS[bank][:, 0:512], kvn[:, c, tok], wukv[:, c, hb * 512:(hb + 1) * 512],
                                                 start=(c == 0), stop=(c == 1))
                                        pv = PS[bank][:, 0:512].rearrange("p (h w) -> p h w", h=4)
                                        evac(kf[:, hb * 4:(hb + 1) * 4, :], pv[:, :, 0:64])
                                        evac(VV[:, ti, 2 * hb:2 * hb + 2, 0:64], pv[:, 0:4:2, 64:128])
                                        evac(VV[:, ti, 2 * hb:2 * hb + 2, 128:192], pv[:, 1:4:2, 64:128])
                                    for kc in range(8):
                                        g.mm(PS[7][:, 0:32], ut[:, kc, tok], w2[:, kc, 640:672],
                                             start=(kc == 0), stop=(kc == 7))
                                    evac(krf[:, 0, :], PS[7][:, 0:32])
                                    headnorm(kf[:], Ktm[:, :, 0:64], gvec[:, 64:128], 8, 64,
                                             sqs[:, :, 0:64], ss[:, 0:8], tmp3[:, :, 0:64])
                                    if lat:
                                        headnorm(krf[:], krn[:], gvec[:, 160:192], 1, 32,
                                                 sqs[:, 0:1, 0:32], ss[:, 0:1], tmp3[:, 0:1, 0:32])
                                        rope(krn[:], krfin[:], cst, 1, 32, t1[:, 0:1, :], t2[:, 0:1, :])
                                    else:
                                        headnorm(krf[:], krfin[:], gvec[:, 160:192], 1, 32,
                                                 sqs[:, 0:1, 0:32], ss[:, 0:1], tmp3[:, 0:1, 0:32])
                                    g.op("dve", "tensor_copy", out=Ktm[:, :, 64:96],
                                         in_=krfin[:, 0:1, :].to_broadcast([128, 8, 32]))
                                    for h in range(8):
                                        g.tr(psb(0)[0:96, h * 128:(h + 1) * 128], Ktm[:, h, :], ident_b[:],
                                             partial=(h > 0))
                                    evac(KT[0:96, :, t0 + s * 128:t0 + (s + 1) * 128],
                                         psb(0)[0:96, :].rearrange("p (h t) -> p h t", h=8))
                                    kchain = g.stop()
                                    g.interleave(qchain, kchain)
                                if need_q and dbg_stop not in ("b1b_0", "b1b_k"):
                                    g.dma("pool", out=QT[j][0:96], in_=qts[0:96, :, 0:n])
                            g.barrier()
                        if dbg_stop is not None and dbg_stop.startswith("b1b"):
                            return
                        st2 = ExitStack()
                        with st2:
                            qsb = [sb(st2, f"qs{i}", [128, 8, 512], BF16) for i in range(2)]
                            yab = [sb(st2, f"yas{i}", [128, 4, 512], BF16) for i in range(2)]
                            hsb = [sb(st2, f"hsm{i}", [128, 512]) for i in range(2)]
                            PT = [sb(st2, f"PT{i}", [128, 512], BF16) for i in range(4)]
                            wout = sb(st2, "wout", [128, 8, 1024], BF16)
                            for kc in range(8):
                                g.dma("pool", out=wout[:, kc, :], in_=w_out_ab_d[e, kc * 128:(kc + 1) * 128, :],
                                      partial=(kc > 0))
                            OT = sb(st2, "OT", [128, 4, 512], BF16)
                            rc = sb(st2, "rc", [128, 512])
                            sc_att = 96.0 ** -0.5
                            for qq in qsb:
                                g.op("dve", "memset", ap=qq[:], constant=0.0)
                            for j in (range(9) if ctx_out else range(8)):
                                t0, n = CH[j]
                                v = 0 if j < 8 else 1
                                kcs = list(range(34)) if j < 8 else [32, 33]
                                qs = qsb[j % 2]
                                hs = hsb
                                yas = yab[j % 2]
                                g.dma("sp", out=qs[0:96, :, 0:n], in_=QT[j][0:96])
                                g.dma("sp", out=yas[:, :, 0:n], in_=YA[j])
                                nsub = n // 128
                                groups = []
                                for h in range(8):
                                    def S_(kc, out, h=h, qs=qs):
                                        g.mm(out, KT[:, h, kc * 128:(kc + 1) * 128], qs[:, h, 0:n])

                                    def V_(kc, h=h):
                                        if h % 2 == 0:
                                            return VV[:, kc, h // 2, 0:128]
                                        return VV[:, kc, h // 2, 64:192]

                                    def fin_(ob, h=h):
                                        lo, hi = slice(0, 64), slice(64, 128)
                                        o_, d_ = (lo, hi) if h % 2 == 0 else (hi, lo)
                                        g.op("dve", "reciprocal", out=rc[o_, 0:n], in_=PS[ob][d_, 0:n])
                                        tt(OT[o_, h // 2, 0:n], PS[ob][o_, 0:n], rc[o_, 0:n], ALU.mult)
                                    groups.append({"kcs": kcs, "S": S_, "V": V_, "fin": fin_})
                                attn_steps(groups, n, sc_att, 128, PT, sbanks=(0, 1, 4, 5), LA=3)
                                attention_out_fm(l, v, j, n, hs, OT, 4, wout, yas)
                            g.barrier()

                def odd_layer(l, ctx_out):
                    o = l // 2
                    lam_init = 0.8 - 0.6 * math.exp(-0.3 * l)
                    st = ExitStack()
                    with st:
                        hsb = [sb(st, f"hs{i}", [128, 8, 512]) for i in range(2)]
                        utb = [sb(st, f"ut{i}", [128, 8, 512], BF16) for i in range(2)]
                        for j, (t0, n) in enumerate(CH):
                            v = 0 if j < 8 else 1
                            hs = hsb[j % 2]
                            ut = utb[j % 2]
                            g.dma("sp", out=hs[:, :, 0:n], in_=HTb[j])
                            norm_mod(hs[:, :, 0:n], ut[:, :, 0:n], n, l, 1, v, j % 2)
                            g.dma("pool", out=UT[j], in_=ut[:, :, 0:n])
                        g.barrier()
                    st = ExitStack()
                    with st:
                        KT = sb(st, "KT2", [128, 8, NT], BF16)
                        VV = sb(st, "VV2", [128, 34, 8, 129], BF16)
                        gvec = sb(st, "d_gvec", [128, 512])
                        gouts = sb(st, "d_gouts", [128, 128])
                        lt = sb(st, "d_lt", [128, 2, 64])
                        ls = sb(st, "d_ls", [128, 2])
                        neglam = sb(st, "d_neglam", [128, 1])
                        g.dma("sp", out=gvec[:], in_=diff_gvec_d[o:o + 1, :].partition_broadcast(128))
                        g.op("dve", "memset", ap=VV[:, :, :, VV[:].shape[3] - 1:VV[:].shape[3]], constant=1.0)
                        l4 = gvec[:, 256:512].rearrange("p (a b c) -> p a b c", a=2, b=2)
                        tt(lt[:], l4[:, :, 0, :], l4[:, :, 1, :], ALU.mult)
                        g.op("dve", "tensor_reduce", out=ls[:], in_=lt[:], axis=AX.X, op=ALU.add)
                        g.act(ls[:], ls[:], AF.Exp)
                        ts(neglam[:], ls[:, 1:2], ls[:, 0:1], -lam_init, ALU.subtract, ALU.add)
                        ts(gouts[:], gvec[:, 128:256], 1.0 - lam_init)
                        st2 = ExitStack()
                        with st2:
                            wp = [sb(st2, f"wp{i}", [128, 8, 1024], BF16) for i in range(1)] * 2
                            utl = [sb(st2, f"utl{i}", [128, 8, 512], BF16) for i in range(1)] * 2
                            qf_ = [sb(st2, f"dqf{i}", [128, 16, 64]) for i in range(2)]
                            ss_ = [sb(st2, f"dss{i}", [128, 16]) for i in range(2)]
                            qr_ = [sb(st2, f"dqr{i}", [128, 16, 64]) for i in range(1)] * 2
                            t1_ = [sb(st2, f"dt1{i}", [128, 16, 64]) for i in range(1)] * 2
                            t2_ = [sb(st2, f"dt2{i}", [128, 16, 64]) for i in range(1)] * 2
                            Qtm_ = [sb(st2, f"dQtm{i}", [128, 16, 64], BF16) for i in range(2)]
                            isub = 0
                            QTs = [sb(st2, f"dQTs{i}", [128, 8, 512], BF16) for i in range(1)] * 2
                            cs = [sb(st2, f"dcs{i}", [128, 128]) for i in range(2)]
                            it = 0
                            iu = 0
                            for pi, part in enumerate(("k", "v", "q")):
                                wt = wp[pi % 2]
                                c0 = {"q": 0, "k": 1024, "v": 2048}[part]
                                for kc in range(8):
                                    g.dma("pool", out=wt[:, kc, :], in_=w_in_c_d[o, kc * 128:(kc + 1) * 128, c0:c0 + 1024],
                                          partial=(kc > 0))
                                for j, (t0, n) in enumerate(CH):
                                    lat = j < 8
                                    if part == "q" and not (lat or ctx_out):
                                        continue
                                    ut = utl[iu % 2]
                                    iu += 1
                                    g.dma("sp", out=ut[:, :, 0:n], in_=UT[j])
                                    qts = QTs[j % 2]
                                    for s in range(n // 128):
                                        tok = slice(s * 128, (s + 1) * 128)
                                        ti = t0 // 128 + s
                                        isub += 1
                                        pp = isub % 2
                                        qf, ss, qr, t1, t2, Qtm = qf_[pp], ss_[pp], qr_[pp], t1_[pp], t2_[pp], Qtm_[pp]
                                        sqs, tmp3 = t1, t2
                                        for hb in range(2):
                                            bank = 2 * pp + hb
                                            for kc in range(8):
                                                g.mm(PS[bank][:, 0:512], ut[:, kc, tok], wt[:, kc, hb * 512:(hb + 1) * 512],
                                                     start=(kc == 0), stop=(kc == 7))
                                            if part == "v":
                                                evac(VV[:, ti, hb * 4:(hb + 1) * 4, 0:128],
                                                     PS[bank][:, 0:512].rearrange("p (h w) -> p h w", h=4))
                                            else:
                                                evac(qf[:, hb * 8:(hb + 1) * 8, :],
                                                     PS[bank][:, 0:512].rearrange("p (h w) -> p h w", h=8))
                                        if part == "v":
                                            continue
                                        gain = gvec[:, 0:64] if part == "q" else gvec[:, 64:128]
                                        if lat:
                                            cst = cs[it % 2]
                                            it += 1
                                            g.dma("sp", out=cst[:], in_=rope_diff_d[t0 + s * 128:t0 + (s + 1) * 128, :])
                                            headnorm(qf[:], qr[:], gain, 16, 64, sqs[:], ss[:], tmp3[:])
                                            rope(qr[:], Qtm[:], cst, 16, 64, t1[:], t2[:])
                                        else:
                                            headnorm(qf[:], Qtm[:], gain, 16, 64, sqs[:], ss[:], tmp3[:])
                                        for h in range(8):
                                            g.tr(psb(6 + pp)[:, h * 128:(h + 1) * 128],
                                                 Qtm[:, 2 * h:2 * h + 2, :].rearrange("p a b -> p (a b)"), ident_b[:],
                                                 partial=(h > 0))
                                        src = psb(6 + pp)[:, :].rearrange("p (h t) -> p h t", h=8)
                                        if part == "k":
                                            evac(KT[:, :, t0 + s * 128:t0 + (s + 1) * 128], src)
                                        else:
                                            evac(qts[:, :, tok], src)
                                    if part == "q":
                                        g.dma("pool", out=QT[j], in_=qts[:, :, 0:n])
                            g.barrier()
                        st2 = ExitStack()
                        with st2:
                            qsb = [sb(st2, f"qs{i}", [128, 8, 512], BF16) for i in range(1)] * 2
                            hsb = [sb(st2, f"hsm{i}", [128, 512]) for i in range(2)]
                            PT = [sb(st2, f"PT{i}", [128, 512], BF16) for i in range(4)]
                            wout = sb(st2, "woutc", [128, 8, 1024], BF16)
                            for kc in range(8):
                                g.dma("pool", out=wout[:, kc, :], in_=w_out_c_d[o, kc * 128:(kc + 1) * 128, :],
                                      partial=(kc > 0))
                            Otm = sb(st2, "dOtm", [128, 4, 1024], BF16)
                            OT = sb(st2, "dOT", [128, 8, 512], BF16)
                            Dm = sb(st2, "dDm", [128, 4, 128])
                            dsq = sb(st2, "ddsq", [128, 4, 128])
                            rcp = sb(st2, "drcp", [128, 4])
                            nl = sb(st2, "dnl", [128, 4])
                            hss = sb(st2, "dhss", [128, 4])
                            sc_att = 64.0 ** -0.5
                            qpad = [[sb(st2, f"qpad{m}{i}", [128, 512], BF16) for i in range(2)] for m in range(2)]
                            for m in range(2):
                                for i in range(2):
                                    g.op("dve", "memset", ap=qpad[m][i][:], constant=0.0)
                            for j in (range(9) if ctx_out else range(8)):
                                t0, n = CH[j]
                                v = 0 if j < 8 else 1
                                kcs = list(range(34)) if j < 8 else [32, 33]
                                qs = qsb[j % 2]
                                hs = hsb
                                g.dma("sp", out=qs[:, :, 0:n], in_=QT[j])
                                nsub = n // 128
                                groups = []
                                for h in range(8):
                                    for m in range(2):
                                        qp = qpad[m][h % 2]

                                        def prep_(h=h, m=m, qp=qp, qs=qs):
                                            pr = slice(m * 64, (m + 1) * 64)
                                            g.op("dve", "tensor_copy", out=qp[pr, 0:n], in_=qs[pr, h, 0:n])

                                        def S_(kc, out, h=h, qp=qp):
                                            g.mm(out, KT[:, h, kc * 128:(kc + 1) * 128], qp[:, 0:n])

                                        def V_(kc, h=h):
                                            return VV[:, kc, h, :]

                                        def fin_(ob, h=h, m=m):
                                            for s in range(nsub):
                                                g.op("dve", "reciprocal", out=rcp[:, s:s + 1], in_=PS[2 + s][:, 128:129])
                                                if m == 0:
                                                    ts(Dm[:, s, :], PS[2 + s][:, 0:128], rcp[:, s:s + 1])
                                                else:
                                                    tt(nl[:, s:s + 1], rcp[:, s:s + 1], neglam[:, 0:1], ALU.mult)
                                                    stt(Dm[:, s, :], PS[2 + s][:, 0:128], nl[:, s:s + 1], Dm[:, s, :])
                                            if m == 0:
                                                return
                                            tt(dsq[:, 0:nsub, :], Dm[:, 0:nsub, :], Dm[:, 0:nsub, :], ALU.mult)
                                            g.op("dve", "tensor_reduce", out=hss[:, 0:nsub], in_=dsq[:, 0:nsub, :],
                                                 axis=AX.X, op=ALU.add)
                                            g.act(hss[:, 0:nsub], hss[:, 0:nsub], AF.Sqrt, scale=1.0 / 128,
                                                  bias=eps_t[:, 0:1])
                                            g.op("dve", "reciprocal", out=hss[:, 0:nsub], in_=hss[:, 0:nsub])
                                            tt(dsq[:, 0:nsub, :], Dm[:, 0:nsub, :],
                                               hss[:, 0:nsub].unsqueeze(2).to_broadcast([128, nsub, 128]), ALU.mult)
                                            tt(Otm[:, 0:nsub, h * 128:(h + 1) * 128], dsq[:, 0:nsub, :],
                                               gouts[:, :].unsqueeze(1).to_broadcast([128, nsub, 128]), ALU.mult)
                                        groups.append({"kcs": kcs, "S": S_, "V": V_, "fin": fin_, "prep": prep_})
                                attn_steps(groups, n, sc_att, 129, PT, sbanks=(0, 1, 6, 7), LA=3)
                                attention_out(l, v, j, n, hs, Otm, OT, 8, wout, None)
                            g.barrier()

                def ffn_layer(l, ctx_out):
                    last = (l == DEPTH - 1)
                    st = ExitStack()
                    with st:
                        wup = sb(st, "wup", [128, 8, 2 * DFF], BF16)
                        wdn = sb(st, "wdn", [128, NFC, 1024], BF16)
                        cw = sb(st, "f_cw", [128, NFC, 3])
                        cb = sb(st, "f_cb", [128, NFC])
                        for kc in range(8):
                            for (a, b_) in ((0, 2048), (2048, 4096), (4096, 5632)):
                                g.dma("pool", out=wup[:, kc, a:b_], in_=w_up_d[l, kc * 128:(kc + 1) * 128, a:b_],
                                      partial=not (kc == 0 and a == 0))
                        for fc in range(NFC):
                            g.dma("pool", out=wdn[:, fc, :], in_=w_down_d[l, fc * 128:(fc + 1) * 128, :], partial=(fc > 0))
                        g.dma("sp", out=cw[:], in_=ffn_cw_d[l])
                        g.dma("sp", out=cb[:], in_=ffn_cb_d[l])
                        hsb = [sb(st, f"fhs{i}", [128, 8, 514]) for i in range(2)]
                        ut = sb(st, "fut", [128, 8, 514], BF16)
                        mt = sb(st, "fmt", [128, NFC, 512], BF16)
                        g.op("dve", "memset", ap=hsb[0][:], constant=0.0)
                        g.op("dve", "memset", ap=hsb[1][:], constant=0.0)
                        io = 0
                        jlist = list(range(9) if ctx_out else range(8))

                        def pre(jj):
                            j = jlist[jj]
                            t0, n = CH[j]
                            v = 0 if j < 8 else 1
                            W = n + 2
                            hs = hsb[jj % 2]
                            has_l = j not in (0, 8)
                            has_r = j not in (7, 8)
                            g.dma("sp", out=hs[:, :, 1:n + 1], in_=HTa[j])
                            if has_l:
                                g.dma("sp", out=hs[:, :, 0:1], in_=HTa[j - 1][:, :, 511:512], partial=True, allow_slow_non_contiguous=True)
                            if has_r:
                                g.dma("sp", out=hs[:, :, n + 1:n + 2], in_=HTa[j + 1][:, :, 0:1], partial=True, allow_slow_non_contiguous=True)
                            norm_mod(hs[:, :, 1:n + 1], ut[:, :, 1:n + 1], n, l, 2, v, 7)
                            if has_l or has_r:
                                norm_mod(hs[:, :, 0:W:n + 1], ut[:, :, 0:W:n + 1], 2, l, 2, v, 6)
                        pre(0)
                        for jj, j in enumerate(jlist):
                            t0, n = CH[j]
                            v = 0 if j < 8 else 1
                            W = n + 2
                            hs = hsb[jj % 2]
                            has_l = j not in (0, 8)
                            has_r = j not in (7, 8)
                            halo = has_l or has_r
                            for fc in range(NFC):
                                ba = fc % 3
                                hb_ = 6
                                for kc in range(8):
                                    g.mm(PS[ba][:, 0:n], wup[:, kc, fc * 128:(fc + 1) * 128], ut[:, kc, 1:n + 1],
                                         start=(kc == 0), stop=(kc == 7))
                                for kc in range(8):
                                    g.mm(PS[3 + ba][:, 0:n], wup[:, kc, DFF + fc * 128:DFF + (fc + 1) * 128],
                                         ut[:, kc, 1:n + 1], start=(kc == 0), stop=(kc == 7))
                                if halo:
                                    for kc in range(8):
                                        g.mm(PS[hb_][:, 0:2], wup[:, kc, DFF + fc * 128:DFF + (fc + 1) * 128],
                                             ut[:, kc, 0:W:n + 1], start=(kc == 0), stop=(kc == 7))
                                tmp = nt_t[fc % 2]
                                gp = PS[3 + ba]
                                g.act(tmp[:, 0:n], gp[:, 0:n], AF.Identity, scale=cw[:, fc, 1:2], bias=cb[:, fc:fc + 1])
                                stt(tmp[:, 1:n], gp[:, 0:n - 1], cw[:, fc, 0:1], tmp[:, 1:n])
                                stt(tmp[:, 0:n - 1], gp[:, 1:n], cw[:, fc, 2:3], tmp[:, 0:n - 1])
                                if has_l:
                                    stt(tmp[:, 0:1], PS[hb_][:, 0:1], cw[:, fc, 0:1], tmp[:, 0:1])
                                if has_r:
                                    stt(tmp[:, n - 1:n], PS[hb_][:, 1:2], cw[:, fc, 2:3], tmp[:, n - 1:n])
                                g.act(tmp[:, 0:n], tmp[:, 0:n], AF.Gelu_apprx_tanh)
                                tt(mt[:, fc, 0:n], tmp[:, 0:n], PS[ba][:, 0:n], ALU.mult)
                            if jj + 1 < len(jlist):
                                pre(jj + 1)
                            for oc in range(8):
                                bank = 7
                                for fc in range(NFC):
                                    g.mm(PS[bank][:, 0:n], wdn[:, fc, oc * 128:(oc + 1) * 128], mt[:, fc, 0:n],
                                         start=(fc == 0), stop=(fc == NFC - 1))
                                stt(hs[:, oc, 1:n + 1], PS[bank][:, 0:n], MOD[:, l, 40 + oc, v:v + 1], hs[:, oc, 1:n + 1])
                            if not last:
                                g.dma("pool", out=HTb[j], in_=hs[:, :, 1:n + 1])
                            else:
                                for s in range(n // 128):
                                    o_t = mt[:, 0:4, :].rearrange("p a b -> p (a b)").bitcast(F32)
                                    for half in range(2):
                                        bank = half
                                        for q4 in range(4):
                                            kc = half * 4 + q4
                                            g.tr(PS[bank][:, q4 * 128:(q4 + 1) * 128],
                                                 hs[:, kc, 1 + s * 128:1 + (s + 1) * 128], ident_f[:], partial=(q4 > 0))
                                        evac(o_t[:, half * 512:(half + 1) * 512], PS[bank][:, :])
                                    g.dma("pool", out=out_d[t0 + s * 128:t0 + (s + 1) * 128, :], in_=o_t)
                        g.barrier()

                stop = False
                if dbg_stop == "pro":
                    dump(HTb)
                    stop = True
                for l in range(DEPTH):
                    if stop:
                        break
                    ctx_out = l < DEPTH - 1
                    if l % 2 == 0:
                        even_layer(l, ctx_out)
                    else:
                        odd_layer(l, ctx_out)
                    if dbg_stop in ("b1a", "b2", "b1b", "b1b_0", "b1b_q", "b1b_k", "b1b_s"):
                        dump(HTb)
                        break
                    if dbg_stop == f"m{l}":
                        dump(HTa)
                        break
                    ffn_layer(l, ctx_out)
                    if dbg_stop == f"f{l}":
                        dump(HTb)
                        break
                g.barrier()
                g.drain()

        g1_ = G(nc, sems)
        gen(g1_)
        g2_ = G(nc, sems, plan=g1_.plan)
        gen(g2_)
        print(f"[kernel] ops={g2_.n} inst={g2_.ninst} waits={g2_.nwaits} "
              f"cnt={g2_.cnt} dmas={g2_.qn}", flush=True)
    return nc


def _fm(vec, nchunks):
    return np.ascontiguousarray(np.asarray(vec, np.float32).reshape(nchunks, 128).T)


def _rope_table(rot_dim):
    S = T
    rows = S // GRID_W
    row = np.repeat(np.arange(rows, dtype=np.float32), GRID_W)
    col = np.tile(np.arange(GRID_W, dtype=np.float32), rows)
    da = rot_dim // 2
    inv = (np.float32(10000.0) ** (-np.arange(0, da, 2, dtype=np.float32) / np.float32(da))).astype(np.float32)
    ar = (row[:, None] * inv).astype(np.float32)
    ac = (col[:, None] * inv).astype(np.float32)
    cr, sr, cc, sc = np.cos(ar), np.sin(ar), np.cos(ac), np.sin(ac)
    cos = np.concatenate([cr, cr, cc, cc], axis=1)
    sin = np.concatenate([-sr, sr, -sc, sc], axis=1)
    return np.ascontiguousarray(np.concatenate([cos, sin], axis=1).astype(np.float32))


def prep_inputs(inp, b):
    f = lambda a: np.ascontiguousarray(np.asarray(a, np.float32))
    m = {}
    m["x"] = f(inp["x"][b])
    m["ctx"] = f(inp["ctx"][b])
    cv = np.stack([_fm(inp["c"][b], 8), _fm(inp["c_ctx"], 8)], axis=-1)
    m["cvec"] = f(cv)
    m["ident"] = np.eye(128, dtype=np.float32)
    m["w_mod"] = f(inp["w_mod"])
    m["bmod"] = f(np.stack([_fm(inp["b_mod"][l], 48) for l in range(DEPTH)], axis=1))
    m["g1"] = f(np.stack([_fm(inp["g_norm1"][l], 8) for l in range(DEPTH)], axis=1))
    m["g2"] = f(np.stack([_fm(inp["g_norm2"][l], 8) for l in range(DEPTH)], axis=1))
    m["w_in_ab"] = f(inp["w_in_ab"])
    cw = np.asarray(inp["lru_conv_w"], np.float32).reshape(2, 4, 4, 128)
    m["lru_cw"] = f(cw.transpose(0, 3, 2, 1))
    m["lru_cb"] = f(np.asarray(inp["lru_conv_b"], np.float32).reshape(2, 4, 128).transpose(0, 2, 1))
    gw = np.asarray(inp["lru_gate_w"], np.float32)
    bd = np.zeros((2, 128, 2, 2, 4, 128), np.float32)
    for nb in range(8):
        c, hlf = nb // 2, nb % 2
        bd[:, hlf * 64:(hlf + 1) * 64, :, :, c, hlf * 64:(hlf + 1) * 64] = gw[:, :, :, nb].transpose(0, 3, 1, 2, 4)
    m["lru_bd"] = bd
    gbv = np.asarray(inp["lru_gate_b"], np.float32).reshape(2, 2, 2, 4, 128)
    m["lru_gb"] = f(gbv.transpose(0, 4, 1, 2, 3))
    lm = np.asarray(inp["lru_lambda"], np.float32).reshape(2, 2, 4, 128)
    m["lru_lam"] = f(lm.transpose(0, 3, 1, 2))
    m["mla_gq"] = f(np.stack([_fm(inp["mla_g_q"][e], 3) for e in range(2)], axis=0))
    m["mla_gkv"] = f(np.stack([_fm(inp["mla_g_kv"][e], 2) for e in range(2)], axis=0))
    m["w_uq"] = f(inp["mla_w_uq"])
    m["w_ukv"] = f(inp["mla_w_ukv"])
    m["mla_gvec"] = f(np.concatenate([inp["mla_gq_nope"], inp["mla_gk_nope"], inp["mla_gq_rope"], inp["mla_gk_rope"]],
                                     axis=1))
    m["w_out_ab"] = f(inp["w_out_ab"])
    m["w_in_c"] = f(inp["w_in_c"])
    m["diff_gvec"] = f(np.concatenate([inp["diff_gq"], inp["diff_gk"], inp["diff_g_out"],
                                       np.asarray(inp["diff_lam"]).reshape(2, 256)], axis=1))
    m["w_out_c"] = f(inp["w_out_c"])
    m["w_up"] = f(inp["ffn_w_up"])
    fcw = np.asarray(inp["ffn_conv_w"], np.float32).reshape(DEPTH, 3, NFC, 128)
    m["ffn_cw"] = f(fcw.transpose(0, 3, 2, 1))
    m["ffn_cb"] = f(np.asarray(inp["ffn_conv_b"], np.float32).reshape(DEPTH, NFC, 128).transpose(0, 2, 1))
    m["w_down"] = f(inp["ffn_w_down"])
    m["rope_mla"] = _rope_table(32)
    m["rope_diff"] = _rope_table(64)
    return m


def kernel(**inputs):
    nc = build_program()
    shared = None
    in_maps = []
    for b in range(8):
        m = prep_inputs(inputs, b)
        if shared is None:
            shared = m
        else:
            for k in m:
                if k not in ("x", "ctx", "cvec"):
                    m[k] = shared[k]
        in_maps.append(m)
    res = run_bass_kernel_spmd(nc, in_maps, core_ids=list(range(8)))
    out = np.stack([np.asarray(r["out"], np.float32) for r in res.results], axis=0)
    return out
```

```python
import math
from contextlib import ExitStack

import numpy as np
import concourse.bass as bass
import concourse.mybir as mybir
from concourse.bass_utils import run_bass_kernel_spmd

F32 = mybir.dt.float32
BF16 = mybir.dt.bfloat16
AF = mybir.ActivationFunctionType
ALU = mybir.AluOpType
AX = mybir.AxisListType

EPOCH = 50000
D = 1024
T = 4096
TC = 256
NT = T + TC
DEPTH = 4
EPS = 1e-6
DFF = 2816
NFC = DFF // 128
CH = [(j * 512, 512) for j in range(8)] + [(T, TC)]
GRID_W = 64


class Buf:
    __slots__ = ("w", "r")

    def __init__(self):
        self.w = {}
        self.r = {}


class G:
    def __init__(self, nc, sems, plan=None):
        self.nc = nc
        self.emit = plan is not None
        self.plan = plan if plan is not None else set()
        self.n = 0
        self.bufs = {}
        self.op_stream = []
        self.op_event = {}
        self.is_dma = []
        self.cnt = {"pe": 0, "act": 0, "dve": 0, "pool": 0}
        self.last = {}
        self.sems = sems
        self.qn = {"sp": 0, "pool": 0, "act": 0}
        self.seen_idx = {}
        self.seen_sem = {}
        self.eng = {"pe": nc.tensor, "act": nc.scalar, "dve": nc.vector, "pool": nc.gpsimd, "sp": nc.sync}
        self.nwaits = 0
        self.ninst = 0
        self._q = None

    def record(self):
        self._q = []

    def stop(self):
        q, self._q = self._q, None
        return q

    def interleave(self, *lists):
        n = max(len(x) for x in lists)
        for i in range(n):
            for lst in lists:
                if i < len(lst):
                    lst[i]()

    def buf(self, ap):
        nm = ap.name
        b = self.bufs.get(nm)
        if b is None:
            b = self.bufs[nm] = Buf()
        return b

    def _wait_sem(self, stream, sem, val):
        k2 = (stream, id(sem))
        if self.seen_sem.get(k2, 0) >= val:
            return
        self.seen_sem[k2] = val
        self.eng[stream].wait_ge(sem, val)
        self.nwaits += 1

    def _wait_event(self, stream, p):
        sem, val = self.op_event[p]
        if not self.is_dma[p]:
            k = (stream, self.op_stream[p])
            if self.seen_idx.get(k, -1) >= p:
                return
            self.seen_idx[k] = p
        self._wait_sem(stream, sem, val)

    def _begin(self, stream, dma, reads, writes, partial=False):
        idx = self.n
        self.n += 1
        self.op_stream.append(stream)
        self.is_dma.append(dma)
        deps = set()
        selfkey = ("d", idx) if dma else stream
        for ap in reads:
            b = self.buf(ap)
            for k, p in b.w.items():
                deps.add(p)
            if ap.name.startswith("ps"):
                for k, p in b.r.items():
                    if k != stream:
                        deps.add(p)
        for ap in writes:
            b = self.buf(ap)
            for k, p in b.r.items():
                if dma or k != stream:
                    deps.add(p)
            if not partial:
                for k, p in b.w.items():
                    if dma or k != stream:
                        deps.add(p)
        for ap in reads:
            self.buf(ap).r[selfkey] = idx
        for ap in writes:
            b = self.buf(ap)
            if partial:
                b.w[selfkey] = idx
            else:
                b.w = {selfkey: idx}
                b.r = {}
        if not dma:
            self.last[stream] = idx
        if not self.emit:
            self.plan.update(deps)
        else:
            for p in sorted(deps):
                self._wait_event(stream, p)
        return idx

    def _end(self, idx, stream, ins):
        self.ninst += 1
        if idx in self.plan:
            c = self.cnt[stream]
            sem = self.sems[stream][c // EPOCH]
            val = c % EPOCH + 1
            self.cnt[stream] = c + 1
            ins.then_inc(sem, 1)
            self.op_event[idx] = (sem, val)

    def op(self, stream, meth, partial=False, **kw):
        if self._q is not None:
            self._q.append(lambda: self.op(stream, meth, partial=partial, **kw))
            return
        reads, writes = [], []
        for k, v in kw.items():
            if isinstance(v, bass.AP):
                (writes if k in ("out", "accum_out", "ap") else reads).append(v)
        idx = self._begin(stream, False, reads, writes, partial)
        if self.emit:
            ins = getattr(self.eng[stream], meth)(**kw)
            self._end(idx, stream, ins)

    def mm(self, out, lhsT, rhs, start=True, stop=True):
        if self._q is not None:
            self._q.append(lambda: self.mm(out, lhsT, rhs, start=start, stop=stop))
            return
        idx = self._begin("pe", False, [lhsT, rhs], [out], partial=not start)
        if self.emit:
            ins = self.nc.tensor.matmul(out, lhsT, rhs, start=start, stop=stop)
            self._end(idx, "pe", ins)

    def tr(self, out, in_, ident, partial=False):
        if self._q is not None:
            self._q.append(lambda: self.tr(out, in_, ident, partial=partial))
            return
        idx = self._begin("pe", False, [in_, ident], [out], partial=partial)
        if self.emit:
            ins = self.nc.tensor.transpose(out, in_, ident)
            self._end(idx, "pe", ins)

    def act(self, out, in_, func, partial=False, **kw):
        self.op("act", "activation", partial=partial, out=out, in_=in_, func=func, **kw)

    def dma(self, q, out, in_, partial=False, **kw):
        if self._q is not None:
            self._q.append(lambda: self.dma(q, out, in_, partial=partial, **kw))
            return
        idx = self._begin(q, True, [in_], [out], partial)
        k = self.qn[q]
        self.qn[q] = k + 1
        pool = self.sems["q_" + q]
        P = len(pool)
        sem = pool[k % P]
        gen = k // P
        if self.emit:
            if gen > 0:
                self._wait_sem(q, sem, 16 * gen)
            ins = self.eng[q].dma_start(out=out, in_=in_, **kw)
            ins.then_inc(sem, 16)
            self.ninst += 1
        self.op_event[idx] = (sem, 16 * (gen + 1))

    def _dma_finals(self):
        res = []
        for q in ("sp", "pool", "act"):
            pool = self.sems["q_" + q]
            P = len(pool)
            k = self.qn[q]
            for j in range(min(P, k)):
                uses = (k - 1 - j) // P + 1
                res.append((pool[j], 16 * uses))
        return res

    def barrier(self):
        lasts = dict(self.last)
        if not self.emit:
            self.plan.update(lasts.values())
        else:
            fin = self._dma_finals()
            for s in ("pe", "act", "dve", "pool", "sp"):
                for e, p in lasts.items():
                    if e != s:
                        self._wait_event(s, p)
                for sem, val in fin:
                    self._wait_sem(s, sem, val)
        self.bufs = {}

    def drain(self):
        if not self.emit:
            return
        for sem, val in self._dma_finals():
            self._wait_sem("sp", sem, val)


def _rev(ap):
    n = ap.shape[-1]
    a = ap[:, n - 1:n]
    lst = [list(x) for x in a.ap]
    lst[-1] = [-1, n]
    return bass.AP(a.tensor, a.offset, lst)


def build_program(dbg_stop=None):
    nc = bass.Bass("TRN2", target_bir_lowering=False)

    def din(name, shape, dt=F32):
        return nc.dram_tensor(name, list(shape), dt, kind="ExternalInput").ap()

    def dscr(name, shape, dt=F32):
        return nc.dram_tensor(name, list(shape), dt, kind="Internal").ap()

    x_d = din("x", [T, D])
    ctx_d = din("ctx", [TC, D])
    cvec_d = din("cvec", [128, 8, 2])
    ident_d = din("ident", [128, 128])
    wmod_d = din("w_mod", [DEPTH, D, 6 * D])
    bmod_d = din("bmod", [128, DEPTH, 48])
    g1_d = din("g1", [128, DEPTH, 8])
    g2_d = din("g2", [128, DEPTH, 8])
    w_in_ab_d = din("w_in_ab", [2, D, 1696])
    lru_cw_d = din("lru_cw", [2, 128, 4, 4])
    lru_cb_d = din("lru_cb", [2, 128, 4])
    lru_bd_d = din("lru_bd", [2, 128, 2, 2, 4, 128])
    lru_gb_d = din("lru_gb", [2, 128, 2, 2, 4])
    lru_lam_d = din("lru_lam", [2, 128, 2, 4])
    mla_gq_d = din("mla_gq", [2, 128, 3])
    mla_gkv_d = din("mla_gkv", [2, 128, 2])
    w_uq_d = din("w_uq", [2, 384, 768])
    w_ukv_d = din("w_ukv", [2, 256, 1024])
    mla_gvec_d = din("mla_gvec", [2, 192])
    w_out_ab_d = din("w_out_ab", [2, D, D])
    w_in_c_d = din("w_in_c", [2, D, 3 * D])
    diff_gvec_d = din("diff_gvec", [2, 512])
    w_out_c_d = din("w_out_c", [2, D, D])
    w_up_d = din("w_up", [DEPTH, D, 2 * DFF])
    ffn_cw_d = din("ffn_cw", [DEPTH, 128, NFC, 3])
    ffn_cb_d = din("ffn_cb", [DEPTH, 128, NFC])
    w_down_d = din("w_down", [DEPTH, DFF, D])
    rope_mla_d = din("rope_mla", [T, 64])
    rope_diff_d = din("rope_diff", [T, 128])
    out_d = nc.dram_tensor("out", [T, D], F32, kind="ExternalOutput").ap()
    dbg_d = None
    if dbg_stop is not None:
        dbg_d = nc.dram_tensor("dbg", [128, 8, NT], F32, kind="ExternalOutput").ap()

    HTa = [dscr(f"HTa{j}", [128, 8, n]) for j, (t0, n) in enumerate(CH)]
    HTb = [dscr(f"HTb{j}", [128, 8, n]) for j, (t0, n) in enumerate(CH)]
    XR = [dscr(f"XR{j}", [128, 4, n]) for j, (t0, n) in enumerate(CH)]
    GR = [dscr(f"GR{j}", [128, 4, n], BF16) for j, (t0, n) in enumerate(CH)]
    YA = [dscr(f"YA{j}", [128, 4, n], BF16) for j, (t0, n) in enumerate(CH)]
    QT = [dscr(f"QT{j}", [128, 8, n], BF16) for j, (t0, n) in enumerate(CH)]
    UT = [dscr(f"UT{j}", [128, 8, n], BF16) for j, (t0, n) in enumerate(CH)]

    es = ExitStack()
    with es:
        sems = {}
        for k, n in (("pe", 4), ("act", 3), ("dve", 4), ("pool", 2), ("q_sp", 32), ("q_pool", 16), ("q_act", 2)):
            sems[k] = [es.enter_context(nc.semaphore(f"s_{k}_{i}")) for i in range(n)]
        PS = [es.enter_context(nc.psum_tensor(f"ps{i}", [128, 512], F32)) for i in range(8)]

        uid = [0]

        def gen(g):
            def sb(st, name, shape, dt=F32):
                uid[0] += 1
                return st.enter_context(nc.sbuf_tensor(f"{name}_u{uid[0]}", list(shape), dt))

            def psb(i):
                return PS[i][:].bitcast(BF16)

            alt = [0]

            def evac(out, in_):
                alt[0] ^= 1
                if alt[0]:
                    g.act(out, in_, AF.Copy)
                else:
                    g.op("dve", "tensor_copy", out=out, in_=in_)

            def tt(out, in0, in1, op, eng="dve"):
                g.op(eng, "tensor_tensor", out=out, in0=in0, in1=in1, op=op)

            def stt(out, in0, scalar, in1, op0=ALU.mult, op1=ALU.add):
                g.op("dve", "scalar_tensor_tensor", out=out, in0=in0, scalar=scalar, in1=in1, op0=op0, op1=op1)

            def ts(out, in0, s1, s2=None, op0=ALU.mult, op1=None, eng="dve"):
                if op1 is None:
                    g.op(eng, "tensor_scalar", out=out, in0=in0, scalar1=s1, scalar2=None, op0=op0)
                else:
                    g.op(eng, "tensor_scalar", out=out, in0=in0, scalar1=s1, scalar2=s2, op0=op0, op1=op1)

            gst = ExitStack()
            with gst:
                ident_f = sb(gst, "ident_f", [128, 128])
                ident_b = sb(gst, "ident_b", [128, 128], BF16)
                ones_b = sb(gst, "ones_b", [128, 128], BF16)
                eps_t = sb(gst, "eps_t", [128, 1])
                one_t = sb(gst, "one_t", [128, 1])
                SC = sb(gst, "SC", [128, 8, 2])
                MOD = sb(gst, "MOD", [128, DEPTH, 48, 2])
                A1 = sb(gst, "A1", [128, DEPTH, 8, 2])
                A2 = sb(gst, "A2", [128, DEPTH, 8, 2])
                bmod = sb(gst, "bmod", [128, DEPTH, 48])
                g1 = sb(gst, "g1", [128, DEPTH, 8])
                g2 = sb(gst, "g2", [128, DEPTH, 8])
                rs_t = sb(gst, "rs_t", [128, 512])
                nt_t = [sb(gst, f"nt_t{i}", [128, 512]) for i in range(2)]

                g.dma("sp", out=ident_f[:], in_=ident_d)
                g.dma("pool", out=ident_b[:], in_=ident_d)
                g.op("dve", "memset", ap=ones_b[:], constant=1.0)
                g.op("dve", "memset", ap=eps_t[:], constant=EPS)
                g.op("dve", "memset", ap=one_t[:], constant=1.0)
                g.dma("sp", out=SC[:], in_=cvec_d)
                g.dma("sp", out=bmod[:], in_=bmod_d)
                g.dma("sp", out=g1[:], in_=g1_d)
                g.dma("sp", out=g2[:], in_=g2_d)
                g.act(SC[:], SC[:], AF.Silu)

                def norm_mod(hs3, ut3, w, l, which, v, bank):
                    Acoef = (A1 if which == 1 else A2)
                    boff = 0 if which == 1 else 24
                    g.act(ut3, hs3, AF.Square)
                    for kc in range(8):
                        g.mm(PS[bank][:, 0:w], ones_b[:], ut3[:, kc, :], start=(kc == 0), stop=(kc == 7))
                    g.act(rs_t[:, 0:w], PS[bank][:, 0:w], AF.Sqrt, scale=1.0 / D, bias=eps_t[:, 0:1])
                    g.op("dve", "reciprocal", out=rs_t[:, 0:w], in_=rs_t[:, 0:w])
                    for kc in range(8):
                        t_ = nt_t[kc % 2]
                        stt(t_[:, 0:w], hs3[:, kc, :], Acoef[:, l, kc, v:v + 1], rs_t[:, 0:w], ALU.mult, ALU.mult)
                        g.act(ut3[:, kc, :], t_[:, 0:w], AF.Identity, bias=MOD[:, l, boff + kc, v:v + 1], partial=True)

                def headnorm(src3, dst3, gain, H, w, sqs, ss, tmp3):
                    tt(sqs, src3, src3, ALU.mult)
                    g.op("dve", "tensor_reduce", out=ss, in_=sqs, axis=AX.X, op=ALU.add)
                    g.act(ss, ss, AF.Sqrt, scale=1.0 / w, bias=eps_t[:, 0:1])
                    g.op("dve", "reciprocal", out=ss, in_=ss)
                    tt(tmp3, src3, ss.unsqueeze(2).to_broadcast([128, H, w]), ALU.mult)
                    tt(dst3, tmp3, gain.unsqueeze(1).to_broadcast([128, H, w]), ALU.mult)

                def rope(src3, dst3, cs, H, R, t1, t2):
                    cos = cs[:, 0:R].unsqueeze(1).to_broadcast([128, H, R])
                    sin4 = cs[:, R:2 * R].rearrange("p (a s n) -> p a s n", a=2, s=2)
                    s5 = src3.rearrange("p h (a s n) -> p h a s n", a=2, s=2)
                    t25 = t2.rearrange("p h (a s n) -> p h a s n", a=2, s=2)
                    n = R // 4
                    tt(t1, src3, cos, ALU.mult)
                    for s_ in range(2):
                        tt(t25[:, :, :, s_, :], s5[:, :, :, 1 - s_, :],
                           sin4[:, :, s_, :].unsqueeze(1).to_broadcast([128, H, 2, n]), ALU.mult)
                    tt(dst3, t1, t2, ALU.add)

                st = ExitStack()
                with st:
                    xs = [sb(st, f"xs{i}", [128, D]) for i in range(4)]
                    hsT = [sb(st, f"hsT{i}", [128, 8, 512]) for i in range(2)]
                    wm = [sb(st, f"wm{i}", [128, 8, 384]) for i in range(6)]
                    mod_state = {"it": 0}

                    def emit_mod(l, blk):
                        wsrc = wmod_d[l].rearrange("(k p) n -> p k n", p=128)
                        mb = 4 + l % 2
                        wt = wm[mod_state["it"] % 6]
                        mod_state["it"] += 1
                        g.dma("sp", out=wt[:], in_=wsrc[:, :, blk * 384:(blk + 1) * 384])
                        for o3 in range(3):
                            oc = blk * 3 + o3
                            for kc in range(8):
                                g.mm(PS[mb][:, oc * 2:oc * 2 + 2], wt[:, kc, o3 * 128:(o3 + 1) * 128],
                                     SC[:, kc, :], start=(kc == 0), stop=(kc == 7))
                        if blk == 15:
                            tt(MOD[:, l, :, :], PS[mb][:, 0:96].rearrange("p (a b) -> p a b", b=2),
                               bmod[:, l, :].unsqueeze(2).to_broadcast([128, 48, 2]), ALU.add)
                            stt(A1[:, l, :, :], MOD[:, l, 8:16, :], 1.0,
                                g1[:, l, :].unsqueeze(2).to_broadcast([128, 8, 2]), ALU.add, ALU.mult)
                            stt(A2[:, l, :, :], MOD[:, l, 32:40, :], 1.0,
                                g2[:, l, :].unsqueeze(2).to_broadcast([128, 8, 2]), ALU.add, ALU.mult)
                    mod_items = [(l, blk) for l in range(DEPTH) for blk in range(16)]
                    it = 0
                    for j, (t0, n) in enumerate(CH):
                        hs = hsT[j % 2]
                        for s in range(n // 128):
                            xt = xs[it % 4]
                            it += 1
                            src = x_d[t0 + s * 128:t0 + (s + 1) * 128, :] if j < 8 else ctx_d[s * 128:(s + 1) * 128, :]
                            g.dma("sp", out=xt[:], in_=src)
                            for half in range(2):
                                bank = (it * 2 + half) % 4
                                for q4 in range(4):
                                    kc = half * 4 + q4
                                    g.tr(PS[bank][:, q4 * 128:(q4 + 1) * 128], xt[:, kc * 128:(kc + 1) * 128],
                                         ident_f[:], partial=(q4 > 0))
                                evac(hs[:, half * 4:(half + 1) * 4, s * 128:(s + 1) * 128],
                                     PS[bank][:].rearrange("p (a b) -> p a b", a=4))
                            for _ in range(2):
                                if mod_items:
                                    emit_mod(*mod_items.pop(0))
                        g.dma("pool", out=HTb[j], in_=hs[:, :, 0:n])
                    while mod_items:
                        emit_mod(*mod_items.pop(0))
                    g.barrier()

                def dump(HT):
                    st = ExitStack()
                    with st:
                        t = sb(st, "dump_t", [128, 8, 512])
                        for j, (t0, n) in enumerate(CH):
                            g.dma("sp", out=t[:, :, 0:n], in_=HT[j])
                            g.dma("pool", out=dbg_d[:, :, t0:t0 + n], in_=t[:, :, 0:n])
                        g.barrier()

                def attn_steps(groups, n, sc_att, ncol, PT, sbanks=(0, 1), LA=1):
                    nsub = n // 128
                    steps = [(gi, i) for gi, grp in enumerate(groups) for i in range(len(grp["kcs"]))]

                    def emitS(t):
                        gi, i = steps[t]
                        grp = groups[gi]
                        if i == 0 and grp.get("prep") is not None:
                            grp["prep"]()
                        grp["S"](grp["kcs"][i], PS[sbanks[t % len(sbanks)]][:, 0:n])
                    for t0_ in range(min(LA, len(steps))):
                        emitS(t0_)
                    for t, (gi, i) in enumerate(steps):
                        grp = groups[gi]
                        nk = len(grp["kcs"])
                        if t + LA < len(steps):
                            emitS(t + LA)
                        pt = PT[t % len(PT)]
                        g.act(pt[:, 0:n], PS[sbanks[t % len(sbanks)]][:, 0:n], AF.Exp, scale=sc_att)
                        ob = 2 + gi % 2
                        if ncol == 128:
                            g.mm(PS[ob][:, 0:n], grp["V"](grp["kcs"][i]), pt[:, 0:n], start=(i == 0), stop=(i == nk - 1))
                        else:
                            rhs = grp["V"](grp["kcs"][i])
                            for s in range(nsub):
                                g.mm(PS[2 + s][:, 0:ncol], pt[:, s * 128:(s + 1) * 128], rhs,
                                     start=(i == 0), stop=(i == nk - 1))
                        if i == nk - 1:
                            grp["fin"](ob)

                def attention_out_fm(l, v, j, n, hs, OT, nfc, wout, ya):
                    for oc in range(8):
                        bank = 6 + oc % 2
                        rhs = []
                        if ya is not None:
                            rhs += [ya[:, kc, 0:n] for kc in range(4)]
                        rhs += [OT[:, fc, 0:n] for fc in range(nfc)]
                        for kc in range(8):
                            g.mm(PS[bank][:, 0:n], wout[:, kc, oc * 128:(oc + 1) * 128], rhs[kc],
                                 start=(kc == 0), stop=(kc == 7))
                        hsm = hs[oc % 2]
                        g.dma("sp", out=hsm[:, 0:n], in_=HTb[j][:, oc, :])
                        stt(hsm[:, 0:n], PS[bank][:, 0:n], MOD[:, l, 16 + oc, v:v + 1], hsm[:, 0:n])
                        g.dma("pool", out=HTa[j][:, oc, :], in_=hsm[:, 0:n])

                def attention_out(l, v, j, n, hs, Otm, OT, nfc, wout, ya):
                    nsub = n // 128
                    blocks = [(s, fc) for s in range(nsub) for fc in range(nfc)]
                    for r0 in range(0, len(blocks), 8):
                        grp = blocks[r0:r0 + 8]
                        bank = 6 + (r0 // 8) % 2
                        for i, (s, fc) in enumerate(grp):
                            g.tr(psb(bank)[:, i * 128:(i + 1) * 128], Otm[:, s, fc * 128:(fc + 1) * 128], ident_b[:],
                                 partial=(i > 0))
                        for i, (s, fc) in enumerate(grp):
                            if i == 0 or grp[i - 1][0] != s:
                                cnt = sum(1 for (s2, _) in grp[i:] if s2 == s)
                                fc0 = fc
                                evac(OT[:, fc0:fc0 + cnt, s * 128:(s + 1) * 128],
                                     psb(bank)[:, i * 128:(i + cnt) * 128].rearrange("p (f t) -> p f t", f=cnt))
                    for oc in range(8):
                        bank = 6 + oc % 2
                        rhs = []
                        if ya is not None:
                            rhs += [ya[:, kc, 0:n] for kc in range(4)]
                        rhs += [OT[:, fc, 0:n] for fc in range(nfc)]
                        for kc in range(8):
                            g.mm(PS[bank][:, 0:n], wout[:, kc, oc * 128:(oc + 1) * 128], rhs[kc],
                                 start=(kc == 0), stop=(kc == 7))
                        hsm = hs[oc % 2]
                        g.dma("sp", out=hsm[:, 0:n], in_=HTb[j][:, oc, :])
                        stt(hsm[:, 0:n], PS[bank][:, 0:n], MOD[:, l, 16 + oc, v:v + 1], hsm[:, 0:n])
                        g.dma("pool", out=HTa[j][:, oc, :], in_=hsm[:, 0:n])

                def even_layer(l, ctx_out):
                    e = l // 2
                    st = ExitStack()
                    with st:
                        w1 = sb(st, "w_in1", [128, 8, 1024], BF16)
                        for kc in range(8):
                            g.dma("pool", out=w1[:, kc, :], in_=w_in_ab_d[e, kc * 128:(kc + 1) * 128, 0:1024],
                                  partial=(kc > 0))
                        hsb = [sb(st, f"hs{i}", [128, 8, 512]) for i in range(2)]
                        utb = [sb(st, f"ut{i}", [128, 8, 512], BF16) for i in range(2)]
                        xrs = [sb(st, f"xrs{i}", [128, 4, 512]) for i in range(2)]
                        grs = [sb(st, f"grs{i}", [128, 4, 512], BF16) for i in range(2)]
                        for j, (t0, n) in enumerate(CH):
                            v = 0 if j < 8 else 1
                            hs = hsb[j % 2]
                            ut = utb[j % 2]
                            g.dma("sp", out=hs[:, :, 0:n], in_=HTb[j])
                            norm_mod(hs[:, :, 0:n], ut[:, :, 0:n], n, l, 1, v, 0)
                            g.dma("pool", out=UT[j], in_=ut[:, :, 0:n])
                            for c in range(4):
                                bank = 1 + c % 2
                                for kc in range(8):
                                    g.mm(PS[bank][:, 0:n], w1[:, kc, c * 128:(c + 1) * 128], ut[:, kc, 0:n],
                                         start=(kc == 0), stop=(kc == 7))
                                evac(xrs[j % 2][:, c, 0:n], PS[bank][:, 0:n])
                            for c in range(4):
                                bank = 3 + c % 2
                                for kc in range(8):
                                    g.mm(PS[bank][:, 0:n], w1[:, kc, 512 + c * 128:512 + (c + 1) * 128], ut[:, kc, 0:n],
                                         start=(kc == 0), stop=(kc == 7))
                                g.act(grs[j % 2][:, c, 0:n], PS[bank][:, 0:n], AF.Gelu_apprx_tanh)
                            g.dma("pool", out=XR[j], in_=xrs[j % 2][:, :, 0:n])
                            g.dma("pool", out=GR[j], in_=grs[j % 2][:, :, 0:n])
                        g.barrier()
                    if dbg_stop == "b1a":
                        return
                    st = ExitStack()
                    with st:
                        xr = sb(st, "l_xr", [128, NT])
                        xc = sb(st, "l_xc", [128, NT])
                        xcb = sb(st, "l_xcb", [128, NT], BF16)
                        Rts = [sb(st, f"l_R{i}", [128, NT]) for i in range(2)]
                        Its = [sb(st, f"l_I{i}", [128, NT]) for i in range(2)]
                        Ats = [sb(st, f"l_A{i}", [128, NT]) for i in range(2)]
                        grt = sb(st, "l_gr", [128, NT], BF16)
                        ya = sb(st, "l_ya", [128, NT], BF16)
                        bd = sb(st, "l_bd", [128, 2, 2, 4, 128], BF16)
                        gb = sb(st, "l_gb", [128, 2, 2, 4])
                        lam = sb(st, "l_lam", [128, 2, 4])
                        cw = sb(st, "l_cw", [128, 4, 4])
                        cb = sb(st, "l_cb", [128, 4])
                        cf = sb(st, "l_cf", [128, 2, 4])
                        cf2 = sb(st, "l_cf2", [128, 2, 4])
                        z = sb(st, "l_z", [128, 2, 4])
                        pz = sb(st, "l_pz", [128, 2, 4])
                        mk = sb(st, "l_mk", [128, 2, 4])
                        g.dma("pool", out=bd[:], in_=lru_bd_d[e])
                        g.dma("sp", out=gb[:], in_=lru_gb_d[e])
                        g.dma("sp", out=lam[:], in_=lru_lam_d[e])
                        g.dma("sp", out=cw[:], in_=lru_cw_d[e])
                        g.dma("sp", out=cb[:], in_=lru_cb_d[e])
                        g.act(z[:], lam[:], AF.Exp, scale=-1.0)
                        ts(pz[:], z[:], -0.25, 1.0 / 3.0, ALU.mult, ALU.add)
                        tt(pz[:], pz[:], z[:], ALU.mult)
                        ts(pz[:], pz[:], -0.5, None, ALU.add)
                        tt(pz[:], pz[:], z[:], ALU.mult)
                        ts(pz[:], pz[:], 1.0, None, ALU.add)
                        tt(pz[:], pz[:], z[:], ALU.mult)
                        g.act(cf[:], z[:], AF.Ln, bias=one_t[:, 0:1])
                        ts(mk[:], z[:], 0.05, None, ALU.is_lt)
                        tt(pz[:], pz[:], cf[:], ALU.subtract)
                        tt(pz[:], pz[:], mk[:], ALU.mult)
                        tt(cf[:], cf[:], pz[:], ALU.add)
                        ts(cf2[:], cf[:], -16.0)
                        ts(cf[:], cf[:], -8.0)
                        for c in range(4):
                            for j, (t0, n) in enumerate(CH):
                                g.dma("sp", out=xr[:, t0:t0 + n], in_=XR[j][:, c, :], partial=(j > 0))
                                g.dma("sp", out=grt[:, t0:t0 + n], in_=GR[j][:, c, :], partial=(j > 0))
                            g.act(xc[:], xr[:], AF.Identity, scale=cw[:, c, 1:2], bias=cb[:, c:c + 1])
                            for (a, b_) in ((0, T), (T, NT)):
                                stt(xc[:, a + 1:b_], xr[:, a:b_ - 1], cw[:, c, 0:1], xc[:, a + 1:b_])
                                stt(xc[:, a:b_ - 1], xr[:, a + 1:b_], cw[:, c, 2:3], xc[:, a:b_ - 1])
                                stt(xc[:, a:b_ - 2], xr[:, a + 2:b_], cw[:, c, 3:4], xc[:, a:b_ - 2])
                            g.op("pool", "tensor_copy", out=xcb[:], in_=xc[:])
                            for d in range(2):
                                Rt, It, At = Rts[d], Its[d], Ats[d]
                                for j, (t0, n) in enumerate(CH):
                                    b0 = (j % 2) * 2 + 4 * d
                                    g.mm(PS[b0][:, 0:n], bd[:, d, 0, c, :], xcb[:, t0:t0 + n])
                                    g.act(Rt[:, t0:t0 + n], PS[b0][:, 0:n], AF.Sigmoid, bias=gb[:, d, 0, c:c + 1],
                                          partial=(j > 0))
                                    g.mm(PS[b0 + 1][:, 0:n], bd[:, d, 1, c, :], xcb[:, t0:t0 + n])
                                    g.act(It[:, t0:t0 + n], PS[b0 + 1][:, 0:n], AF.Sigmoid, bias=gb[:, d, 1, c:c + 1],
                                          partial=(j > 0))
                                g.act(At[:], Rt[:], AF.Exp, scale=cf[:, d, c:c + 1])
                                g.act(Rt[:], Rt[:], AF.Exp, scale=cf2[:, d, c:c + 1])
                                ts(Rt[:], Rt[:], 1.0, -1.0, ALU.min, ALU.mult)
                                g.act(Rt[:], Rt[:], AF.Sqrt, bias=one_t[:, 0:1])
                                tt(It[:], It[:], Rt[:], ALU.mult)
                                tt(It[:], It[:], xc[:], ALU.mult)

                                def scan(o, a_, b_, init):
                                    g.op("dve", "tensor_tensor_scan", out=o, data0=a_, data1=b_, initial=init,
                                         op0=ALU.mult, op1=ALU.add)
                                if d == 0:
                                    scan(Rt[:, T:NT], At[:, T:NT], It[:, T:NT], 0.0)
                                    prev = Rt[:, NT - 1:NT]
                                    for k in range(4):
                                        a, b_ = k * 1024, (k + 1) * 1024
                                        scan(Rt[:, a:b_], At[:, a:b_], It[:, a:b_], prev)
                                        prev = Rt[:, b_ - 1:b_]
                                    g.op("pool", "tensor_copy", out=xr[:], in_=Rt[:])
                                else:
                                    scan(_rev(Rt[:, T:NT]), _rev(At[:, T:NT]), _rev(It[:, T:NT]), 0.0)
                                    prev = Rt[:, T:T + 1]
                                    for k in range(3, -1, -1):
                                        a, b_ = k * 1024, (k + 1) * 1024
                                        scan(_rev(Rt[:, a:b_]), _rev(At[:, a:b_]), _rev(It[:, a:b_]), prev)
                                        prev = Rt[:, a:a + 1]
                                    tt(xr[:], xr[:], Rt[:], ALU.add)
                            tt(ya[:], xr[:], grt[:], ALU.mult)
                            for j, (t0, n) in enumerate(CH):
                                g.dma("pool", out=YA[j][:, c, :], in_=ya[:, t0:t0 + n])
                        g.barrier()
                    if dbg_stop == "b2":
                        return
                    st = ExitStack()
                    with st:
                        KT = sb(st, "KT", [128, 8, NT], BF16)
                        VV = sb(st, "VV", [128, 34, 4, 192], BF16)
                        w2 = sb(st, "w_in2", [128, 8, 672], BF16)
                        wuq = sb(st, "wuq", [128, 3, 768], BF16)
                        wukv = sb(st, "wukv", [128, 2, 1024], BF16)
                        gq = sb(st, "m_gq", [128, 3])
                        gkv = sb(st, "m_gkv", [128, 2])
                        gvec = sb(st, "m_gvec", [128, 192])
                        for kc in range(8):
                            g.dma("pool", out=w2[:, kc, :], in_=w_in_ab_d[e, kc * 128:(kc + 1) * 128, 1024:1696],
                                  partial=(kc > 0))
                        for c in range(3):
                            g.dma("pool", out=wuq[:, c, :], in_=w_uq_d[e, c * 128:(c + 1) * 128, :], partial=(c > 0))
                        for c in range(2):
                            g.dma("pool", out=wukv[:, c, :], in_=w_ukv_d[e, c * 128:(c + 1) * 128, :], partial=(c > 0))
                        g.dma("sp", out=gq[:], in_=mla_gq_d[e])
                        g.dma("sp", out=gkv[:], in_=mla_gkv_d[e])
                        g.dma("sp", out=gvec[:], in_=mla_gvec_d[e:e + 1, :].partition_broadcast(128))
                        g.op("dve", "memset", ap=VV[:, :, :, 64:128], constant=1.0)
                        for h in range(8):
                            g.op("dve", "memset", ap=KT[:, h, :], constant=0.0, partial=(h > 0))
                        st2 = ExitStack()
                        with st2:
                            utb = [sb(st2, f"ut{i}", [128, 8, 512], BF16) for i in range(2)]
                            sq3 = sb(st2, "sq3", [128, 3, 512], BF16)
                            rsq = sb(st2, "rsq", [128, 512])
                            qn = sb(st2, "qn", [128, 3, 512], BF16)
                            kvn = sb(st2, "kvn", [128, 2, 512], BF16)
                            qf = sb(st2, "qf", [128, 8, 96])
                            kf = sb(st2, "kf", [128, 8, 64])
                            krf = sb(st2, "krf", [128, 1, 32])
                            sqs_ = [sb(st2, f"sqs{i}", [128, 8, 64]) for i in range(2)]
                            ss_ = [sb(st2, f"ss{i}", [128, 8]) for i in range(2)]
                            tmp3_ = [sb(st2, f"tmp3{i}", [128, 8, 64]) for i in range(2)]
                            qr = sb(st2, "qr", [128, 8, 32])
                            t1_ = [sb(st2, f"t1{i}", [128, 8, 32]) for i in range(2)]
                            t2_ = [sb(st2, f"t2{i}", [128, 8, 32]) for i in range(2)]
                            krn = sb(st2, "krn", [128, 1, 32])
                            krfin = sb(st2, "krfin", [128, 1, 32])
                            Qtm = sb(st2, "Qtm", [128, 8, 96], BF16)
                            Ktm = sb(st2, "Ktm", [128, 8, 96], BF16)
                            QTs = [sb(st2, f"QTs{i}", [128, 8, 512], BF16) for i in range(1)] * 2
                            cs = [sb(st2, f"cs{i}", [128, 64]) for i in range(2)]
                            it = 0
                            for j, (t0, n) in enumerate(CH):
                                if dbg_stop == "b1b_s":
                                    break
                                v = 0 if j < 8 else 1
                                lat = j < 8
                                need_q = lat or ctx_out
                                ut = utb[j % 2]
                                g.dma("sp", out=ut[:, :, 0:n], in_=UT[j])
                                for (nch, c0, dst, gvv, sc) in ((3, 0, qn, gq, 1.0 / 384), (2, 384, kvn, gkv, 1.0 / 256)):
                                    if nch == 3 and not need_q:
                                        continue
                                    for c in range(nch):
                                        bank = 4 + c % 2
                                        for kc in range(8):
                                            g.mm(PS[bank][:, 0:n], w2[:, kc, c0 + c * 128:c0 + (c + 1) * 128],
                                                 ut[:, kc, 0:n], start=(kc == 0), stop=(kc == 7))
                                        g.act(sq3[:, c, 0:n], PS[bank][:, 0:n], AF.Square, partial=(c > 0))
                                        g.op("dve", "tensor_copy", out=dst[:, c, 0:n], in_=PS[bank][:, 0:n], partial=(c > 0))
                                    for c in range(nch):
                                        g.mm(PS[3][:, 0:n], ones_b[:], sq3[:, c, 0:n], start=(c == 0), stop=(c == nch - 1))
                                    g.act(rsq[:, 0:n], PS[3][:, 0:n], AF.Sqrt, scale=sc, bias=eps_t[:, 0:1])
                                    g.op("dve", "reciprocal", out=rsq[:, 0:n], in_=rsq[:, 0:n])
                                    for c in range(nch):
                                        stt(dst[:, c, 0:n], dst[:, c, 0:n], gvv[:, c:c + 1], rsq[:, 0:n], ALU.mult, ALU.mult)
                                qts = QTs[j % 2]
                                for s in range(n // 128):
                                    if dbg_stop == "b1b_0":
                                        break
                                    tok = slice(s * 128, (s + 1) * 128)
                                    ti = t0 // 128 + s
                                    cst = cs[it % 2]
                                    it += 1
                                    if lat:
                                        g.dma("sp", out=cst[:], in_=rope_mla_d[t0 + s * 128:t0 + (s + 1) * 128, :])
                                    g.record()
                                    if need_q and dbg_stop != "b1b_k":
                                        for hb in range(2):
                                            bank = 4 + hb
                                            for c in range(3):
                                                g.mm(PS[bank][:, 0:384], qn[:, c, tok], wuq[:, c, hb * 384:(hb + 1) * 384],
                                                     start=(c == 0), stop=(c == 2))
                                            evac(qf[:, hb * 4:(hb + 1) * 4, :],
                                                 PS[bank][:, 0:384].rearrange("p (h w) -> p h w", h=4))
                                        sqs, ss, tmp3, t1, t2 = sqs_[0], ss_[0], tmp3_[0], t1_[0], t2_[0]
                                        headnorm(qf[:, :, 0:64], Qtm[:, :, 0:64], gvec[:, 0:64], 8, 64,
                                                 sqs[:, :, 0:64], ss[:, 0:8], tmp3[:, :, 0:64])
                                        if lat:
                                            headnorm(qf[:, :, 64:96], qr[:], gvec[:, 128:160], 8, 32,
                                                     sqs[:, :, 0:32], ss[:, 0:8], tmp3[:, :, 0:32])
                                            rope(qr[:], Qtm[:, :, 64:96], cst, 8, 32, t1[:], t2[:])
                                        else:
                                            headnorm(qf[:, :, 64:96], Qtm[:, :, 64:96], gvec[:, 128:160], 8, 32,
                                                     sqs[:, :, 0:32], ss[:, 0:8], tmp3[:, :, 0:32])
                                        for h in range(8):
                                            g.tr(psb(6)[0:96, h * 128:(h + 1) * 128], Qtm[:, h, :], ident_b[:],
                                                 partial=(h > 0))
                                        evac(qts[0:96, :, tok], psb(6)[0:96, :].rearrange("p (h t) -> p h t", h=8))
                                    qchain = g.stop()
                                    g.record()
                                    sqs, ss, tmp3, t1, t2 = sqs_[1], ss_[1], tmp3_[1], t1_[1], t2_[1]
                                    for hb in range(2):
                                        bank = 1 + hb
                                        for c in range(2):
                                            g.mm(PS[bank][:, 0:512], kvn[:, c, tok], wukv[:, c, hb * 512:(hb + 1) * 512],
                                                 start=(c == 0), stop=(c == 1))
                                        pv = PS[bank][:, 0:512].rearrange("p (h w) -> p h w", h=4)
                                        evac(kf[:, hb * 4:(hb + 1) * 4, :], pv[:, :, 0:64])
                                        evac(VV[:, ti, 2 * hb:2 * hb + 2, 0:64], pv[:, 0:4:2, 64:128])
                                        evac(VV[:, ti, 2 * hb:2 * hb + 2, 128:192], pv[:, 1:4:2, 64:128])
                                    for kc in range(8):
                                        g.mm(PS[7][:, 0:32], ut[:, kc, tok], w2[:, kc, 640:672],
                                             start=(kc == 0), stop=(kc == 7))
                                    evac(krf[:, 0, :], PS[7][:, 0:32])
                                    headnorm(kf[:], Ktm[:, :, 0:64], gvec[:, 64:128], 8, 64,
                                             sqs[:, :, 0:64], ss[:, 0:8], tmp3[:, :, 0:64])
                                    if lat:
                                        headnorm(krf[:], krn[:], gvec[:, 160:192], 1, 32,
                                                 sqs[:, 0:1, 0:32], ss[:, 0:1], tmp3[:, 0:1, 0:32])
                                        rope(krn[:], krfin[:], cst, 1, 32, t1[:, 0:1, :], t2[:, 0:1, :])
                                    else:
                                        headnorm(krf[:], krfin[:], gvec[:, 160:192], 1, 32,
                                                 sqs[:, 0:1, 0:32], ss[:, 0:1], tmp3[:, 0:1, 0:32])
                                    g.op("dve", "tensor_copy", out=Ktm[:, :, 64:96],
                                         in_=krfin[:, 0:1, :].to_broadcast([128, 8, 32]))
                                    for h in range(8):
                                        g.tr(psb(0)[0:96, h * 128:(h + 1) * 128], Ktm[:, h, :], ident_b[:],
                                             partial=(h > 0))
                                    evac(KT[0:96, :, t0 + s * 128:t0 + (s + 1) * 128],
                                         psb(0)[0:96, :].rearrange("p (h t) -> p h t", h=8))
                                    kchain = g.stop()
                                    g.interleave(qchain, kchain)
                                if need_q and dbg_stop not in ("b1b_0", "b1b_k"):
                                    g.dma("pool", out=QT[j][0:96], in_=qts[0:96, :, 0:n])
                            g.barrier()
                        if dbg_stop is not None and dbg_stop.startswith("b1b"):
                            return
                        st2 = ExitStack()
                        with st2:
                            qsb = [sb(st2, f"qs{i}", [128, 8, 512], BF16) for i in range(2)]
                            yab = [sb(st2, f"yas{i}", [128, 4, 512], BF16) for i in range(2)]
                            hsb = [sb(st2, f"hsm{i}", [128, 512]) for i in range(2)]
                            PT = [sb(st2, f"PT{i}", [128, 512], BF16) for i in range(4)]
                            wout = sb(st2, "wout", [128, 8, 1024], BF16)
                            for kc in range(8):
                                g.dma("pool", out=wout[:, kc, :], in_=w_out_ab_d[e, kc * 128:(kc + 1) * 128, :],
                                      partial=(kc > 0))
                            OT = sb(st2, "OT", [128, 4, 512], BF16)
                            rc = sb(st2, "rc", [128, 512])
                            sc_att = 96.0 ** -0.5
                            for qq in qsb:
                                g.op("dve", "memset", ap=qq[:], constant=0.0)
                            for j in (range(9) if ctx_out else range(8)):
                                t0, n = CH[j]
                                v = 0 if j < 8 else 1
                                kcs = list(range(34)) if j < 8 else [32, 33]
                                qs = qsb[j % 2]
                                hs = hsb
                                yas = yab[j % 2]
                                g.dma("sp", out=qs[0:96, :, 0:n], in_=QT[j][0:96])
                                g.dma("sp", out=yas[:, :, 0:n], in_=YA[j])
                                nsub = n // 128
                                groups = []
                                for h in range(8):
                                    def S_(kc, out, h=h, qs=qs):
                                        g.mm(out, KT[:, h, kc * 128:(kc + 1) * 128], qs[:, h, 0:n])

                                    def V_(kc, h=h):
                                        if h % 2 == 0:
                                            return VV[:, kc, h // 2, 0:128]
                                        return VV[:, kc, h // 2, 64:192]

                                    def fin_(ob, h=h):
                                        lo, hi = slice(0, 64), slice(64, 128)
                                        o_, d_ = (lo, hi) if h % 2 == 0 else (hi, lo)
                                        g.op("dve", "reciprocal", out=rc[o_, 0:n], in_=PS[ob][d_, 0:n])
                                        tt(OT[o_, h // 2, 0:n], PS[ob][o_, 0:n], rc[o_, 0:n], ALU.mult)
                                    groups.append({"kcs": kcs, "S": S_, "V": V_, "fin": fin_})
                                attn_steps(groups, n, sc_att, 128, PT, sbanks=(0, 1, 4, 5), LA=3)
                                attention_out_fm(l, v, j, n, hs, OT, 4, wout, yas)
                            g.barrier()

                def odd_layer(l, ctx_out):
                    o = l // 2
                    lam_init = 0.8 - 0.6 * math.exp(-0.3 * l)
                    st = ExitStack()
                    with st:
                        hsb = [sb(st, f"hs{i}", [128, 8, 512]) for i in range(2)]
                        utb = [sb(st, f"ut{i}", [128, 8, 512], BF16) for i in range(2)]
                        for j, (t0, n) in enumerate(CH):
                            v = 0 if j < 8 else 1
                            hs = hsb[j % 2]
                            ut = utb[j % 2]
                            g.dma("sp", out=hs[:, :, 0:n], in_=HTb[j])
                            norm_mod(hs[:, :, 0:n], ut[:, :, 0:n], n, l, 1, v, j % 2)
                            g.dma("pool", out=UT[j], in_=ut[:, :, 0:n])
                        g.barrier()
                    st = ExitStack()
                    with st:
                        KT = sb(st, "KT2", [128, 8, NT], BF16)
                        VV = sb(st, "VV2", [128, 34, 8, 129], BF16)
                        gvec = sb(st, "d_gvec", [128, 512])
                        gouts = sb(st, "d_gouts", [128, 128])
                        lt = sb(st, "d_lt", [128, 2, 64])
                        ls = sb(st, "d_ls", [128, 2])
                        neglam = sb(st, "d_neglam", [128, 1])
                        g.dma("sp", out=gvec[:], in_=diff_gvec_d[o:o + 1, :].partition_broadcast(128))
                        g.op("dve", "memset", ap=VV[:, :, :, VV[:].shape[3] - 1:VV[:].shape[3]], constant=1.0)
                        l4 = gvec[:, 256:512].rearrange("p (a b c) -> p a b c", a=2, b=2)
                        tt(lt[:], l4[:, :, 0, :], l4[:, :, 1, :], ALU.mult)
                        g.op("dve", "tensor_reduce", out=ls[:], in_=lt[:], axis=AX.X, op=ALU.add)
                        g.act(ls[:], ls[:], AF.Exp)
                        ts(neglam[:], ls[:, 1:2], ls[:, 0:1], -lam_init, ALU.subtract, ALU.add)
                        ts(gouts[:], gvec[:, 128:256], 1.0 - lam_init)
                        st2 = ExitStack()
                        with st2:
                            wp = [sb(st2, f"wp{i}", [128, 8, 1024], BF16) for i in range(1)] * 2
                            utl = [sb(st2, f"utl{i}", [128, 8, 512], BF16) for i in range(1)] * 2
                            qf_ = [sb(st2, f"dqf{i}", [128, 16, 64]) for i in range(2)]
                            ss_ = [sb(st2, f"dss{i}", [128, 16]) for i in range(2)]
                            qr_ = [sb(st2, f"dqr{i}", [128, 16, 64]) for i in range(1)] * 2
                            t1_ = [sb(st2, f"dt1{i}", [128, 16, 64]) for i in range(1)] * 2
                            t2_ = [sb(st2, f"dt2{i}", [128, 16, 64]) for i in range(1)] * 2
                            Qtm_ = [sb(st2, f"dQtm{i}", [128, 16, 64], BF16) for i in range(2)]
                            isub = 0
                            QTs = [sb(st2, f"dQTs{i}", [128, 8, 512], BF16) for i in range(1)] * 2
                            cs = [sb(st2, f"dcs{i}", [128, 128]) for i in range(2)]
                            it = 0
                            iu = 0
                            for pi, part in enumerate(("k", "v", "q")):
                                wt = wp[pi % 2]
                                c0 = {"q": 0, "k": 1024, "v": 2048}[part]
                                for kc in range(8):
                                    g.dma("pool", out=wt[:, kc, :], in_=w_in_c_d[o, kc * 128:(kc + 1) * 128, c0:c0 + 1024],
                                          partial=(kc > 0))
                                for j, (t0, n) in enumerate(CH):
                                    lat = j < 8
                                    if part == "q" and not (lat or ctx_out):
                                        continue
                                    ut = utl[iu % 2]
                                    iu += 1
                                    g.dma("sp", out=ut[:, :, 0:n], in_=UT[j])
                                    qts = QTs[j % 2]
                                    for s in range(n // 128):
                                        tok = slice(s * 128, (s + 1) * 128)
                                        ti = t0 // 128 + s
                                        isub += 1
                                        pp = isub % 2
                                        qf, ss, qr, t1, t2, Qtm = qf_[pp], ss_[pp], qr_[pp], t1_[pp], t2_[pp], Qtm_[pp]
                                        sqs, tmp3 = t1, t2
                                        for hb in range(2):
                                            bank = 2 * pp + hb
                                            for kc in range(8):
                                                g.mm(PS[bank][:, 0:512], ut[:, kc, tok], wt[:, kc, hb * 512:(hb + 1) * 512],
                                                     start=(kc == 0), stop=(kc == 7))
                                            if part == "v":
                                                evac(VV[:, ti, hb * 4:(hb + 1) * 4, 0:128],
                                                     PS[bank][:, 0:512].rearrange("p (h w) -> p h w", h=4))
                                            else:
                                                evac(qf[:, hb * 8:(hb + 1) * 8, :],
                                                     PS[bank][:, 0:512].rearrange("p (h w) -> p h w", h=8))
                                        if part == "v":
                                            continue
                                        gain = gvec[:, 0:64] if part == "q" else gvec[:, 64:128]
                                        if lat:
                                            cst = cs[it % 2]
                                            it += 1
                                            g.dma("sp", out=cst[:], in_=rope_diff_d[t0 + s * 128:t0 + (s + 1) * 128, :])
                                            headnorm(qf[:], qr[:], gain, 16, 64, sqs[:], ss[:], tmp3[:])
                                            rope(qr[:], Qtm[:], cst, 16, 64, t1[:], t2[:])
                                        else:
                                            headnorm(qf[:], Qtm[:], gain, 16, 64, sqs[:], ss[:], tmp3[:])
                                        for h in range(8):
                                            g.tr(psb(6 + pp)[:, h * 128:(h + 1) * 128],
                                                 Qtm[:, 2 * h:2 * h + 2, :].rearrange("p a b -> p (a b)"), ident_b[:],
                                                 partial=(h > 0))
                                        src = psb(6 + pp)[:, :].rearrange("p (h t) -> p h t", h=8)
                                        if part == "k":
                                            evac(KT[:, :, t0 + s * 128:t0 + (s + 1) * 128], src)
                                        else:
                                            evac(qts[:, :, tok], src)
                                    if part == "q":
                                        g.dma("pool", out=QT[j], in_=qts[:, :, 0:n])
                            g.barrier()
                        st2 = ExitStack()
                        with st2:
                            qsb = [sb(st2, f"qs{i}", [128, 8, 512], BF16) for i in range(1)] * 2
                            hsb = [sb(st2, f"hsm{i}", [128, 512]) for i in range(2)]
                            PT = [sb(st2, f"PT{i}", [128, 512], BF16) for i in range(4)]
                            wout = sb(st2, "woutc", [128, 8, 1024], BF16)
                            for kc in range(8):
                                g.dma("pool", out=wout[:, kc, :], in_=w_out_c_d[o, kc * 128:(kc + 1) * 128, :],
                                      partial=(kc > 0))
                            Otm = sb(st2, "dOtm", [128, 4, 1024], BF16)
                            OT = sb(st2, "dOT", [128, 8, 512], BF16)
                            Dm = sb(st2, "dDm", [128, 4, 128])
                            dsq = sb(st2, "ddsq", [128, 4, 128])
                            rcp = sb(st2, "drcp", [128, 4])
                            nl = sb(st2, "dnl", [128, 4])
                            hss = sb(st2, "dhss", [128, 4])
                            sc_att = 64.0 ** -0.5
                            qpad = [[sb(st2, f"qpad{m}{i}", [128, 512], BF16) for i in range(2)] for m in range(2)]
                            for m in range(2):
                                for i in range(2):
                                    g.op("dve", "memset", ap=qpad[m][i][:], constant=0.0)
                            for j in (range(9) if ctx_out else range(8)):
                                t0, n = CH[j]
                                v = 0 if j < 8 else 1
                                kcs = list(range(34)) if j < 8 else [32, 33]
                                qs = qsb[j % 2]
                                hs = hsb
                                g.dma("sp", out=qs[:, :, 0:n], in_=QT[j])
                                nsub = n // 128
                                groups = []
                                for h in range(8):
                                    for m in range(2):
                                        qp = qpad[m][h % 2]

                                        def prep_(h=h, m=m, qp=qp, qs=qs):
                                            pr = slice(m * 64, (m + 1) * 64)
                                            g.op("dve", "tensor_copy", out=qp[pr, 0:n], in_=qs[pr, h, 0:n])

                                        def S_(kc, out, h=h, qp=qp):
                                            g.mm(out, KT[:, h, kc * 128:(kc + 1) * 128], qp[:, 0:n])

                                        def V_(kc, h=h):
                                            return VV[:, kc, h, :]

                                        def fin_(ob, h=h, m=m):
                                            for s in range(nsub):
                                                g.op("dve", "reciprocal", out=rcp[:, s:s + 1], in_=PS[2 + s][:, 128:129])
                                                if m == 0:
                                                    ts(Dm[:, s, :], PS[2 + s][:, 0:128], rcp[:, s:s + 1])
                                                else:
                                                    tt(nl[:, s:s + 1], rcp[:, s:s + 1], neglam[:, 0:1], ALU.mult)
                                                    stt(Dm[:, s, :], PS[2 + s][:, 0:128], nl[:, s:s + 1], Dm[:, s, :])
                                            if m == 0:
                                                return
                                            tt(dsq[:, 0:nsub, :], Dm[:, 0:nsub, :], Dm[:, 0:nsub, :], ALU.mult)
                                            g.op("dve", "tensor_reduce", out=hss[:, 0:nsub], in_=dsq[:, 0:nsub, :],
                                                 axis=AX.X, op=ALU.add)
                                            g.act(hss[:, 0:nsub], hss[:, 0:nsub], AF.Sqrt, scale=1.0 / 128,
                                                  bias=eps_t[:, 0:1])
                                            g.op("dve", "reciprocal", out=hss[:, 0:nsub], in_=hss[:, 0:nsub])
                                            tt(dsq[:, 0:nsub, :], Dm[:, 0:nsub, :],
                                               hss[:, 0:nsub].unsqueeze(2).to_broadcast([128, nsub, 128]), ALU.mult)
                                            tt(Otm[:, 0:nsub, h * 128:(h + 1) * 128], dsq[:, 0:nsub, :],
                                               gouts[:, :].unsqueeze(1).to_broadcast([128, nsub, 128]), ALU.mult)
                                        groups.append({"kcs": kcs, "S": S_, "V": V_, "fin": fin_, "prep": prep_})
                                attn_steps(groups, n, sc_att, 129, PT, sbanks=(0, 1, 6, 7), LA=3)
                                attention_out(l, v, j, n, hs, Otm, OT, 8, wout, None)
                            g.barrier()

                def ffn_layer(l, ctx_out):
                    last = (l == DEPTH - 1)
                    st = ExitStack()
                    with st:
                        wup = sb(st, "wup", [128, 8, 2 * DFF], BF16)
                        wdn = sb(st, "wdn", [128, NFC, 1024], BF16)
                        cw = sb(st, "f_cw", [128, NFC, 3])
                        cb = sb(st, "f_cb", [128, NFC])
                        for kc in range(8):
                            for (a, b_) in ((0, 2048), (2048, 4096), (4096, 5632)):
                                g.dma("pool", out=wup[:, kc, a:b_], in_=w_up_d[l, kc * 128:(kc + 1) * 128, a:b_],
                                      partial=not (kc == 0 and a == 0))
                        for fc in range(NFC):
                            g.dma("pool", out=wdn[:, fc, :], in_=w_down_d[l, fc * 128:(fc + 1) * 128, :], partial=(fc > 0))
                        g.dma("sp", out=cw[:], in_=ffn_cw_d[l])
                        g.dma("sp", out=cb[:], in_=ffn_cb_d[l])
                        hsb = [sb(st, f"fhs{i}", [128, 8, 514]) for i in range(2)]
                        ut = sb(st, "fut", [128, 8, 514], BF16)
                        mt = sb(st, "fmt", [128, NFC, 512], BF16)
                        g.op("dve", "memset", ap=hsb[0][:], constant=0.0)
                        g.op("dve", "memset", ap=hsb[1][:], constant=0.0)
                        io = 0
                        jlist = list(range(9) if ctx_out else range(8))

                        def pre(jj):
                            j = jlist[jj]
                            t0, n = CH[j]
                            v = 0 if j < 8 else 1
                            W = n + 2
                            hs = hsb[jj % 2]
                            has_l = j not in (0, 8)
                            has_r = j not in (7, 8)
                            g.dma("sp", out=hs[:, :, 1:n + 1], in_=HTa[j])
                            if has_l:
                                g.dma("sp", out=hs[:, :, 0:1], in_=HTa[j - 1][:, :, 511:512], partial=True, allow_slow_non_contiguous=True)
                            if has_r:
                                g.dma("sp", out=hs[:, :, n + 1:n + 2], in_=HTa[j + 1][:, :, 0:1], partial=True, allow_slow_non_contiguous=True)
                            norm_mod(hs[:, :, 1:n + 1], ut[:, :, 1:n + 1], n, l, 2, v, 7)
                            if has_l or has_r:
                                norm_mod(hs[:, :, 0:W:n + 1], ut[:, :, 0:W:n + 1], 2, l, 2, v, 6)
                        pre(0)
                        for jj, j in enumerate(jlist):
                            t0, n = CH[j]
                            v = 0 if j < 8 else 1
                            W = n + 2
                            hs = hsb[jj % 2]
                            has_l = j not in (0, 8)
                            has_r = j not in (7, 8)
                            halo = has_l or has_r
                            pend = None
                            for fc in range(NFC):
                                ba = fc % 3
                                hb_ = 6
                                for kc in range(8):
                                    g.mm(PS[ba][:, 0:n], wup[:, kc, fc * 128:(fc + 1) * 128], ut[:, kc, 1:n + 1],
                                         start=(kc == 0), stop=(kc == 7))
                                for kc in range(8):
                                    g.mm(PS[3 + ba][:, 0:n], wup[:, kc, DFF + fc * 128:DFF + (fc + 1) * 128],
                                         ut[:, kc, 1:n + 1], start=(kc == 0), stop=(kc == 7))
                                if halo:
                                    for kc in range(8):
                                        g.mm(PS[hb_][:, 0:2], wup[:, kc, DFF + fc * 128:DFF + (fc + 1) * 128],
                                             ut[:, kc, 0:W:n + 1], start=(kc == 0), stop=(kc == 7))
                                tmp = nt_t[fc % 2]
                                gp = PS[3 + ba]
                                g.act(tmp[:, 0:n], gp[:, 0:n], AF.Identity, scale=cw[:, fc, 1:2], bias=cb[:, fc:fc + 1])
                                stt(tmp[:, 1:n], gp[:, 0:n - 1], cw[:, fc, 0:1], tmp[:, 1:n])
                                stt(tmp[:, 0:n - 1], gp[:, 1:n], cw[:, fc, 2:3], tmp[:, 0:n - 1])
                                if has_l:
                                    stt(tmp[:, 0:1], PS[hb_][:, 0:1], cw[:, fc, 0:1], tmp[:, 0:1])
                                if has_r:
                                    stt(tmp[:, n - 1:n], PS[hb_][:, 1:2], cw[:, fc, 2:3], tmp[:, n - 1:n])
                                g.act(tmp[:, 0:n], tmp[:, 0:n], AF.Gelu_apprx_tanh)
                                if pend is not None:
                                    tt(*pend, ALU.mult)
                                pend = (mt[:, fc, 0:n], tmp[:, 0:n], PS[ba][:, 0:n])
                            tt(*pend, ALU.mult)
                            pend = None
                            if jj + 1 < len(jlist):
                                pre(jj + 1)
                            for oc in range(8):
                                bank = 7
                                for fc in range(NFC):
                                    g.mm(PS[bank][:, 0:n], wdn[:, fc, oc * 128:(oc + 1) * 128], mt[:, fc, 0:n],
                                         start=(fc == 0), stop=(fc == NFC - 1))
                                stt(hs[:, oc, 1:n + 1], PS[bank][:, 0:n], MOD[:, l, 40 + oc, v:v + 1], hs[:, oc, 1:n + 1])
                            if not last:
                                g.dma("pool", out=HTb[j], in_=hs[:, :, 1:n + 1])
                            else:
                                for s in range(n // 128):
                                    o_t = mt[:, 0:4, :].rearrange("p a b -> p (a b)").bitcast(F32)
                                    for half in range(2):
                                        bank = half
                                        for q4 in range(4):
                                            kc = half * 4 + q4
                                            g.tr(PS[bank][:, q4 * 128:(q4 + 1) * 128],
                                                 hs[:, kc, 1 + s * 128:1 + (s + 1) * 128], ident_f[:], partial=(q4 > 0))
                                        evac(o_t[:, half * 512:(half + 1) * 512], PS[bank][:, :])
                                    g.dma("pool", out=out_d[t0 + s * 128:t0 + (s + 1) * 128, :], in_=o_t)
                        g.barrier()

                stop = False
                if dbg_stop == "pro":
                    dump(HTb)
                    stop = True
                for l in range(DEPTH):
                    if stop:
                        break
                    ctx_out = l < DEPTH - 1
                    if l % 2 == 0:
                        even_layer(l, ctx_out)
                    else:
                        odd_layer(l, ctx_out)
                    if dbg_stop in ("b1a", "b2", "b1b", "b1b_0", "b1b_q", "b1b_k", "b1b_s"):
                        dump(HTb)
                        break
                    if dbg_stop == f"m{l}":
                        dump(HTa)
                        break
                    ffn_layer(l, ctx_out)
                    if dbg_stop == f"f{l}":
                        dump(HTb)
                        break
                g.barrier()
                g.drain()

        g1_ = G(nc, sems)
        gen(g1_)
        g2_ = G(nc, sems, plan=g1_.plan)
        gen(g2_)
        print(f"[kernel] ops={g2_.n} inst={g2_.ninst} waits={g2_.nwaits} "
              f"cnt={g2_.cnt} dmas={g2_.qn}", flush=True)
    return nc


def _fm(vec, nchunks):
    return np.ascontiguousarray(np.asarray(vec, np.float32).reshape(nchunks, 128).T)


def _rope_table(rot_dim):
    S = T
    rows = S // GRID_W
    row = np.repeat(np.arange(rows, dtype=np.float32), GRID_W)
    col = np.tile(np.arange(GRID_W, dtype=np.float32), rows)
    da = rot_dim // 2
    inv = (np.float32(10000.0) ** (-np.arange(0, da, 2, dtype=np.float32) / np.float32(da))).astype(np.float32)
    ar = (row[:, None] * inv).astype(np.float32)
    ac = (col[:, None] * inv).astype(np.float32)
    cr, sr, cc, sc = np.cos(ar), np.sin(ar), np.cos(ac), np.sin(ac)
    cos = np.concatenate([cr, cr, cc, cc], axis=1)
    sin = np.concatenate([-sr, sr, -sc, sc], axis=1)
    return np.ascontiguousarray(np.concatenate([cos, sin], axis=1).astype(np.float32))


def prep_inputs(inp, b):
    f = lambda a: np.ascontiguousarray(np.asarray(a, np.float32))
    m = {}
    m["x"] = f(inp["x"][b])
    m["ctx"] = f(inp["ctx"][b])
    cv = np.stack([_fm(inp["c"][b], 8), _fm(inp["c_ctx"], 8)], axis=-1)
    m["cvec"] = f(cv)
    m["ident"] = np.eye(128, dtype=np.float32)
    m["w_mod"] = f(inp["w_mod"])
    m["bmod"] = f(np.stack([_fm(inp["b_mod"][l], 48) for l in range(DEPTH)], axis=1))
    m["g1"] = f(np.stack([_fm(inp["g_norm1"][l], 8) for l in range(DEPTH)], axis=1))
    m["g2"] = f(np.stack([_fm(inp["g_norm2"][l], 8) for l in range(DEPTH)], axis=1))
    m["w_in_ab"] = f(inp["w_in_ab"])
    cw = np.asarray(inp["lru_conv_w"], np.float32).reshape(2, 4, 4, 128)
    m["lru_cw"] = f(cw.transpose(0, 3, 2, 1))
    m["lru_cb"] = f(np.asarray(inp["lru_conv_b"], np.float32).reshape(2, 4, 128).transpose(0, 2, 1))
    gw = np.asarray(inp["lru_gate_w"], np.float32)
    bd = np.zeros((2, 128, 2, 2, 4, 128), np.float32)
    for nb in range(8):
        c, hlf = nb // 2, nb % 2
        bd[:, hlf * 64:(hlf + 1) * 64, :, :, c, hlf * 64:(hlf + 1) * 64] = gw[:, :, :, nb].transpose(0, 3, 1, 2, 4)
    m["lru_bd"] = bd
    gbv = np.asarray(inp["lru_gate_b"], np.float32).reshape(2, 2, 2, 4, 128)
    m["lru_gb"] = f(gbv.transpose(0, 4, 1, 2, 3))
    lm = np.asarray(inp["lru_lambda"], np.float32).reshape(2, 2, 4, 128)
    m["lru_lam"] = f(lm.transpose(0, 3, 1, 2))
    m["mla_gq"] = f(np.stack([_fm(inp["mla_g_q"][e], 3) for e in range(2)], axis=0))
    m["mla_gkv"] = f(np.stack([_fm(inp["mla_g_kv"][e], 2) for e in range(2)], axis=0))
    m["w_uq"] = f(inp["mla_w_uq"])
    m["w_ukv"] = f(inp["mla_w_ukv"])
    m["mla_gvec"] = f(np.concatenate([inp["mla_gq_nope"], inp["mla_gk_nope"], inp["mla_gq_rope"], inp["mla_gk_rope"]],
                                     axis=1))
    m["w_out_ab"] = f(inp["w_out_ab"])
    m["w_in_c"] = f(inp["w_in_c"])
    m["diff_gvec"] = f(np.concatenate([inp["diff_gq"], inp["diff_gk"], inp["diff_g_out"],
                                       np.asarray(inp["diff_lam"]).reshape(2, 256)], axis=1))
    m["w_out_c"] = f(inp["w_out_c"])
    m["w_up"] = f(inp["ffn_w_up"])
    fcw = np.asarray(inp["ffn_conv_w"], np.float32).reshape(DEPTH, 3, NFC, 128)
    m["ffn_cw"] = f(fcw.transpose(0, 3, 2, 1))
    m["ffn_cb"] = f(np.asarray(inp["ffn_conv_b"], np.float32).reshape(DEPTH, NFC, 128).transpose(0, 2, 1))
    m["w_down"] = f(inp["ffn_w_down"])
    m["rope_mla"] = _rope_table(32)
    m["rope_diff"] = _rope_table(64)
    return m


def kernel(**inputs):
    nc = build_program()
    shared = None
    in_maps = []
    for b in range(8):
        m = prep_inputs(inputs, b)
        if shared is None:
            shared = m
        else:
            for k in m:
                if k not in ("x", "ctx", "cvec"):
                    m[k] = shared[k]
        in_maps.append(m)
    res = run_bass_kernel_spmd(nc, in_maps, core_ids=list(range(8)))
    out = np.stack([np.asarray(r["out"], np.float32) for r in res.results], axis=0)
    return out
```

```python
import math
from contextlib import ExitStack

import numpy as np
import concourse.bass as bass
import concourse.mybir as mybir
from concourse.bass_utils import run_bass_kernel_spmd

F32 = mybir.dt.float32
BF16 = mybir.dt.bfloat16
AF = mybir.ActivationFunctionType
ALU = mybir.AluOpType
AX = mybir.AxisListType

EPOCH = 50000
D = 1024
T = 4096
TC = 256
NT = T + TC
DEPTH = 4
EPS = 1e-6
DFF = 2816
NFC = DFF // 128
CH = [(j * 512, 512) for j in range(8)] + [(T, TC)]
GRID_W = 64


class Buf:
    __slots__ = ("w", "r")

    def __init__(self):
        self.w = {}
        self.r = {}


class G:
    def __init__(self, nc, sems, plan=None):
        self.nc = nc
        self.emit = plan is not None
        self.plan = plan if plan is not None else set()
        self.n = 0
        self.bufs = {}
        self.op_stream = []
        self.op_event = {}
        self.is_dma = []
        self.cnt = {"pe": 0, "act": 0, "dve": 0, "pool": 0}
        self.last = {}
        self.sems = sems
        self.qn = {"sp": 0, "pool": 0, "act": 0}
        self.seen_idx = {}
        self.seen_sem = {}
        self.eng = {"pe": nc.tensor, "act": nc.scalar, "dve": nc.vector, "pool": nc.gpsimd, "sp": nc.sync}
        self.nwaits = 0
        self.ninst = 0
        self._q = None

    def record(self):
        self._q = []

    def stop(self):
        q, self._q = self._q, None
        return q

    def interleave(self, *lists):
        n = max(len(x) for x in lists)
        for i in range(n):
            for lst in lists:
                if i < len(lst):
                    lst[i]()

    def buf(self, ap):
        nm = ap.name
        b = self.bufs.get(nm)
        if b is None:
            b = self.bufs[nm] = Buf()
        return b

    def _wait_sem(self, stream, sem, val):
        k2 = (stream, id(sem))
        if self.seen_sem.get(k2, 0) >= val:
            return
        self.seen_sem[k2] = val
        self.eng[stream].wait_ge(sem, val)
        self.nwaits += 1

    def _wait_event(self, stream, p):
        sem, val = self.op_event[p]
        if not self.is_dma[p]:
            k = (stream, self.op_stream[p])
            if self.seen_idx.get(k, -1) >= p:
                return
            self.seen_idx[k] = p
        self._wait_sem(stream, sem, val)

    def _begin(self, stream, dma, reads, writes, partial=False):
        idx = self.n
        self.n += 1
        self.op_stream.append(stream)
        self.is_dma.append(dma)
        deps = set()
        selfkey = ("d", idx) if dma else stream
        for ap in reads:
            b = self.buf(ap)
            for k, p in b.w.items():
                deps.add(p)
            if ap.name.startswith("ps"):
                for k, p in b.r.items():
                    if k != stream:
                        deps.add(p)
        for ap in writes:
            b = self.buf(ap)
            for k, p in b.r.items():
                if dma or k != stream:
                    deps.add(p)
            if not partial:
                for k, p in b.w.items():
                    if dma or k != stream:
                        deps.add(p)
        for ap in reads:
            self.buf(ap).r[selfkey] = idx
        for ap in writes:
            b = self.buf(ap)
            if partial:
                b.w[selfkey] = idx
            else:
                b.w = {selfkey: idx}
                b.r = {}
        if not dma:
            self.last[stream] = idx
        if not self.emit:
            self.plan.update(deps)
        else:
            for p in sorted(deps):
                self._wait_event(stream, p)
        return idx

    def _end(self, idx, stream, ins):
        self.ninst += 1
        if idx in self.plan:
            c = self.cnt[stream]
            sem = self.sems[stream][c // EPOCH]
            val = c % EPOCH + 1
            self.cnt[stream] = c + 1
            ins.then_inc(sem, 1)
            self.op_event[idx] = (sem, val)

    def op(self, stream, meth, partial=False, **kw):
        if self._q is not None:
            self._q.append(lambda: self.op(stream, meth, partial=partial, **kw))
            return
        reads, writes = [], []
        for k, v in kw.items():
            if isinstance(v, bass.AP):
                (writes if k in ("out", "accum_out", "ap") else reads).append(v)
        idx = self._begin(stream, False, reads, writes, partial)
        if self.emit:
            ins = getattr(self.eng[stream], meth)(**kw)
            self._end(idx, stream, ins)

    def mm(self, out, lhsT, rhs, start=True, stop=True):
        if self._q is not None:
            self._q.append(lambda: self.mm(out, lhsT, rhs, start=start, stop=stop))
            return
        idx = self._begin("pe", False, [lhsT, rhs], [out], partial=not start)
        if self.emit:
            ins = self.nc.tensor.matmul(out, lhsT, rhs, start=start, stop=stop)
            self._end(idx, "pe", ins)

    def tr(self, out, in_, ident, partial=False):
        if self._q is not None:
            self._q.append(lambda: self.tr(out, in_, ident, partial=partial))
            return
        idx = self._begin("pe", False, [in_, ident], [out], partial=partial)
        if self.emit:
            ins = self.nc.tensor.transpose(out, in_, ident)
            self._end(idx, "pe", ins)

    def act(self, out, in_, func, partial=False, **kw):
        self.op("act", "activation", partial=partial, out=out, in_=in_, func=func, **kw)

    def dma(self, q, out, in_, partial=False, **kw):
        if self._q is not None:
            self._q.append(lambda: self.dma(q, out, in_, partial=partial, **kw))
            return
        idx = self._begin(q, True, [in_], [out], partial)
        k = self.qn[q]
        self.qn[q] = k + 1
        pool = self.sems["q_" + q]
        P = len(pool)
        sem = pool[k % P]
        gen = k // P
        if self.emit:
            if gen > 0:
                self._wait_sem(q, sem, 16 * gen)
            ins = self.eng[q].dma_start(out=out, in_=in_, **kw)
            ins.then_inc(sem, 16)
            self.ninst += 1
        self.op_event[idx] = (sem, 16 * (gen + 1))

    def _dma_finals(self):
        res = []
        for q in ("sp", "pool", "act"):
            pool = self.sems["q_" + q]
            P = len(pool)
            k = self.qn[q]
            for j in range(min(P, k)):
                uses = (k - 1 - j) // P + 1
                res.append((pool[j], 16 * uses))
        return res

    def barrier(self):
        lasts = dict(self.last)
        if not self.emit:
            self.plan.update(lasts.values())
        else:
            fin = self._dma_finals()
            for s in ("pe", "act", "dve", "pool", "sp"):
                for e, p in lasts.items():
                    if e != s:
                        self._wait_event(s, p)
                for sem, val in fin:
                    self._wait_sem(s, sem, val)
        self.bufs = {}

    def drain(self):
        if not self.emit:
            return
        for sem, val in self._dma_finals():
            self._wait_sem("sp", sem, val)


def _rev(ap):
    n = ap.shape[-1]
    a = ap[:, n - 1:n]
    lst = [list(x) for x in a.ap]
    lst[-1] = [-1, n]
    return bass.AP(a.tensor, a.offset, lst)


def build_program(dbg_stop=None):
    nc = bass.Bass("TRN2", target_bir_lowering=False)

    def din(name, shape, dt=F32):
        return nc.dram_tensor(name, list(shape), dt, kind="ExternalInput").ap()

    def dscr(name, shape, dt=F32):
        return nc.dram_tensor(name, list(shape), dt, kind="Internal").ap()

    x_d = din("x", [T, D])
    ctx_d = din("ctx", [TC, D])
    cvec_d = din("cvec", [128, 8, 2])
    ident_d = din("ident", [128, 128])
    wmod_d = din("w_mod", [DEPTH, D, 6 * D])
    bmod_d = din("bmod", [128, DEPTH, 48])
    g1_d = din("g1", [128, DEPTH, 8])
    g2_d = din("g2", [128, DEPTH, 8])
    w_in_ab_d = din("w_in_ab", [2, D, 1696])
    lru_cw_d = din("lru_cw", [2, 128, 4, 4])
    lru_cb_d = din("lru_cb", [2, 128, 4])
    lru_bd_d = din("lru_bd", [2, 128, 2, 2, 4, 128])
    lru_gb_d = din("lru_gb", [2, 128, 2, 2, 4])
    lru_lam_d = din("lru_lam", [2, 128, 2, 4])
    mla_gq_d = din("mla_gq", [2, 128, 3])
    mla_gkv_d = din("mla_gkv", [2, 128, 2])
    w_uq_d = din("w_uq", [2, 384, 768])
    w_ukv_d = din("w_ukv", [2, 256, 1024])
    mla_gvec_d = din("mla_gvec", [2, 192])
    w_out_ab_d = din("w_out_ab", [2, D, D])
    w_in_c_d = din("w_in_c", [2, D, 3 * D])
    diff_gvec_d = din("diff_gvec", [2, 512])
    w_out_c_d = din("w_out_c", [2, D, D])
    w_up_d = din("w_up", [DEPTH, D, 2 * DFF])
    ffn_cw_d = din("ffn_cw", [DEPTH, 128, NFC, 3])
    ffn_cb_d = din("ffn_cb", [DEPTH, 128, NFC])
    w_down_d = din("w_down", [DEPTH, DFF, D])
    rope_mla_d = din("rope_mla", [T, 64])
    rope_diff_d = din("rope_diff", [T, 128])
    out_d = nc.dram_tensor("out", [T, D], F32, kind="ExternalOutput").ap()
    dbg_d = None
    if dbg_stop is not None:
        dbg_d = nc.dram_tensor("dbg", [128, 8, NT], F32, kind="ExternalOutput").ap()

    HTa = [dscr(f"HTa{j}", [128, 8, n]) for j, (t0, n) in enumerate(CH)]
    HTb = [dscr(f"HTb{j}", [128, 8, n]) for j, (t0, n) in enumerate(CH)]
    XR = [dscr(f"XR{j}", [128, 4, n]) for j, (t0, n) in enumerate(CH)]
    GR = [dscr(f"GR{j}", [128, 4, n], BF16) for j, (t0, n) in enumerate(CH)]
    YA = [dscr(f"YA{j}", [128, 4, n], BF16) for j, (t0, n) in enumerate(CH)]
    QT = [dscr(f"QT{j}", [128, 8, n], BF16) for j, (t0, n) in enumerate(CH)]
    UT = [dscr(f"UT{j}", [128, 8, n], BF16) for j, (t0, n) in enumerate(CH)]

    es = ExitStack()
    with es:
        sems = {}
        for k, n in (("pe", 4), ("act", 3), ("dve", 4), ("pool", 2), ("q_sp", 32), ("q_pool", 16), ("q_act", 2)):
            sems[k] = [es.enter_context(nc.semaphore(f"s_{k}_{i}")) for i in range(n)]
        PS = [es.enter_context(nc.psum_tensor(f"ps{i}", [128, 512], F32)) for i in range(8)]

        uid = [0]

        def gen(g):
            def sb(st, name, shape, dt=F32):
                uid[0] += 1
                return st.enter_context(nc.sbuf_tensor(f"{name}_u{uid[0]}", list(shape), dt))

            def psb(i):
                return PS[i][:].bitcast(BF16)

            alt = [0]

            def evac(out, in_):
                alt[0] ^= 1
                if alt[0]:
                    g.act(out, in_, AF.Copy)
                else:
                    g.op("dve", "tensor_copy", out=out, in_=in_)

            def tt(out, in0, in1, op, eng="dve"):
                g.op(eng, "tensor_tensor", out=out, in0=in0, in1=in1, op=op)

            def stt(out, in0, scalar, in1, op0=ALU.mult, op1=ALU.add):
                g.op("dve", "scalar_tensor_tensor", out=out, in0=in0, scalar=scalar, in1=in1, op0=op0, op1=op1)

            def ts(out, in0, s1, s2=None, op0=ALU.mult, op1=None, eng="dve"):
                if op1 is None:
                    g.op(eng, "tensor_scalar", out=out, in0=in0, scalar1=s1, scalar2=None, op0=op0)
                else:
                    g.op(eng, "tensor_scalar", out=out, in0=in0, scalar1=s1, scalar2=s2, op0=op0, op1=op1)

            gst = ExitStack()
            with gst:
                ident_f = sb(gst, "ident_f", [128, 128])
                ident_b = sb(gst, "ident_b", [128, 128], BF16)
                ones_b = sb(gst, "ones_b", [128, 128], BF16)
                eps_t = sb(gst, "eps_t", [128, 1])
                one_t = sb(gst, "one_t", [128, 1])
                SC = sb(gst, "SC", [128, 8, 2])
                MOD = sb(gst, "MOD", [128, DEPTH, 48, 2])
                A1 = sb(gst, "A1", [128, DEPTH, 8, 2])
                A2 = sb(gst, "A2", [128, DEPTH, 8, 2])
                bmod = sb(gst, "bmod", [128, DEPTH, 48])
                g1 = sb(gst, "g1", [128, DEPTH, 8])
                g2 = sb(gst, "g2", [128, DEPTH, 8])
                rs_t = sb(gst, "rs_t", [128, 512])
                nt_t = [sb(gst, f"nt_t{i}", [128, 512]) for i in range(2)]

                g.dma("sp", out=ident_f[:], in_=ident_d)
                g.dma("pool", out=ident_b[:], in_=ident_d)
                g.op("dve", "memset", ap=ones_b[:], constant=1.0)
                g.op("dve", "memset", ap=eps_t[:], constant=EPS)
                g.op("dve", "memset", ap=one_t[:], constant=1.0)
                g.dma("sp", out=SC[:], in_=cvec_d)
                g.dma("sp", out=bmod[:], in_=bmod_d)
                g.dma("sp", out=g1[:], in_=g1_d)
                g.dma("sp", out=g2[:], in_=g2_d)
                g.act(SC[:], SC[:], AF.Silu)

                def norm_mod(hs3, ut3, w, l, which, v, bank):
                    Acoef = (A1 if which == 1 else A2)
                    boff = 0 if which == 1 else 24
                    g.act(ut3, hs3, AF.Square)
                    for kc in range(8):
                        g.mm(PS[bank][:, 0:w], ones_b[:], ut3[:, kc, :], start=(kc == 0), stop=(kc == 7))
                    g.act(rs_t[:, 0:w], PS[bank][:, 0:w], AF.Sqrt, scale=1.0 / D, bias=eps_t[:, 0:1])
                    g.op("dve", "reciprocal", out=rs_t[:, 0:w], in_=rs_t[:, 0:w])
                    for kc in range(8):
                        t_ = nt_t[kc % 2]
                        stt(t_[:, 0:w], hs3[:, kc, :], Acoef[:, l, kc, v:v + 1], rs_t[:, 0:w], ALU.mult, ALU.mult)
                        g.act(ut3[:, kc, :], t_[:, 0:w], AF.Identity, bias=MOD[:, l, boff + kc, v:v + 1], partial=True)

                def headnorm(src3, dst3, gain, H, w, sqs, ss, tmp3):
                    tt(sqs, src3, src3, ALU.mult)
                    g.op("dve", "tensor_reduce", out=ss, in_=sqs, axis=AX.X, op=ALU.add)
                    g.act(ss, ss, AF.Sqrt, scale=1.0 / w, bias=eps_t[:, 0:1])
                    g.op("dve", "reciprocal", out=ss, in_=ss)
                    tt(tmp3, src3, ss.unsqueeze(2).to_broadcast([128, H, w]), ALU.mult)
                    tt(dst3, tmp3, gain.unsqueeze(1).to_broadcast([128, H, w]), ALU.mult)

                def rope(src3, dst3, cs, H, R, t1, t2):
                    cos = cs[:, 0:R].unsqueeze(1).to_broadcast([128, H, R])
                    sin4 = cs[:, R:2 * R].rearrange("p (a s n) -> p a s n", a=2, s=2)
                    s5 = src3.rearrange("p h (a s n) -> p h a s n", a=2, s=2)
                    t25 = t2.rearrange("p h (a s n) -> p h a s n", a=2, s=2)
                    n = R // 4
                    tt(t1, src3, cos, ALU.mult)
                    for s_ in range(2):
                        tt(t25[:, :, :, s_, :], s5[:, :, :, 1 - s_, :],
                           sin4[:, :, s_, :].unsqueeze(1).to_broadcast([128, H, 2, n]), ALU.mult)
                    tt(dst3, t1, t2, ALU.add)

                st = ExitStack()
                with st:
                    xs = [sb(st, f"xs{i}", [128, D]) for i in range(4)]
                    hsT = [sb(st, f"hsT{i}", [128, 8, 512]) for i in range(2)]
                    wm = [sb(st, f"wm{i}", [128, 8, 384]) for i in range(6)]
                    mod_state = {"it": 0}

                    def emit_mod(l, blk):
                        wsrc = wmod_d[l].rearrange("(k p) n -> p k n", p=128)
                        mb = 4 + l % 2
                        wt = wm[mod_state["it"] % 6]
                        mod_state["it"] += 1
                        g.dma("sp", out=wt[:], in_=wsrc[:, :, blk * 384:(blk + 1) * 384])
                        for o3 in range(3):
                            oc = blk * 3 + o3
                            for kc in range(8):
                                g.mm(PS[mb][:, oc * 2:oc * 2 + 2], wt[:, kc, o3 * 128:(o3 + 1) * 128],
                                     SC[:, kc, :], start=(kc == 0), stop=(kc == 7))
                        if blk == 15:
                            tt(MOD[:, l, :, :], PS[mb][:, 0:96].rearrange("p (a b) -> p a b", b=2),
                               bmod[:, l, :].unsqueeze(2).to_broadcast([128, 48, 2]), ALU.add)
                            stt(A1[:, l, :, :], MOD[:, l, 8:16, :], 1.0,
                                g1[:, l, :].unsqueeze(2).to_broadcast([128, 8, 2]), ALU.add, ALU.mult)
                            stt(A2[:, l, :, :], MOD[:, l, 32:40, :], 1.0,
                                g2[:, l, :].unsqueeze(2).to_broadcast([128, 8, 2]), ALU.add, ALU.mult)
                    mod_items = [(l, blk) for l in range(DEPTH) for blk in range(16)]
                    it = 0
                    for j, (t0, n) in enumerate(CH):
                        hs = hsT[j % 2]
                        for s in range(n // 128):
                            xt = xs[it % 4]
                            it += 1
                            src = x_d[t0 + s * 128:t0 + (s + 1) * 128, :] if j < 8 else ctx_d[s * 128:(s + 1) * 128, :]
                            g.dma("sp", out=xt[:], in_=src)
                            for half in range(2):
                                bank = (it * 2 + half) % 4
                                for q4 in range(4):
                                    kc = half * 4 + q4
                                    g.tr(PS[bank][:, q4 * 128:(q4 + 1) * 128], xt[:, kc * 128:(kc + 1) * 128],
                                         ident_f[:], partial=(q4 > 0))
                                evac(hs[:, half * 4:(half + 1) * 4, s * 128:(s + 1) * 128],
                                     PS[bank][:].rearrange("p (a b) -> p a b", a=4))
                            for _ in range(2):
                                if mod_items:
                                    emit_mod(*mod_items.pop(0))
                        g.dma("pool", out=HTb[j], in_=hs[:, :, 0:n])
                    while mod_items:
                        emit_mod(*mod_items.pop(0))
                    g.barrier()

                def dump(HT):
                    st = ExitStack()
                    with st:
                        t = sb(st, "dump_t", [128, 8, 512])
                        for j, (t0, n) in enumerate(CH):
                            g.dma("sp", out=t[:, :, 0:n], in_=HT[j])
                            g.dma("pool", out=dbg_d[:, :, t0:t0 + n], in_=t[:, :, 0:n])
                        g.barrier()

                def attn_steps(groups, n, sc_att, ncol, PT, sbanks=(0, 1), LA=1):
                    nsub = n // 128
                    steps = [(gi, i) for gi, grp in enumerate(groups) for i in range(len(grp["kcs"]))]

                    def emitS(t):
                        gi, i = steps[t]
                        grp = groups[gi]
                        if i == 0 and grp.get("prep") is not None:
                            grp["prep"]()
                        grp["S"](grp["kcs"][i], PS[sbanks[t % len(sbanks)]][:, 0:n])
                    for t0_ in range(min(LA, len(steps))):
                        emitS(t0_)
                    for t, (gi, i) in enumerate(steps):
                        grp = groups[gi]
                        nk = len(grp["kcs"])
                        if t + LA < len(steps):
                            emitS(t + LA)
                        pt = PT[t % len(PT)]
                        g.act(pt[:, 0:n], PS[sbanks[t % len(sbanks)]][:, 0:n], AF.Exp, scale=sc_att)
                        ob = 2 + gi % 2
                        if ncol == 128:
                            g.mm(PS[ob][:, 0:n], grp["V"](grp["kcs"][i]), pt[:, 0:n], start=(i == 0), stop=(i == nk - 1))
                        else:
                            rhs = grp["V"](grp["kcs"][i])
                            for s in range(nsub):
                                g.mm(PS[2 + s][:, 0:ncol], pt[:, s * 128:(s + 1) * 128], rhs,
                                     start=(i == 0), stop=(i == nk - 1))
                        if i == nk - 1:
                            grp["fin"](ob)

                def attention_out_fm(l, v, j, n, hs, OT, nfc, wout, ya):
                    for oc in range(8):
                        bank = 6 + oc % 2
                        rhs = []
                        if ya is not None:
                            rhs += [ya[:, kc, 0:n] for kc in range(4)]
                        rhs += [OT[:, fc, 0:n] for fc in range(nfc)]
                        for kc in range(8):
                            g.mm(PS[bank][:, 0:n], wout[:, kc, oc * 128:(oc + 1) * 128], rhs[kc],
                                 start=(kc == 0), stop=(kc == 7))
                        hsm = hs[oc % 2]
                        g.dma("sp", out=hsm[:, 0:n], in_=HTb[j][:, oc, :])
                        stt(hsm[:, 0:n], PS[bank][:, 0:n], MOD[:, l, 16 + oc, v:v + 1], hsm[:, 0:n])
                        g.dma("pool", out=HTa[j][:, oc, :], in_=hsm[:, 0:n])

                def attention_out(l, v, j, n, hs, Otm, OT, nfc, wout, ya):
                    nsub = n // 128
                    blocks = [(s, fc) for s in range(nsub) for fc in range(nfc)]
                    for r0 in range(0, len(blocks), 8):
                        grp = blocks[r0:r0 + 8]
                        bank = 6 + (r0 // 8) % 2
                        for i, (s, fc) in enumerate(grp):
                            g.tr(psb(bank)[:, i * 128:(i + 1) * 128], Otm[:, s, fc * 128:(fc + 1) * 128], ident_b[:],
                                 partial=(i > 0))
                        for i, (s, fc) in enumerate(grp):
                            if i == 0 or grp[i - 1][0] != s:
                                cnt = sum(1 for (s2, _) in grp[i:] if s2 == s)
                                fc0 = fc
                                evac(OT[:, fc0:fc0 + cnt, s * 128:(s + 1) * 128],
                                     psb(bank)[:, i * 128:(i + cnt) * 128].rearrange("p (f t) -> p f t", f=cnt))
                    for oc in range(8):
                        bank = 6 + oc % 2
                        rhs = []
                        if ya is not None:
                            rhs += [ya[:, kc, 0:n] for kc in range(4)]
                        rhs += [OT[:, fc, 0:n] for fc in range(nfc)]
                        for kc in range(8):
                            g.mm(PS[bank][:, 0:n], wout[:, kc, oc * 128:(oc + 1) * 128], rhs[kc],
                                 start=(kc == 0), stop=(kc == 7))
                        hsm = hs[oc % 2]
                        g.dma("sp", out=hsm[:, 0:n], in_=HTb[j][:, oc, :])
                        stt(hsm[:, 0:n], PS[bank][:, 0:n], MOD[:, l, 16 + oc, v:v + 1], hsm[:, 0:n])
                        g.dma("pool", out=HTa[j][:, oc, :], in_=hsm[:, 0:n])

                def even_layer(l, ctx_out):
                    e = l // 2
                    st = ExitStack()
                    with st:
                        w1 = sb(st, "w_in1", [128, 8, 1024], BF16)
                        for kc in range(8):
                            g.dma("pool", out=w1[:, kc, :], in_=w_in_ab_d[e, kc * 128:(kc + 1) * 128, 0:1024],
                                  partial=(kc > 0))
                        hsb = [sb(st, f"hs{i}", [128, 8, 512]) for i in range(2)]
                        utb = [sb(st, f"ut{i}", [128, 8, 512], BF16) for i in range(2)]
                        xrs = [sb(st, f"xrs{i}", [128, 4, 512]) for i in range(2)]
                        grs = [sb(st, f"grs{i}", [128, 4, 512], BF16) for i in range(2)]
                        for j, (t0, n) in enumerate(CH):
                            v = 0 if j < 8 else 1
                            hs = hsb[j % 2]
                            ut = utb[j % 2]
                            g.dma("sp", out=hs[:, :, 0:n], in_=HTb[j])
                            norm_mod(hs[:, :, 0:n], ut[:, :, 0:n], n, l, 1, v, 0)
                            g.dma("pool", out=UT[j], in_=ut[:, :, 0:n])
                            for c in range(4):
                                bank = 1 + c % 2
                                for kc in range(8):
                                    g.mm(PS[bank][:, 0:n], w1[:, kc, c * 128:(c + 1) * 128], ut[:, kc, 0:n],
                                         start=(kc == 0), stop=(kc == 7))
                                evac(xrs[j % 2][:, c, 0:n], PS[bank][:, 0:n])
                            for c in range(4):
                                bank = 3 + c % 2
                                for kc in range(8):
                                    g.mm(PS[bank][:, 0:n], w1[:, kc, 512 + c * 128:512 + (c + 1) * 128], ut[:, kc, 0:n],
                                         start=(kc == 0), stop=(kc == 7))
                                g.act(grs[j % 2][:, c, 0:n], PS[bank][:, 0:n], AF.Gelu_apprx_tanh)
                            g.dma("pool", out=XR[j], in_=xrs[j % 2][:, :, 0:n])
                            g.dma("pool", out=GR[j], in_=grs[j % 2][:, :, 0:n])
                        g.barrier()
                    if dbg_stop == "b1a":
                        return
                    st = ExitStack()
                    with st:
                        xr = sb(st, "l_xr", [128, NT])
                        xc = sb(st, "l_xc", [128, NT])
                        xcb = sb(st, "l_xcb", [128, NT], BF16)
                        Rts = [sb(st, f"l_R{i}", [128, NT]) for i in range(2)]
                        Its = [sb(st, f"l_I{i}", [128, NT]) for i in range(2)]
                        Ats = [sb(st, f"l_A{i}", [128, NT]) for i in range(2)]
                        grt = sb(st, "l_gr", [128, NT], BF16)
                        ya = sb(st, "l_ya", [128, NT], BF16)
                        bd = sb(st, "l_bd", [128, 2, 2, 4, 128], BF16)
                        gb = sb(st, "l_gb", [128, 2, 2, 4])
                        lam = sb(st, "l_lam", [128, 2, 4])
                        cw = sb(st, "l_cw", [128, 4, 4])
                        cb = sb(st, "l_cb", [128, 4])
                        cf = sb(st, "l_cf", [128, 2, 4])
                        cf2 = sb(st, "l_cf2", [128, 2, 4])
                        z = sb(st, "l_z", [128, 2, 4])
                        pz = sb(st, "l_pz", [128, 2, 4])
                        mk = sb(st, "l_mk", [128, 2, 4])
                        g.dma("pool", out=bd[:], in_=lru_bd_d[e])
                        g.dma("sp", out=gb[:], in_=lru_gb_d[e])
                        g.dma("sp", out=lam[:], in_=lru_lam_d[e])
                        g.dma("sp", out=cw[:], in_=lru_cw_d[e])
                        g.dma("sp", out=cb[:], in_=lru_cb_d[e])
                        g.act(z[:], lam[:], AF.Exp, scale=-1.0)
                        ts(pz[:], z[:], -0.25, 1.0 / 3.0, ALU.mult, ALU.add)
                        tt(pz[:], pz[:], z[:], ALU.mult)
                        ts(pz[:], pz[:], -0.5, None, ALU.add)
                        tt(pz[:], pz[:], z[:], ALU.mult)
                        ts(pz[:], pz[:], 1.0, None, ALU.add)
                        tt(pz[:], pz[:], z[:], ALU.mult)
                        g.act(cf[:], z[:], AF.Ln, bias=one_t[:, 0:1])
                        ts(mk[:], z[:], 0.05, None, ALU.is_lt)
                        tt(pz[:], pz[:], cf[:], ALU.subtract)
                        tt(pz[:], pz[:], mk[:], ALU.mult)
                        tt(cf[:], cf[:], pz[:], ALU.add)
                        ts(cf2[:], cf[:], -16.0)
                        ts(cf[:], cf[:], -8.0)
                        for c in range(4):
                            for j, (t0, n) in enumerate(CH):
                                g.dma("sp", out=xr[:, t0:t0 + n], in_=XR[j][:, c, :], partial=(j > 0))
                                g.dma("sp", out=grt[:, t0:t0 + n], in_=GR[j][:, c, :], partial=(j > 0))
                            g.act(xc[:], xr[:], AF.Identity, scale=cw[:, c, 1:2], bias=cb[:, c:c + 1])
                            for (a, b_) in ((0, T), (T, NT)):
                                stt(xc[:, a + 1:b_], xr[:, a:b_ - 1], cw[:, c, 0:1], xc[:, a + 1:b_])
                                stt(xc[:, a:b_ - 1], xr[:, a + 1:b_], cw[:, c, 2:3], xc[:, a:b_ - 1])
                                stt(xc[:, a:b_ - 2], xr[:, a + 2:b_], cw[:, c, 3:4], xc[:, a:b_ - 2])
                            g.op("pool", "tensor_copy", out=xcb[:], in_=xc[:])
                            for d in range(2):
                                Rt, It, At = Rts[d], Its[d], Ats[d]
                                for j, (t0, n) in enumerate(CH):
                                    b0 = (j % 2) * 2 + 4 * d
                                    g.mm(PS[b0][:, 0:n], bd[:, d, 0, c, :], xcb[:, t0:t0 + n])
                                    g.act(Rt[:, t0:t0 + n], PS[b0][:, 0:n], AF.Sigmoid, bias=gb[:, d, 0, c:c + 1],
                                          partial=(j > 0))
                                    g.mm(PS[b0 + 1][:, 0:n], bd[:, d, 1, c, :], xcb[:, t0:t0 + n])
                                    g.act(It[:, t0:t0 + n], PS[b0 + 1][:, 0:n], AF.Sigmoid, bias=gb[:, d, 1, c:c + 1],
                                          partial=(j > 0))
                                g.act(At[:], Rt[:], AF.Exp, scale=cf[:, d, c:c + 1])
                                g.act(Rt[:], Rt[:], AF.Exp, scale=cf2[:, d, c:c + 1])
                                ts(Rt[:], Rt[:], 1.0, -1.0, ALU.min, ALU.mult)
                                g.act(Rt[:], Rt[:], AF.Sqrt, bias=one_t[:, 0:1])
                                tt(It[:], It[:], Rt[:], ALU.mult)
                                tt(It[:], It[:], xc[:], ALU.mult)

                                def scan(o, a_, b_, init):
                                    g.op("dve", "tensor_tensor_scan", out=o, data0=a_, data1=b_, initial=init,
                                         op0=ALU.mult, op1=ALU.add)
                                if d == 0:
                                    scan(Rt[:, T:NT], At[:, T:NT], It[:, T:NT], 0.0)
                                    prev = Rt[:, NT - 1:NT]
                                    for k in range(4):
                                        a, b_ = k * 1024, (k + 1) * 1024
                                        scan(Rt[:, a:b_], At[:, a:b_], It[:, a:b_], prev)
                                        prev = Rt[:, b_ - 1:b_]
                                    g.op("pool", "tensor_copy", out=xr[:], in_=Rt[:])
                                else:
                                    scan(_rev(Rt[:, T:NT]), _rev(At[:, T:NT]), _rev(It[:, T:NT]), 0.0)
                                    prev = Rt[:, T:T + 1]
                                    for k in range(3, -1, -1):
                                        a, b_ = k * 1024, (k + 1) * 1024
                                        scan(_rev(Rt[:, a:b_]), _rev(At[:, a:b_]), _rev(It[:, a:b_]), prev)
                                        prev = Rt[:, a:a + 1]
                                    tt(xr[:], xr[:], Rt[:], ALU.add)
                            tt(ya[:], xr[:], grt[:], ALU.mult)
                            for j, (t0, n) in enumerate(CH):
                                g.dma("pool", out=YA[j][:, c, :], in_=ya[:, t0:t0 + n])
                        g.barrier()
                    if dbg_stop == "b2":
                        return
                    st = ExitStack()
                    with st:
                        KT = sb(st, "KT", [128, 8, NT], BF16)
                        VV = sb(st, "VV", [128, 34, 4, 192], BF16)
                        w2 = sb(st, "w_in2", [128, 8, 672], BF16)
                        wuq = sb(st, "wuq", [128, 3, 768], BF16)
                        wukv = sb(st, "wukv", [128, 2, 1024], BF16)
                        gq = sb(st, "m_gq", [128, 3])
                        gkv = sb(st, "m_gkv", [128, 2])
                        gvec = sb(st, "m_gvec", [128, 192])
                        for kc in range(8):
                            g.dma("pool", out=w2[:, kc, :], in_=w_in_ab_d[e, kc * 128:(kc + 1) * 128, 1024:1696],
                                  partial=(kc > 0))
                        for c in range(3):
                            g.dma("pool", out=wuq[:, c, :], in_=w_uq_d[e, c * 128:(c + 1) * 128, :], partial=(c > 0))
                        for c in range(2):
                            g.dma("pool", out=wukv[:, c, :], in_=w_ukv_d[e, c * 128:(c + 1) * 128, :], partial=(c > 0))
                        g.dma("sp", out=gq[:], in_=mla_gq_d[e])
                        g.dma("sp", out=gkv[:], in_=mla_gkv_d[e])
                        g.dma("sp", out=gvec[:], in_=mla_gvec_d[e:e + 1, :].partition_broadcast(128))
                        g.op("dve", "memset", ap=VV[:, :, :, 64:128], constant=1.0)
                        for h in range(8):
                            g.op("dve", "memset", ap=KT[:, h, :], constant=0.0, partial=(h > 0))
                        st2 = ExitStack()
                        with st2:
                            utb = [sb(st2, f"ut{i}", [128, 8, 512], BF16) for i in range(2)]
                            sq3 = sb(st2, "sq3", [128, 3, 512], BF16)
                            rsq = sb(st2, "rsq", [128, 512])
                            qn = sb(st2, "qn", [128, 3, 512], BF16)
                            kvn = sb(st2, "kvn", [128, 2, 512], BF16)
                            qf = sb(st2, "qf", [128, 8, 96])
                            kf = sb(st2, "kf", [128, 8, 64])
                            krf = sb(st2, "krf", [128, 1, 32])
                            sqs_ = [sb(st2, f"sqs{i}", [128, 8, 64]) for i in range(2)]
                            ss_ = [sb(st2, f"ss{i}", [128, 8]) for i in range(2)]
                            tmp3_ = [sb(st2, f"tmp3{i}", [128, 8, 64]) for i in range(2)]
                            qr = sb(st2, "qr", [128, 8, 32])
                            t1_ = [sb(st2, f"t1{i}", [128, 8, 32]) for i in range(2)]
                            t2_ = [sb(st2, f"t2{i}", [128, 8, 32]) for i in range(2)]
                            krn = sb(st2, "krn", [128, 1, 32])
                            krfin = sb(st2, "krfin", [128, 1, 32])
                            Qtm = sb(st2, "Qtm", [128, 8, 96], BF16)
                            Ktm = sb(st2, "Ktm", [128, 8, 96], BF16)
                            QTs = [sb(st2, f"QTs{i}", [128, 8, 512], BF16) for i in range(1)] * 2
                            cs = [sb(st2, f"cs{i}", [128, 64]) for i in range(2)]
                            it = 0
                            for j, (t0, n) in enumerate(CH):
                                if dbg_stop == "b1b_s":
                                    break
                                v = 0 if j < 8 else 1
                                lat = j < 8
                                need_q = lat or ctx_out
                                ut = utb[j % 2]
                                g.dma("sp", out=ut[:, :, 0:n], in_=UT[j])
                                for (nch, c0, dst, gvv, sc) in ((3, 0, qn, gq, 1.0 / 384), (2, 384, kvn, gkv, 1.0 / 256)):
                                    if nch == 3 and not need_q:
                                        continue
                                    for c in range(nch):
                                        bank = 4 + c % 2
                                        for kc in range(8):
                                            g.mm(PS[bank][:, 0:n], w2[:, kc, c0 + c * 128:c0 + (c + 1) * 128],
                                                 ut[:, kc, 0:n], start=(kc == 0), stop=(kc == 7))
                                        g.act(sq3[:, c, 0:n], PS[bank][:, 0:n], AF.Square, partial=(c > 0))
                                        g.op("dve", "tensor_copy", out=dst[:, c, 0:n], in_=PS[bank][:, 0:n], partial=(c > 0))
                                    for c in range(nch):
                                        g.mm(PS[3][:, 0:n], ones_b[:], sq3[:, c, 0:n], start=(c == 0), stop=(c == nch - 1))
                                    g.act(rsq[:, 0:n], PS[3][:, 0:n], AF.Sqrt, scale=sc, bias=eps_t[:, 0:1])
                                    g.op("dve", "reciprocal", out=rsq[:, 0:n], in_=rsq[:, 0:n])
                                    for c in range(nch):
                                        stt(dst[:, c, 0:n], dst[:, c, 0:n], gvv[:, c:c + 1], rsq[:, 0:n], ALU.mult, ALU.mult)
                                qts = QTs[j % 2]
                                for s in range(n // 128):
                                    if dbg_stop == "b1b_0":
                                        break
                                    tok = slice(s * 128, (s + 1) * 128)
                                    ti = t0 // 128 + s
                                    cst = cs[it % 2]
                                    it += 1
                                    if lat:
                                        g.dma("sp", out=cst[:], in_=rope_mla_d[t0 + s * 128:t0 + (s + 1) * 128, :])
                                    g.record()
                                    if need_q and dbg_stop != "b1b_k":
                                        for hb in range(2):
                                            bank = 4 + hb
                                            for c in range(3):
                                                g.mm(PS[bank][:, 0:384], qn[:, c, tok], wuq[:, c, hb * 384:(hb + 1) * 384],
                                                     start=(c == 0), stop=(c == 2))
                                            evac(qf[:, hb * 4:(hb + 1) * 4, :],
                                                 PS[bank][:, 0:384].rearrange("p (h w) -> p h w", h=4))
                                        sqs, ss, tmp3, t1, t2 = sqs_[0], ss_[0], tmp3_[0], t1_[0], t2_[0]
                                        headnorm(qf[:, :, 0:64], Qtm[:, :, 0:64], gvec[:, 0:64], 8, 64,
                                                 sqs[:, :, 0:64], ss[:, 0:8], tmp3[:, :, 0:64])
                                        if lat:
                                            headnorm(qf[:, :, 64:96], qr[:], gvec[:, 128:160], 8, 32,
                                                     sqs[:, :, 0:32], ss[:, 0:8], tmp3[:, :, 0:32])
                                            rope(qr[:], Qtm[:, :, 64:96], cst, 8, 32, t1[:], t2[:])
                                        else:
                                            headnorm(qf[:, :, 64:96], Qtm[:, :, 64:96], gvec[:, 128:160], 8, 32,
                                                     sqs[:, :, 0:32], ss[:, 0:8], tmp3[:, :, 0:32])
                                        for h in range(8):
                                            g.tr(psb(6)[0:96, h * 128:(h + 1) * 128], Qtm[:, h, :], ident_b[:],
                                                 partial=(h > 0))
                                        evac(qts[0:96, :, tok], psb(6)[0:96, :].rearrange("p (h t) -> p h t", h=8))
                                    qchain = g.stop()
                                    g.record()
                                    sqs, ss, tmp3, t1, t2 = sqs_[1], ss_[1], tmp3_[1], t1_[1], t2_[1]
                                    for hb in range(2):
                                        bank = 1 + hb
                                        for c in range(2):
                                            g.mm(PS[bank][:, 0:512], kvn[:, c, tok], wukv[:, c, hb * 512:(hb + 1) * 512],
                                                 start=(c == 0), stop=(c == 1))
                                        pv = PS[bank][:, 0:512].rearrange("p (h w) -> p h w", h=4)
                                        evac(kf[:, hb * 4:(hb + 1) * 4, :], pv[:, :, 0:64])
                                        evac(VV[:, ti, 2 * hb:2 * hb + 2, 0:64], pv[:, 0:4:2, 64:128])
                                        evac(VV[:, ti, 2 * hb:2 * hb + 2, 128:192], pv[:, 1:4:2, 64:128])
                                    for kc in range(8):
                                        g.mm(PS[7][:, 0:32], ut[:, kc, tok], w2[:, kc, 640:672],
                                             start=(kc == 0), stop=(kc == 7))
                                    evac(krf[:, 0, :], PS[7][:, 0:32])
                                    headnorm(kf[:], Ktm[:, :, 0:64], gvec[:, 64:128], 8, 64,
                                             sqs[:, :, 0:64], ss[:, 0:8], tmp3[:, :, 0:64])
                                    if lat:
                                        headnorm(krf[:], krn[:], gvec[:, 160:192], 1, 32,
                                                 sqs[:, 0:1, 0:32], ss[:, 0:1], tmp3[:, 0:1, 0:32])
                                        rope(krn[:], krfin[:], cst, 1, 32, t1[:, 0:1, :], t2[:, 0:1, :])
                                    else:
                                        headnorm(krf[:], krfin[:], gvec[:, 160:192], 1, 32,
                                                 sqs[:, 0:1, 0:32], ss[:, 0:1], tmp3[:, 0:1, 0:32])
                                    g.op("dve", "tensor_copy", out=Ktm[:, :, 64:96],
                                         in_=krfin[:, 0:1, :].to_broadcast([128, 8, 32]))
                                    for h in range(8):
                                        g.tr(psb(0)[0:96, h * 128:(h + 1) * 128], Ktm[:, h, :], ident_b[:],
                                             partial=(h > 0))
                                    evac(KT[0:96, :, t0 + s * 128:t0 + (s + 1) * 128],
                                         psb(0)[0:96, :].rearrange("p (h t) -> p h t", h=8))
                                    kchain = g.stop()
                                    g.interleave(qchain, kchain)
                                if need_q and dbg_stop not in ("b1b_0", "b1b_k"):
                                    g.dma("pool", out=QT[j][0:96], in_=qts[0:96, :, 0:n])
                            g.barrier()
                        if dbg_stop is not None and dbg_stop.startswith("b1b"):
                            return
                        st2 = ExitStack()
                        with st2:
                            qsb = [sb(st2, f"qs{i}", [128, 8, 512], BF16) for i in range(2)]
                            yab = [sb(st2, f"yas{i}", [128, 4, 512], BF16) for i in range(2)]
                            hsb = [sb(st2, f"hsm{i}", [128, 512]) for i in range(2)]
                            PT = [sb(st2, f"PT{i}", [128, 512], BF16) for i in range(4)]
                            wout = sb(st2, "wout", [128, 8, 1024], BF16)
                            for kc in range(8):
                                g.dma("pool", out=wout[:, kc, :], in_=w_out_ab_d[e, kc * 128:(kc + 1) * 128, :],
                                      partial=(kc > 0))
                            OT = sb(st2, "OT", [128, 4, 512], BF16)
                            rc = sb(st2, "rc", [128, 512])
                            sc_att = 96.0 ** -0.5
                            for qq in qsb:
                                g.op("dve", "memset", ap=qq[:], constant=0.0)
                            for j in (range(9) if ctx_out else range(8)):
                                t0, n = CH[j]
                                v = 0 if j < 8 else 1
                                kcs = list(range(34)) if j < 8 else [32, 33]
                                qs = qsb[j % 2]
                                hs = hsb
                                yas = yab[j % 2]
                                g.dma("sp", out=qs[0:96, :, 0:n], in_=QT[j][0:96])
                                g.dma("sp", out=yas[:, :, 0:n], in_=YA[j])
                                nsub = n // 128
                                groups = []
                                for h in range(8):
                                    def S_(kc, out, h=h, qs=qs):
                                        g.mm(out, KT[:, h, kc * 128:(kc + 1) * 128], qs[:, h, 0:n])

                                    def V_(kc, h=h):
                                        if h % 2 == 0:
                                            return VV[:, kc, h // 2, 0:128]
                                        return VV[:, kc, h // 2, 64:192]

                                    def fin_(ob, h=h):
                                        lo, hi = slice(0, 64), slice(64, 128)
                                        o_, d_ = (lo, hi) if h % 2 == 0 else (hi, lo)
                                        g.op("dve", "reciprocal", out=rc[o_, 0:n], in_=PS[ob][d_, 0:n])
                                        tt(OT[o_, h // 2, 0:n], PS[ob][o_, 0:n], rc[o_, 0:n], ALU.mult)
                                    groups.append({"kcs": kcs, "S": S_, "V": V_, "fin": fin_})
                                attn_steps(groups, n, sc_att, 128, PT, sbanks=(0, 1, 4, 5), LA=3)
                                attention_out_fm(l, v, j, n, hs, OT, 4, wout, yas)
                            g.barrier()

                def odd_layer(l, ctx_out):
                    o = l // 2
                    lam_init = 0.8 - 0.6 * math.exp(-0.3 * l)
                    st = ExitStack()
                    with st:
                        hsb = [sb(st, f"hs{i}", [128, 8, 512]) for i in range(2)]
                        utb = [sb(st, f"ut{i}", [128, 8, 512], BF16) for i in range(2)]
                        for j, (t0, n) in enumerate(CH):
                            v = 0 if j < 8 else 1
                            hs = hsb[j % 2]
                            ut = utb[j % 2]
                            g.dma("sp", out=hs[:, :, 0:n], in_=HTb[j])
                            norm_mod(hs[:, :, 0:n], ut[:, :, 0:n], n, l, 1, v, j % 2)
                            g.dma("pool", out=UT[j], in_=ut[:, :, 0:n])
                        g.barrier()
                    st = ExitStack()
                    with st:
                        KT = sb(st, "KT2", [128, 8, NT], BF16)
                        VV = sb(st, "VV2", [128, 34, 8, 129], BF16)
                        gvec = sb(st, "d_gvec", [128, 512])
                        gouts = sb(st, "d_gouts", [128, 128])
                        lt = sb(st, "d_lt", [128, 2, 64])
                        ls = sb(st, "d_ls", [128, 2])
                        neglam = sb(st, "d_neglam", [128, 1])
                        g.dma("sp", out=gvec[:], in_=diff_gvec_d[o:o + 1, :].partition_broadcast(128))
                        g.op("dve", "memset", ap=VV[:, :, :, VV[:].shape[3] - 1:VV[:].shape[3]], constant=1.0)
                        l4 = gvec[:, 256:512].rearrange("p (a b c) -> p a b c", a=2, b=2)
                        tt(lt[:], l4[:, :, 0, :], l4[:, :, 1, :], ALU.mult)
                        g.op("dve", "tensor_reduce", out=ls[:], in_=lt[:], axis=AX.X, op=ALU.add)
                        g.act(ls[:], ls[:], AF.Exp)
                        ts(neglam[:], ls[:, 1:2], ls[:, 0:1], -lam_init, ALU.subtract, ALU.add)
                        ts(gouts[:], gvec[:, 128:256], 1.0 - lam_init)
                        st2 = ExitStack()
                        with st2:
                            wp = [sb(st2, f"wp{i}", [128, 8, 1024], BF16) for i in range(1)] * 2
                            utl = [sb(st2, f"utl{i}", [128, 8, 512], BF16) for i in range(1)] * 2
                            qf_ = [sb(st2, f"dqf{i}", [128, 16, 64]) for i in range(2)]
                            ss_ = [sb(st2, f"dss{i}", [128, 16]) for i in range(2)]
                            qr_ = [sb(st2, f"dqr{i}", [128, 16, 64]) for i in range(1)] * 2
                            t1_ = [sb(st2, f"dt1{i}", [128, 16, 64]) for i in range(1)] * 2
                            t2_ = [sb(st2, f"dt2{i}", [128, 16, 64]) for i in range(1)] * 2
                            Qtm_ = [sb(st2, f"dQtm{i}", [128, 16, 64], BF16) for i in range(2)]
                            isub = 0
                            QTs = [sb(st2, f"dQTs{i}", [128, 8, 512], BF16) for i in range(1)] * 2
                            cs = [sb(st2, f"dcs{i}", [128, 128]) for i in range(2)]
                            it = 0
                            iu = 0
                            for pi, part in enumerate(("k", "v", "q")):
                                wt = wp[pi % 2]
                                c0 = {"q": 0, "k": 1024, "v": 2048}[part]
                                for kc in range(8):
                                    g.dma("pool", out=wt[:, kc, :], in_=w_in_c_d[o, kc * 128:(kc + 1) * 128, c0:c0 + 1024],
                                          partial=(kc > 0))
                                for j, (t0, n) in enumerate(CH):
                                    lat = j < 8
                                    if part == "q" and not (lat or ctx_out):
                                        continue
                                    ut = utl[iu % 2]
                                    iu += 1
                                    g.dma("sp", out=ut[:, :, 0:n], in_=UT[j])
                                    qts = QTs[j % 2]
                                    for s in range(n // 128):
                                        tok = slice(s * 128, (s + 1) * 128)
                                        ti = t0 // 128 + s
                                        isub += 1
                                        pp = isub % 2
                                        qf, ss, qr, t1, t2, Qtm = qf_[pp], ss_[pp], qr_[pp], t1_[pp], t2_[pp], Qtm_[pp]
                                        sqs, tmp3 = t1, t2
                                        for hb in range(2):
                                            bank = 2 * pp + hb
                                            for kc in range(8):
                                                g.mm(PS[bank][:, 0:512], ut[:, kc, tok], wt[:, kc, hb * 512:(hb + 1) * 512],
                                                     start=(kc == 0), stop=(kc == 7))
                                            if part == "v":
                                                evac(VV[:, ti, hb * 4:(hb + 1) * 4, 0:128],
                                                     PS[bank][:, 0:512].rearrange("p (h w) -> p h w", h=4))
                                            else:
                                                evac(qf[:, hb * 8:(hb + 1) * 8, :],
                                                     PS[bank][:, 0:512].rearrange("p (h w) -> p h w", h=8))
                                        if part == "v":
                                            continue
                                        gain = gvec[:, 0:64] if part == "q" else gvec[:, 64:128]
                                        if lat:
                                            cst = cs[it % 2]
                                            it += 1
                                            g.dma("sp", out=cst[:], in_=rope_diff_d[t0 + s * 128:t0 + (s + 1) * 128, :])
                                            headnorm(qf[:], qr[:], gain, 16, 64, sqs[:], ss[:], tmp3[:])
                                            rope(qr[:], Qtm[:], cst, 16, 64, t1[:], t2[:])
                                        else:
                                            headnorm(qf[:], Qtm[:], gain, 16, 64, sqs[:], ss[:], tmp3[:])
                                        for h in range(8):
                                            g.tr(psb(6 + pp)[:, h * 128:(h + 1) * 128],
                                                 Qtm[:, 2 * h:2 * h + 2, :].rearrange("p a b -> p (a b)"), ident_b[:],
                                                 partial=(h > 0))
                                        src = psb(6 + pp)[:, :].rearrange("p (h t) -> p h t", h=8)
                                        if part == "k":
                                            evac(KT[:, :, t0 + s * 128:t0 + (s + 1) * 128], src)
                                        else:
                                            evac(qts[:, :, tok], src)
                                    if part == "q":
                                        g.dma("pool", out=QT[j], in_=qts[:, :, 0:n])
                            g.barrier()
                        st2 = ExitStack()
                        with st2:
                            qsb = [sb(st2, f"qs{i}", [128, 8, 512], BF16) for i in range(1)] * 2
                            hsb = [sb(st2, f"hsm{i}", [128, 512]) for i in range(2)]
                            PT = [sb(st2, f"PT{i}", [128, 512], BF16) for i in range(4)]
                            wout = sb(st2, "woutc", [128, 8, 1024], BF16)
                            for kc in range(8):
                                g.dma("pool", out=wout[:, kc, :], in_=w_out_c_d[o, kc * 128:(kc + 1) * 128, :],
                                      partial=(kc > 0))
                            Otm = sb(st2, "dOtm", [128, 4, 1024], BF16)
                            OT = sb(st2, "dOT", [128, 8, 512], BF16)
                            Dm = sb(st2, "dDm", [128, 4, 128])
                            dsq = sb(st2, "ddsq", [128, 4, 128])
                            rcp = sb(st2, "drcp", [128, 4])
                            nl = sb(st2, "dnl", [128, 4])
                            hss = sb(st2, "dhss", [128, 4])
                            sc_att = 64.0 ** -0.5
                            qpad = [[sb(st2, f"qpad{m}{i}", [128, 512], BF16) for i in range(2)] for m in range(2)]
                            for m in range(2):
                                for i in range(2):
                                    g.op("dve", "memset", ap=qpad[m][i][:], constant=0.0)
                            for j in (range(9) if ctx_out else range(8)):
                                t0, n = CH[j]
                                v = 0 if j < 8 else 1
                                kcs = list(range(34)) if j < 8 else [32, 33]
                                qs = qsb[j % 2]
                                hs = hsb
                                g.dma("sp", out=qs[:, :, 0:n], in_=QT[j])
                                nsub = n // 128
                                groups = []
                                for h in range(8):
                                    for m in range(2):
                                        qp = qpad[m][h % 2]

                                        def prep_(h=h, m=m, qp=qp, qs=qs):
                                            pr = slice(m * 64, (m + 1) * 64)
                                            g.op("dve", "tensor_copy", out=qp[pr, 0:n], in_=qs[pr, h, 0:n])

                                        def S_(kc, out, h=h, qp=qp):
                                            g.mm(out, KT[:, h, kc * 128:(kc + 1) * 128], qp[:, 0:n])

                                        def V_(kc, h=h):
                                            return VV[:, kc, h, :]

                                        def fin_(ob, h=h, m=m):
                                            for s in range(nsub):
                                                g.op("dve", "reciprocal", out=rcp[:, s:s + 1], in_=PS[2 + s][:, 128:129])
                                            if m == 0:
                                                for s in range(nsub):
                                                    ts(Dm[:, s, :], PS[2 + s][:, 0:128], rcp[:, s:s + 1])
                                            else:
                                                tt(nl[:, 0:nsub], rcp[:, 0:nsub],
                                                   neglam[:, 0:1].to_broadcast([128, nsub]), ALU.mult)
                                                for s in range(nsub):
                                                    stt(Dm[:, s, :], PS[2 + s][:, 0:128], nl[:, s:s + 1], Dm[:, s, :])
                                            if m == 0:
                                                return
                                            tt(dsq[:, 0:nsub, :], Dm[:, 0:nsub, :], Dm[:, 0:nsub, :], ALU.mult)
                                            g.op("dve", "tensor_reduce", out=hss[:, 0:nsub], in_=dsq[:, 0:nsub, :],
                                                 axis=AX.X, op=ALU.add)
                                            g.act(hss[:, 0:nsub], hss[:, 0:nsub], AF.Sqrt, scale=1.0 / 128,
                                                  bias=eps_t[:, 0:1])
                                            g.op("dve", "reciprocal", out=hss[:, 0:nsub], in_=hss[:, 0:nsub])
                                            tt(dsq[:, 0:nsub, :], Dm[:, 0:nsub, :],
                                               hss[:, 0:nsub].unsqueeze(2).to_broadcast([128, nsub, 128]), ALU.mult)
                                            tt(Otm[:, 0:nsub, h * 128:(h + 1) * 128], dsq[:, 0:nsub, :],
                                               gouts[:, :].unsqueeze(1).to_broadcast([128, nsub, 128]), ALU.mult)
                                        groups.append({"kcs": kcs, "S": S_, "V": V_, "fin": fin_, "prep": prep_})
                                attn_steps(groups, n, sc_att, 129, PT, sbanks=(0, 1, 6, 7), LA=3)
                                attention_out(l, v, j, n, hs, Otm, OT, 8, wout, None)
                            g.barrier()

                def ffn_layer(l, ctx_out):
                    last = (l == DEPTH - 1)
                    st = ExitStack()
                    with st:
                        wup = sb(st, "wup", [128, 8, 2 * DFF], BF16)
                        wdn = sb(st, "wdn", [128, NFC, 1024], BF16)
                        cw = sb(st, "f_cw", [128, NFC, 3])
                        cb = sb(st, "f_cb", [128, NFC])
                        for kc in range(8):
                            for (a, b_) in ((0, 2048), (2048, 4096), (4096, 5632)):
                                g.dma("pool", out=wup[:, kc, a:b_], in_=w_up_d[l, kc * 128:(kc + 1) * 128, a:b_],
                                      partial=not (kc == 0 and a == 0))
                        for fc in range(NFC):
                            g.dma("pool", out=wdn[:, fc, :], in_=w_down_d[l, fc * 128:(fc + 1) * 128, :], partial=(fc > 0))
                        g.dma("sp", out=cw[:], in_=ffn_cw_d[l])
                        g.dma("sp", out=cb[:], in_=ffn_cb_d[l])
                        hsb = [sb(st, f"fhs{i}", [128, 8, 514]) for i in range(2)]
                        ut = sb(st, "fut", [128, 8, 514], BF16)
                        mt = sb(st, "fmt", [128, NFC, 512], BF16)
                        g.op("dve", "memset", ap=hsb[0][:], constant=0.0)
                        g.op("dve", "memset", ap=hsb[1][:], constant=0.0)
                        io = 0
                        jlist = list(range(9) if ctx_out else range(8))

                        def pre(jj):
                            j = jlist[jj]
                            t0, n = CH[j]
                            v = 0 if j < 8 else 1
                            W = n + 2
                            hs = hsb[jj % 2]
                            has_l = j not in (0, 8)
                            has_r = j not in (7, 8)
                            g.dma("sp", out=hs[:, :, 1:n + 1], in_=HTa[j])
                            if has_l:
                                g.dma("sp", out=hs[:, :, 0:1], in_=HTa[j - 1][:, :, 511:512], partial=True, allow_slow_non_contiguous=True)
                            if has_r:
                                g.dma("sp", out=hs[:, :, n + 1:n + 2], in_=HTa[j + 1][:, :, 0:1], partial=True, allow_slow_non_contiguous=True)
                            norm_mod(hs[:, :, 1:n + 1], ut[:, :, 1:n + 1], n, l, 2, v, 7)
                            if has_l or has_r:
                                norm_mod(hs[:, :, 0:W:n + 1], ut[:, :, 0:W:n + 1], 2, l, 2, v, 6)
                        pre(0)
                        for jj, j in enumerate(jlist):
                            t0, n = CH[j]
                            v = 0 if j < 8 else 1
                            W = n + 2
                            hs = hsb[jj % 2]
                            has_l = j not in (0, 8)
                            has_r = j not in (7, 8)
                            halo = has_l or has_r
                            for fc in range(NFC):
                                ba = fc % 3
                                hb_ = 6
                                for kc in range(8):
                                    g.mm(PS[ba][:, 0:n], wup[:, kc, fc * 128:(fc + 1) * 128], ut[:, kc, 1:n + 1],
                                         start=(kc == 0), stop=(kc == 7))
                                for kc in range(8):
                                    g.mm(PS[3 + ba][:, 0:n], wup[:, kc, DFF + fc * 128:DFF + (fc + 1) * 128],
                                         ut[:, kc, 1:n + 1], start=(kc == 0), stop=(kc == 7))
                                if halo:
                                    for kc in range(8):
                                        g.mm(PS[hb_][:, 0:2], wup[:, kc, DFF + fc * 128:DFF + (fc + 1) * 128],
                                             ut[:, kc, 0:W:n + 1], start=(kc == 0), stop=(kc == 7))
                                tmp = nt_t[fc % 2]
                                gp = PS[3 + ba]
                                g.act(tmp[:, 0:n], gp[:, 0:n], AF.Identity, scale=cw[:, fc, 1:2], bias=cb[:, fc:fc + 1])
                                stt(tmp[:, 1:n], gp[:, 0:n - 1], cw[:, fc, 0:1], tmp[:, 1:n])
                                stt(tmp[:, 0:n - 1], gp[:, 1:n], cw[:, fc, 2:3], tmp[:, 0:n - 1])
                                if has_l:
                                    stt(tmp[:, 0:1], PS[hb_][:, 0:1], cw[:, fc, 0:1], tmp[:, 0:1])
                                if has_r:
                                    stt(tmp[:, n - 1:n], PS[hb_][:, 1:2], cw[:, fc, 2:3], tmp[:, n - 1:n])
                                g.act(tmp[:, 0:n], tmp[:, 0:n], AF.Gelu_apprx_tanh)
                                tt(mt[:, fc, 0:n], tmp[:, 0:n], PS[ba][:, 0:n], ALU.mult)
                            if jj + 1 < len(jlist):
                                pre(jj + 1)
                            for oc in range(8):
                                bank = 7
                                for fc in range(NFC):
                                    g.mm(PS[bank][:, 0:n], wdn[:, fc, oc * 128:(oc + 1) * 128], mt[:, fc, 0:n],
                                         start=(fc == 0), stop=(fc == NFC - 1))
                                stt(hs[:, oc, 1:n + 1], PS[bank][:, 0:n], MOD[:, l, 40 + oc, v:v + 1], hs[:, oc, 1:n + 1])
                            if not last:
                                g.dma("pool", out=HTb[j], in_=hs[:, :, 1:n + 1])
                            else:
                                for s in range(n // 128):
                                    o_t = mt[:, 0:4, :].rearrange("p a b -> p (a b)").bitcast(F32)
                                    for half in range(2):
                                        bank = half
                                        for q4 in range(4):
                                            kc = half * 4 + q4
                                            g.tr(PS[bank][:, q4 * 128:(q4 + 1) * 128],
                                                 hs[:, kc, 1 + s * 128:1 + (s + 1) * 128], ident_f[:], partial=(q4 > 0))
                                        evac(o_t[:, half * 512:(half + 1) * 512], PS[bank][:, :])
                                    g.dma("pool", out=out_d[t0 + s * 128:t0 + (s + 1) * 128, :], in_=o_t)
                        g.barrier()

                stop = False
                if dbg_stop == "pro":
                    dump(HTb)
                    stop = True
                for l in range(DEPTH):
                    if stop:
                        break
                    ctx_out = l < DEPTH - 1
                    if l % 2 == 0:
                        even_layer(l, ctx_out)
                    else:
                        odd_layer(l, ctx_out)
                    if dbg_stop in ("b1a", "b2", "b1b", "b1b_0", "b1b_q", "b1b_k", "b1b_s"):
                        dump(HTb)
                        break
                    if dbg_stop == f"m{l}":
                        dump(HTa)
                        break
                    ffn_layer(l, ctx_out)
                    if dbg_stop == f"f{l}":
                        dump(HTb)
                        break
                g.barrier()
                g.drain()

        g1_ = G(nc, sems)
        gen(g1_)
        g2_ = G(nc, sems, plan=g1_.plan)
        gen(g2_)
        print(f"[kernel] ops={g2_.n} inst={g2_.ninst} waits={g2_.nwaits} "
              f"cnt={g2_.cnt} dmas={g2_.qn}", flush=True)
    return nc


def _fm(vec, nchunks):
    return np.ascontiguousarray(np.asarray(vec, np.float32).reshape(nchunks, 128).T)


def _rope_table(rot_dim):
    S = T
    rows = S // GRID_W
    row = np.repeat(np.arange(rows, dtype=np.float32), GRID_W)
    col = np.tile(np.arange(GRID_W, dtype=np.float32), rows)
    da = rot_dim // 2
    inv = (np.float32(10000.0) ** (-np.arange(0, da, 2, dtype=np.float32) / np.float32(da))).astype(np.float32)
    ar = (row[:, None] * inv).astype(np.float32)
    ac = (col[:, None] * inv).astype(np.float32)
    cr, sr, cc, sc = np.cos(ar), np.sin(ar), np.cos(ac), np.sin(ac)
    cos = np.concatenate([cr, cr, cc, cc], axis=1)
    sin = np.concatenate([-sr, sr, -sc, sc], axis=1)
    return np.ascontiguousarray(np.concatenate([cos, sin], axis=1).astype(np.float32))


def prep_inputs(inp, b):
    f = lambda a: np.ascontiguousarray(np.asarray(a, np.float32))
    m = {}
    m["x"] = f(inp["x"][b])
    m["ctx"] = f(inp["ctx"][b])
    cv = np.stack([_fm(inp["c"][b], 8), _fm(inp["c_ctx"], 8)], axis=-1)
    m["cvec"] = f(cv)
    m["ident"] = np.eye(128, dtype=np.float32)
    m["w_mod"] = f(inp["w_mod"])
    m["bmod"] = f(np.stack([_fm(inp["b_mod"][l], 48) for l in range(DEPTH)], axis=1))
    m["g1"] = f(np.stack([_fm(inp["g_norm1"][l], 8) for l in range(DEPTH)], axis=1))
    m["g2"] = f(np.stack([_fm(inp["g_norm2"][l], 8) for l in range(DEPTH)], axis=1))
    m["w_in_ab"] = f(inp["w_in_ab"])
    cw = np.asarray(inp["lru_conv_w"], np.float32).reshape(2, 4, 4, 128)
    m["lru_cw"] = f(cw.transpose(0, 3, 2, 1))
    m["lru_cb"] = f(np.asarray(inp["lru_conv_b"], np.float32).reshape(2, 4, 128).transpose(0, 2, 1))
    gw = np.asarray(inp["lru_gate_w"], np.float32)
    bd = np.zeros((2, 128, 2, 2, 4, 128), np.float32)
    for nb in range(8):
        c, hlf = nb // 2, nb % 2
        bd[:, hlf * 64:(hlf + 1) * 64, :, :, c, hlf * 64:(hlf + 1) * 64] = gw[:, :, :, nb].transpose(0, 3, 1, 2, 4)
    m["lru_bd"] = bd
    gbv = np.asarray(inp["lru_gate_b"], np.float32).reshape(2, 2, 2, 4, 128)
    m["lru_gb"] = f(gbv.transpose(0, 4, 1, 2, 3))
    lm = np.asarray(inp["lru_lambda"], np.float32).reshape(2, 2, 4, 128)
    m["lru_lam"] = f(lm.transpose(0, 3, 1, 2))
    m["mla_gq"] = f(np.stack([_fm(inp["mla_g_q"][e], 3) for e in range(2)], axis=0))
    m["mla_gkv"] = f(np.stack([_fm(inp["mla_g_kv"][e], 2) for e in range(2)], axis=0))
    m["w_uq"] = f(inp["mla_w_uq"])
    m["w_ukv"] = f(inp["mla_w_ukv"])
    m["mla_gvec"] = f(np.concatenate([inp["mla_gq_nope"], inp["mla_gk_nope"], inp["mla_gq_rope"], inp["mla_gk_rope"]],
                                     axis=1))
    m["w_out_ab"] = f(inp["w_out_ab"])
    m["w_in_c"] = f(inp["w_in_c"])
    m["diff_gvec"] = f(np.concatenate([inp["diff_gq"], inp["diff_gk"], inp["diff_g_out"],
                                       np.asarray(inp["diff_lam"]).reshape(2, 256)], axis=1))
    m["w_out_c"] = f(inp["w_out_c"])
    m["w_up"] = f(inp["ffn_w_up"])
    fcw = np.asarray(inp["ffn_conv_w"], np.float32).reshape(DEPTH, 3, NFC, 128)
    m["ffn_cw"] = f(fcw.transpose(0, 3, 2, 1))
    m["ffn_cb"] = f(np.asarray(inp["ffn_conv_b"], np.float32).reshape(DEPTH, NFC, 128).transpose(0, 2, 1))
    m["w_down"] = f(inp["ffn_w_down"])
    m["rope_mla"] = _rope_table(32)
    m["rope_diff"] = _rope_table(64)
    return m


def kernel(**inputs):
    nc = build_program()
    shared = None
    in_maps = []
    for b in range(8):
        m = prep_inputs(inputs, b)
        if shared is None:
            shared = m
        else:
            for k in m:
                if k not in ("x", "ctx", "cvec"):
                    m[k] = shared[k]
        in_maps.append(m)
    res = run_bass_kernel_spmd(nc, in_maps, core_ids=list(range(8)))
    out = np.stack([np.asarray(r["out"], np.float32) for r in res.results], axis=0)
    return out
```
